# Optimizing a Trainium2 kernel written in Bass

```python
import jax, jax.numpy as jnp
from jax import lax
import numpy as np

D_MODEL = 4096
BATCH = 2
SEQ = 4096
DEPTH = 1

CHUNK = 64
Q_BLOCK = 128
D_MIX = D_MODEL
D_CONV = D_MIX // 2
CONV_WIDTH = 3
N_HEADS = 16
V_HEAD_DIM = 128
D_ATTN = N_HEADS * V_HEAD_DIM
QK_NOPE_DIM = 128
ROPE_DIM = 64
QK_HEAD_DIM = QK_NOPE_DIM + ROPE_DIM
Q_LORA = 1024
KV_LORA = 512
ROPE_THETA = 10000.0
NORM_EPS = 1e-6
ATTN_SCALE = QK_HEAD_DIM ** -0.5

PROJ_SIZES = (D_CONV, D_CONV, D_CONV, D_CONV, Q_LORA, KV_LORA, ROPE_DIM, D_ATTN)
D_IN_PROJ = sum(PROJ_SIZES)
PROJ_SPLITS = tuple(int(v) for v in np.cumsum(PROJ_SIZES)[:-1])

kernel_name = "hymba_shortconv_mla_hybrid_block"


def rmsnorm(x, g):
    xf = x.astype(jnp.float32)
    y = xf * lax.rsqrt(jnp.mean(xf * xf, axis=-1, keepdims=True) + NORM_EPS)
    return (y * g.astype(jnp.float32)).astype(x.dtype)


def rope_tables(seq):
    pos = jnp.arange(seq, dtype=jnp.float32)
    inv_freq = 1.0 / (ROPE_THETA ** (jnp.arange(0, ROPE_DIM, 2, dtype=jnp.float32) / ROPE_DIM))
    ang = pos[:, None] * inv_freq[None, :]
    return jnp.cos(ang), jnp.sin(ang)


def apply_rope(x, cos, sin):
    xf = x.astype(jnp.float32)
    x1, x2 = jnp.split(xf, 2, axis=-1)
    c = cos[None, :, None, :]
    s = sin[None, :, None, :]
    return jnp.concatenate([x1 * c - x2 * s, x2 * c + x1 * s], axis=-1).astype(x.dtype)


def causal_short_conv(u, w):
    s = u.shape[1]
    up = jnp.pad(u, ((0, 0), (CONV_WIDTH - 1, 0), (0, 0)))
    return sum(w[j] * up[:, j:j + s] for j in range(CONV_WIDTH))


def chunk_causal_attention(q, k, v):
    b, s, h, dq = q.shape
    nqb = s // Q_BLOCK
    qb = q.reshape(b, nqb, Q_BLOCK, h, dq).transpose(1, 0, 2, 3, 4)
    key_chunk = jnp.arange(s) // CHUNK

    def one_block(args):
        i, qi = args
        q_chunk = (i * Q_BLOCK + jnp.arange(Q_BLOCK)) // CHUNK
        mask = key_chunk[None, :] <= q_chunk[:, None]
        sc = jnp.einsum('bqhd,bkhd->bhqk', qi, k, preferred_element_type=jnp.float32) * ATTN_SCALE
        sc = jnp.where(mask[None, None], sc, -jnp.inf)
        p = jax.nn.softmax(sc, axis=-1).astype(v.dtype)
        return jnp.einsum('bhqk,bkhd->bqhd', p, v)

    out = lax.map(one_block, (jnp.arange(nqb), qb))
    return out.transpose(1, 0, 2, 3, 4).reshape(b, s, h, v.shape[-1])


def hybrid_layer(x, g_in, w_in, conv_w, q_norm_g, w_uq, kv_norm_g, w_ukv, w_out, cos, sin):
    b, s, _ = x.shape
    xn = rmsnorm(x, g_in)
    proj = jnp.einsum('bsd,de->bse', xn, w_in)
    gB, gC, h, z_conv, c_q, c_kv, k_rope, z_attn = jnp.split(proj, PROJ_SPLITS, axis=-1)

    y_conv = gB * causal_short_conv(gC * h, conv_w)
    y_conv = y_conv * jax.nn.silu(z_conv)

    q = jnp.einsum('bsr,re->bse', rmsnorm(c_q, q_norm_g), w_uq).reshape(b, s, N_HEADS, QK_HEAD_DIM)
    q_nope, q_pe = q[..., :QK_NOPE_DIM], q[..., QK_NOPE_DIM:]
    kv = jnp.einsum('bsr,re->bse', rmsnorm(c_kv, kv_norm_g), w_ukv).reshape(b, s, N_HEADS, QK_NOPE_DIM + V_HEAD_DIM)
    k_nope, v = kv[..., :QK_NOPE_DIM], kv[..., QK_NOPE_DIM:]
    q_pe = apply_rope(q_pe, cos, sin)
    k_pe = apply_rope(k_rope[:, :, None, :], cos, sin)
    q_full = jnp.concatenate([q_nope, q_pe], axis=-1)
    k_full = jnp.concatenate([k_nope, jnp.broadcast_to(k_pe, (b, s, N_HEADS, ROPE_DIM))], axis=-1)
    attn = chunk_causal_attention(q_full, k_full, v).reshape(b, s, D_ATTN)
    y_attn = attn * jax.nn.silu(z_attn)

    y = jnp.concatenate([y_conv, y_attn], axis=-1)
    return x + jnp.einsum('bse,ed->bsd', y, w_out)


def setup_inputs(seed: int = 0) -> dict:
    key = jax.random.key(seed)
    ks = jax.random.split(key, 12)
    nrm = jax.random.normal
    return {
        "x": nrm(ks[0], (BATCH, SEQ, D_MODEL), jnp.float32),
        "g_in": 1.0 + 0.02 * nrm(ks[1], (DEPTH, D_MODEL), jnp.float32),
        "w_in": nrm(ks[2], (DEPTH, D_MODEL, D_IN_PROJ), jnp.float32) * D_MODEL ** -0.5,
        "conv_w": nrm(ks[3], (DEPTH, CONV_WIDTH, D_CONV), jnp.float32) * CONV_WIDTH ** -0.5,
        "q_norm_g": 1.0 + 0.02 * nrm(ks[4], (DEPTH, Q_LORA), jnp.float32),
        "w_uq": nrm(ks[5], (DEPTH, Q_LORA, N_HEADS * QK_HEAD_DIM), jnp.float32) * Q_LORA ** -0.5,
        "kv_norm_g": 1.0 + 0.02 * nrm(ks[6], (DEPTH, KV_LORA), jnp.float32),
        "w_ukv": nrm(ks[7], (DEPTH, KV_LORA, N_HEADS * (QK_NOPE_DIM + V_HEAD_DIM)), jnp.float32) * KV_LORA ** -0.5,
        "w_out": nrm(ks[8], (DEPTH, D_MIX, D_MODEL), jnp.float32) * D_MIX ** -0.5,
        "g_final": 1.0 + 0.02 * nrm(ks[9], (D_MODEL,), jnp.float32),
    }


def reference(x, g_in, w_in, conv_w, q_norm_g, w_uq, kv_norm_g, w_ukv, w_out, g_final):
    cos, sin = rope_tables(x.shape[1])
    h = x
    for l in range(DEPTH):
        h = hybrid_layer(h, g_in[l], w_in[l], conv_w[l], q_norm_g[l], w_uq[l],
                         kv_norm_g[l], w_ukv[l], w_out[l], cos, sin)
    return rmsnorm(h, g_final)
```

```python
import contextlib
import numpy as np
import concourse.bass as bass
import concourse.mybir as mybir
from concourse.bass_utils import run_bass_kernel_spmd

F32 = mybir.dt.float32
BF16 = mybir.dt.bfloat16
AF = mybir.ActivationFunctionType
ALU = mybir.AluOpType
AX = mybir.AxisListType

NCORES = 8
D = 4096
S = 4096
BLK = 512
NMT = 93
EPS = 1e-6
ATTN_SCALE = 192 ** -0.5
ORDER = [0, 1, 2, 4, 5, 6, 3, 7]
NEG = -30000.0


class Tr:
    __slots__ = ("lastw", "readers", "sem", "cnt", "name")

    def __init__(self, name=""):
        self.lastw = None
        self.readers = []
        self.sem = None
        self.cnt = 0
        self.name = name


class Ins:
    __slots__ = ("eng", "fn", "deps", "signal", "ordinal", "dma_tr", "dma_val")

    def __init__(self, eng, fn, deps):
        self.eng = eng
        self.fn = fn
        self.deps = deps
        self.signal = False
        self.ordinal = 0
        self.dma_tr = None
        self.dma_val = 0


ENGS = ("pe", "act", "dve", "pool", "sp")


class Prog:
    def __init__(self):
        self.ins = {e: [] for e in ENGS}
        self.last = {e: None for e in ENGS}
        self.dma_trs = []
        self.sync_same_engine = True

    def _deps(self, eng, reads, writes):
        deps = []
        for t in reads:
            if t.lastw is not None:
                deps.append(t.lastw)
        for t in writes:
            if t.lastw is not None:
                deps.append(t.lastw)
            deps.extend(t.readers)
        out = []
        for d in deps:
            if d[0] == "c":
                i = d[1]
                if i.eng == "pe" and eng == "pe":
                    continue
                if i.eng == eng and not self.sync_same_engine:
                    continue
                i.signal = True
            out.append(d)
        return out

    def add(self, eng, fn, reads=(), writes=()):
        ins = Ins(eng, fn, self._deps(eng, reads, writes))
        ev = ("c", ins)
        for t in reads:
            t.readers.append(ev)
        for t in writes:
            t.lastw = ev
            t.readers = []
        self.ins[eng].append(ins)
        self.last[eng] = ins
        return ins

    def dma(self, eng, fn, reads=(), writes=(), sem_tr=None):
        ins = Ins(eng, fn, self._deps(eng, reads, writes))
        if sem_tr is None:
            sem_tr = writes[0]
        if sem_tr not in self.dma_trs:
            self.dma_trs.append(sem_tr)
        sem_tr.cnt += 16
        ins.dma_tr = sem_tr
        ins.dma_val = sem_tr.cnt
        ev = ("d", sem_tr, sem_tr.cnt)
        for t in reads:
            t.readers.append(ev)
        for t in writes:
            t.lastw = ev
            t.readers = []
        self.ins[eng].append(ins)
        return ins

    def barrier(self):
        evs = []
        for e in ("pe", "act", "dve", "pool"):
            if self.last[e] is not None:
                self.last[e].signal = True
                evs.append(("c", self.last[e]))
        for t in self.dma_trs:
            evs.append(("d", t, t.cnt))
        for e in ENGS:
            self.ins[e].append(Ins(e, None, list(evs)))

    def emit(self, nc, stack):
        esem = {e: stack.enter_context(nc.semaphore("sem_" + e)) for e in ("pe", "act", "dve", "pool")}
        for i, t in enumerate(self.dma_trs):
            t.sem = stack.enter_context(nc.semaphore("dsem%d" % i))
        for e in ("pe", "act", "dve", "pool"):
            n = 0
            for i in self.ins[e]:
                if i.signal:
                    n += 1
                    i.ordinal = n
        block = stack.enter_context(nc.Block())

        def run(ename, eng):
            waited = {}
            for i in self.ins[ename]:
                for d in i.deps:
                    if d[0] == "c":
                        sem, val, key = esem[d[1].eng], d[1].ordinal, d[1].eng
                    else:
                        sem, val, key = d[1].sem, d[2], id(d[1])
                    if waited.get(key, 0) >= val:
                        continue
                    waited[key] = val
                    eng.wait_ge(sem, val)
                if i.fn is None:
                    continue
                r = i.fn(eng)
                if i.dma_tr is not None:
                    r.then_inc(i.dma_tr.sem, 16)
                elif i.signal:
                    r.then_inc(esem[ename], 1)

        @block.tensor
        def _(e):
            run("pe", e)

        @block.scalar
        def _(e):
            run("act", e)

        @block.vector
        def _(e):
            run("dve", e)

        @block.gpsimd
        def _(e):
            run("pool", e)

        @block.sync
        def _(e):
            run("sp", e)


R_XG = 0
R_Y = 65536
R_CKVN = 131072
R_K2 = 163840
R_CQG = 172032
R_MISC = 188416
ARENA = 188416 + 20480 + 2048


def build_program(dbg=False):
    nc = bass.Bass("TRN2", target_bir_lowering=False)
    P = Prog()

    def din(name, shape):
        return nc.dram_tensor(name, list(shape), F32, kind="ExternalInput").ap()

    xT = din("xT", [8, 128, 32, 512])
    xres = din("xres", [1024, D])
    xh = din("xh", [128, 32, 4])
    cs_tab = din("cs_tab", [8, 128, 512])
    gate_d = din("gate", [128, 6])
    w_in_t = din("w_in_t", [NMT, 128, 32, 128])
    w_uq_t = din("w_uq_t", [16, 128, 8, 256])
    w_ukv_t = din("w_ukv_t", [8, 128, 4, 512])
    w_out_t = din("w_out_t", [8, 8, 128, 4, 512])
    g_in_d = din("g_in_t", [128, 32])
    g_q_d = din("g_q_t", [128, 8])
    g_kv_d = din("g_kv_t", [128, 4])
    convw_d = din("convw_t", [128, 48])
    gfin_d = din("gfin_b", [128, D])
    dmat_d = din("dmat", [128, 128])
    out = nc.dram_tensor("out", [1024, D], F32, kind="ExternalOutput").ap()

    stack = contextlib.ExitStack()
    with stack:
        arena = stack.enter_context(nc.sbuf_tensor("arena", [128, ARENA // 4], F32))
        psb_t = [stack.enter_context(nc.psum_tensor("psb%d" % i, [128, 512], F32)) for i in range(8)]
        psb = [t[:, :] for t in psb_t]
        tps = [Tr("ps%d" % i) for i in range(8)]

        def v32(off, n):
            return arena[:, off // 4: off // 4 + n]

        def v16(off, n):
            return arena[:, off // 4: off // 4 + n // 2].bitcast(BF16)

        mo = [R_MISC]

        def misc32(n):
            a = v32(mo[0], n)
            mo[0] += 4 * n
            return a

        def misc16(n):
            a = v16(mo[0], n)
            mo[0] += 2 * n
            return a

        g_in_t = misc32(32)
        g_q_t = misc32(8)
        g_kv_t = misc32(4)
        convw = misc32(48)
        gate = misc32(6)
        ssq = misc32(32)
        tot = misc32(4)
        rfin = misc32(4)
        rbh = misc32(4)
        rbh2 = misc32(4)
        uh = misc32(4)
        dmat = misc32(128)
        ones_bf = misc16(128)
        rb = [misc32(512) for _ in range(2)]
        cs = [misc32(512) for _ in range(2)]
        rqb = [misc32(512) for _ in range(2)]
        csq = [misc32(512) for _ in range(2)]
        rkvb = misc32(512)
        assert mo[0] <= ARENA, mo[0]
        t_const = Tr("const")
        t_rb = [Tr(), Tr()]
        t_cs = [Tr(), Tr()]
        t_rqb = [Tr(), Tr()]
        t_csq = [Tr(), Tr()]
        t_rkvb = Tr()
        t_small = Tr("small")

        XG = [v16(R_XG + b * 32768, 16384).rearrange("p (k t) -> p k t", k=32) for b in range(2)]
        t_xg = [[Tr() for _ in range(8)] for _ in range(2)]
        Y = v16(R_Y, 32768).rearrange("p (f t) -> p f t", f=32)
        t_y = [[Tr() for _ in range(2)] for _ in range(32)]
        CKVN = v16(R_CKVN, 16384).rearrange("p (k t) -> p k t", k=4)
        t_ckvn = [Tr() for _ in range(8)]
        K2 = v16(R_K2, 4096)
        t_k2 = [Tr() for _ in range(8)]
        CQG = v16(R_CQG, 8192).rearrange("p (k t) -> p k t", k=8)
        t_cqg = [[Tr() for _ in range(2)] for _ in range(8)]

        for dst, src in ((g_in_t, g_in_d), (g_q_t, g_q_d), (g_kv_t, g_kv_d), (convw, convw_d),
                         (gate, gate_d), (dmat, dmat_d)):
            P.dma("sp", lambda e, dst=dst, src=src: e.dma_start(out=dst, in_=src), writes=[Tr()])
        P.add("dve", lambda e: e.memset(ones_bf, 1.0), writes=[Tr()])

        def mm(out_, lhsT, rhs, start, stop, reads, writes):
            P.add("pe", lambda e: e.matmul(out_, lhsT, rhs, start=start, stop=stop), reads=reads, writes=writes)

        def rstd_from_psum(bank, tbank, scale, tmp, ttmp, dst, tdst, n=512):
            P.add("act", lambda e: e.activation(out=tmp[:, 0:n], in_=bank[:, 0:n], func=AF.Sqrt, scale=scale, bias=eps_ap),
                  reads=[tbank, t_const], writes=[ttmp])
            P.add("dve", lambda e: e.reciprocal(out=dst[:, 0:n], in_=tmp[:, 0:n]), reads=[ttmp], writes=[tdst])

        eps_ap = misc32(1) if False else None
        eps_ap = v32(mo[0], 1)
        mo[0] += 4
        P.add("dve", lambda e: e.memset(eps_ap, EPS), writes=[Tr()])
        P.barrier()

        WCKV = v16(R_Y, 32 * 640).rearrange("p (k c) -> p k c", k=32)
        t_wckv = Tr()
        o1 = R_Y + 40960
        xst = [v32(o1 + b * 4096, 1024).rearrange("p (k t) -> p k t", k=2) for b in range(4)]
        t_xst = [Tr() for _ in range(4)]
        ckvf = v32(o1 + 16384, 2048).rearrange("p (k t) -> p k t", k=4)
        t_ckvf = [Tr() for _ in range(4)]
        xsq = [v16(R_CQG + b * 2048, 1024).rearrange("p (k t) -> p k t", k=2) for b in range(4)]
        t_xsq = [Tr() for _ in range(4)]
        tk = v32(R_CQG + 8192, 512)
        t_tk = Tr()
        ckvsq = v16(R_CQG + 10240, 2048).rearrange("p (k t) -> p k t", k=4)
        t_ckvsq = Tr()
        sqt = v32(R_CQG + 14336, 512)
        t_sqt = Tr()
        t_xgk = [[Tr() for _ in range(32)] for _ in range(2)]
        for b_ in range(2):
            for g_ in range(8):
                t_xg[b_][g_] = None

        for m in range(5):
            P.dma("pool", lambda e, m=m: e.dma_start(out=WCKV[:, :, m * 128:(m + 1) * 128], in_=w_in_t[m],
                                                       max_dma_last_dim=8192), writes=[t_wckv])
        nchunk = [0]

        def chunk(i, kg):
            s = ORDER[i]
            xb = i % 2
            stb = nchunk[0] % 4
            nchunk[0] += 1
            P.dma("sp", lambda e: e.dma_start(out=xst[stb], in_=xT[s, :, 2 * kg:2 * kg + 2, :]), writes=[t_xst[stb]])
            for j in range(2):
                kc = 2 * kg + j
                P.add("dve", lambda e, kc=kc, j=j: e.tensor_scalar(
                    out=XG[xb][:, kc, :], in0=xst[stb][:, j, :], scalar1=g_in_t[:, kc:kc + 1], scalar2=None,
                    op0=ALU.mult), reads=[t_xst[stb], t_const], writes=[t_xgk[xb][kc]])
            P.add("act", lambda e: e.activation(out=xsq[stb], in_=xst[stb], func=AF.Square),
                  reads=[t_xst[stb]], writes=[t_xsq[stb]])
            for j in range(2):
                mm(psb[0], ones_bf, xsq[stb][:, j, :], kg == 0 and j == 0, kg == 15 and j == 1,
                   [t_xsq[stb], t_const], [tps[0]])
            for j in range(2):
                kc = 2 * kg + j
                for m in range(5):
                    mm(psb[1 + m], WCKV[:, kc, m * 128:(m + 1) * 128], XG[xb][:, kc, :], kc == 0, kc == 31,
                       [t_wckv, t_xgk[xb][kc]], [tps[1 + m]])

        def tail_a(i):
            s = ORDER[i]
            xb = i % 2
            for m in range(4):
                if m % 2 == 0:
                    P.add("act", lambda e, m=m: e.activation(out=ckvf[:, m, :], in_=psb[1 + m], func=AF.Copy),
                          reads=[tps[1 + m]], writes=[t_ckvf[m]])
                else:
                    P.add("dve", lambda e, m=m: e.tensor_copy(out=ckvf[:, m, :], in_=psb[1 + m]),
                          reads=[tps[1 + m]], writes=[t_ckvf[m]])
            P.add("act", lambda e: e.activation(out=tk, in_=psb[5], func=AF.Copy), reads=[tps[5]], writes=[t_tk])
            P.add("act", lambda e: e.activation(out=rb[xb], in_=psb[0], func=AF.Sqrt, scale=1.0 / D, bias=eps_ap),
                  reads=[tps[0], t_const], writes=[t_rb[xb]])
            P.add("dve", lambda e: e.reciprocal(out=rb[xb], in_=rb[xb]), reads=[t_rb[xb]], writes=[t_rb[xb]])
            P.dma("sp", lambda e: e.dma_start(out=cs[xb], in_=cs_tab[s]), writes=[t_cs[xb]])
            P.add("act", lambda e: e.activation(out=ckvsq, in_=ckvf, func=AF.Square), reads=t_ckvf, writes=[t_ckvsq])
            P.add("pool", lambda e: e.tensor_tensor(out=tk, in0=tk, in1=rb[xb], op=ALU.mult),
                  reads=[t_tk, t_rb[xb]], writes=[t_tk])
            P.add("pool", lambda e: e.tensor_tensor(out=tk, in0=tk, in1=cs[xb], op=ALU.mult),
                  reads=[t_tk, t_cs[xb]], writes=[t_tk])
            P.add("pool", lambda e: e.tensor_tensor(out=rkvb, in0=rb[xb], in1=rb[xb], op=ALU.mult),
                  reads=[t_rb[xb]], writes=[t_rkvb])

        def tail_pe(i):
            s = ORDER[i]
            xb = i % 2
            for j in range(4):
                mm(psb[6], ones_bf, ckvsq[:, j, :], j == 0, j == 3, [t_ckvsq, t_const], [tps[6]])
            mm(psb[7], dmat, tk, True, True, [t_tk, t_const], [tps[7]])
            P.add("dve", lambda e: e.tensor_tensor(out=rkvb, in0=psb[6], in1=rkvb, op=ALU.mult),
                  reads=[tps[6], t_rkvb], writes=[t_rkvb])
            P.add("act", lambda e: e.activation(out=rkvb, in_=rkvb, func=AF.Sqrt, scale=1.0 / 512, bias=eps_ap),
                  reads=[t_rkvb, t_const], writes=[t_rkvb])
            P.add("dve", lambda e: e.reciprocal(out=rkvb, in_=rkvb), reads=[t_rkvb], writes=[t_rkvb])
            P.add("pool", lambda e: e.tensor_tensor(out=rkvb, in0=rkvb, in1=rb[xb], op=ALU.mult),
                  reads=[t_rkvb, t_rb[xb]], writes=[t_rkvb])
            P.add("act", lambda e: e.activation(out=K2[:, s * 512:(s + 1) * 512], in_=psb[7], func=AF.Copy),
                  reads=[tps[7]], writes=[t_k2[s]])
            for m in range(4):
                P.add("dve", lambda e, m=m: e.scalar_tensor_tensor(
                    out=CKVN[:, m, s * 512:(s + 1) * 512], in0=ckvf[:, m, :], scalar=g_kv_t[:, m:m + 1], in1=rkvb,
                    op0=ALU.mult, op1=ALU.mult), reads=[t_ckvf[m], t_rkvb, t_const], writes=[t_ckvn[s]])

        for i in range(8):
            for kg in range(16):
                chunk(i, kg)
                if i > 0 and kg == 3:
                    tail_pe(i - 1)
            tail_a(i)
        tail_pe(7)
        P.barrier()
        for b_ in range(2):
            for g_ in range(8):
                t_xg[b_][g_] = Tr()

        def wstream(region_off, nslots=3):
            return ([v16(region_off + b * 8192, 4096).rearrange("p (k c) -> p k c", k=32) for b in range(nslots)],
                    [Tr() for _ in range(nslots)])

        WS, t_ws = wstream(R_Y)
        o2 = R_Y + 24576
        ev = [v32(o2 + b * 2048, 512) for b in range(2)]
        t_ev = [Tr(), Tr()]
        sq2 = [v16(o2 + 4096 + b * 1024, 512) for b in range(2)]
        t_sq2 = [Tr(), Tr()]
        sqt2a = v32(o2 + 6144, 512)
        mts = list(range(5, 29))

        def wload(idx, m):
            dst, tdst = WS[idx % 3], t_ws[idx % 3]
            P.dma("pool", lambda e: e.dma_start(out=dst, in_=w_in_t[m], max_dma_last_dim=8192), writes=[tdst])

        for idx in range(3):
            wload(idx, mts[idx])
        nev = 0
        for idx, m in enumerate(mts):
            ws, tw = WS[idx % 3], t_ws[idx % 3]
            for bi in range(2):
                bank = (2 * idx + bi) % 4
                for kc in range(32):
                    mm(psb[bank], ws[:, kc, :], XG[bi][:, kc, :], kc == 0, kc == 31, [tw, t_xg[bi][kc // 4]], [tps[bank]])
                eb = nev % 2
                nev += 1
                P.add("dve", lambda e, bank=bank, bi=bi, eb=eb: e.tensor_tensor(
                    out=ev[eb], in0=psb[bank], in1=rb[bi], op=ALU.mult), reads=[tps[bank], t_rb[bi]], writes=[t_ev[eb]])
                if m < 13:
                    j = m - 5
                    P.add("act", lambda e, eb=eb: e.activation(out=sq2[eb], in_=ev[eb], func=AF.Square),
                          reads=[t_ev[eb]], writes=[t_sq2[eb]])
                    mm(psb[4 + bi], ones_bf, sq2[eb], j == 0, j == 7, [t_sq2[eb], t_const], [tps[4 + bi]])
                    P.add("dve", lambda e, eb=eb, j=j, bi=bi: e.tensor_scalar(
                        out=CQG[:, j, bi * 512:(bi + 1) * 512], in0=ev[eb], scalar1=g_q_t[:, j:j + 1], scalar2=None,
                        op0=ALU.mult), reads=[t_ev[eb], t_const], writes=[t_cqg[j][bi]])
                else:
                    hh = m - 13
                    P.add("act", lambda e, eb=eb, hh=hh, bi=bi: e.activation(
                        out=Y[:, 16 + hh, bi * 512:(bi + 1) * 512], in_=ev[eb], func=AF.Silu),
                        reads=[t_ev[eb]], writes=[t_y[16 + hh][bi]])
            if idx + 3 < len(mts):
                wload(idx + 3, mts[idx + 3])
            if m == 12:
                for bi in range(2):
                    rstd_from_psum(psb[4 + bi], tps[4 + bi], 1.0 / 1024, sqt2a, t_sqt, rqb[bi], t_rqb[bi])
                    P.add("dve", lambda e, bi=bi: e.tensor_tensor(out=csq[bi], in0=cs[bi], in1=rqb[bi], op=ALU.mult),
                          reads=[t_cs[bi], t_rqb[bi]], writes=[t_csq[bi]])
        P.barrier()

        o4 = R_XG
        KH = [v16(o4 + b * 8192, 4096) for b in range(2)]
        VH = v16(o4 + 16384, 8192).rearrange("p (t d) -> p t d", t=32)
        QN = [v16(o4 + 32768 + b * 2048, 1024).rearrange("p (b t) -> p b t", b=2) for b in range(2)]
        TQ = [v16(o4 + 36864 + b * 2048, 1024).rearrange("p (b t) -> p b t", b=2) for b in range(2)]
        PT = [v16(o4 + 40960 + b * 1024, 512) for b in range(4)]
        rc = [v32(o4 + 45056 + b * 2048, 512) for b in range(2)]
        WQ = [v16(o4 + 49152 + b * 4096, 2048).rearrange("p (k c) -> p k c", k=8) for b in range(2)]
        WKV = [v16(o4 + 57344 + b * 4096, 2048).rearrange("p (k c) -> p k c", k=4) for b in range(2)]
        t_kh = [[Tr() for _ in range(8)] for _ in range(2)]
        t_vh = [Tr() for _ in range(8)]
        t_qn = [[Tr(), Tr()] for _ in range(2)]
        t_tq = [[Tr(), Tr()] for _ in range(2)]
        t_pt = [Tr() for _ in range(4)]
        t_rc = [Tr(), Tr()]
        t_wq = [Tr(), Tr()]
        t_wkv = [Tr(), Tr()]

        def hload(hh):
            P.dma("pool", lambda e: e.dma_start(out=WQ[hh % 2], in_=w_uq_t[hh], max_dma_last_dim=8192), writes=[t_wq[hh % 2]])

        def pload(pr):
            P.dma("pool", lambda e: e.dma_start(out=WKV[pr % 2], in_=w_ukv_t[pr], max_dma_last_dim=8192), writes=[t_wkv[pr % 2]])

        hload(0)
        hload(1)
        pload(0)
        pload(1)
        stg = v32(188416 + 20480, 512)
        t_stg = Tr()
        XGA = v16(R_Y, 16384).rearrange("p (k t) -> p k t", k=32)
        t_xga = [Tr() for _ in range(32)]
        npt = 0
        nmisc = 0
        def pf_dma(kc):
            P.dma("sp", lambda e: e.dma_start(out=stg, in_=xT[3, :, kc, :]), writes=[t_stg])

        def pf_mul(kc):
            P.add("dve", lambda e: e.tensor_scalar(
                out=XGA[:, kc, :], in0=stg, scalar1=g_in_t[:, kc:kc + 1], scalar2=None, op0=ALU.mult),
                reads=[t_stg, t_const], writes=[t_xga[kc]])

        for hh in range(16):
            hb = hh % 2
            pr, hp = divmod(hh, 2)
            pf_dma(2 * hh)
            wq, wkv, twkv = WQ[hb], WKV[pr % 2], t_wkv[pr % 2]
            for bi in range(2):
                for part in range(2):
                    bank = nmisc % 2
                    nmisc += 1
                    for kc in range(8):
                        mm(psb[bank], wq[:, kc, part * 128:(part + 1) * 128], CQG[:, kc, bi * 512:(bi + 1) * 512],
                           kc == 0, kc == 7, [t_wq[hb], t_cqg[kc][bi]], [tps[bank]])
                    if part == 0:
                        P.add("dve", lambda e, bank=bank, hb=hb, bi=bi: e.tensor_tensor(
                            out=QN[hb][:, bi, :], in0=psb[bank], in1=rqb[bi], op=ALU.mult),
                            reads=[tps[bank], t_rqb[bi]], writes=[t_qn[hb][bi]])
                    else:
                        P.add("dve", lambda e, bank=bank, hb=hb, bi=bi: e.tensor_tensor(
                            out=TQ[hb][:, bi, :], in0=psb[bank], in1=csq[bi], op=ALU.mult),
                            reads=[tps[bank], t_csq[bi]], writes=[t_tq[hb][bi]])
            for s in range(8):
                bank = nmisc % 2
                nmisc += 1
                for kc in range(4):
                    mm(psb[bank], wkv[:, kc, hp * 128:(hp + 1) * 128], CKVN[:, kc, s * 512:(s + 1) * 512], kc == 0, kc == 3,
                       [twkv, t_ckvn[s]], [tps[bank]])
                P.add("act", lambda e, bank=bank, hb=hb, s=s: e.activation(
                    out=KH[hb][:, s * 512:(s + 1) * 512], in_=psb[bank], func=AF.Copy),
                    reads=[tps[bank]], writes=[t_kh[hb][s]])
            if hp == 0:
                for s_ in range(8):
                    for half in range(2):
                        bank = nmisc % 2
                        nmisc += 1
                        for t2 in range(2):
                            tt = 2 * half + t2
                            for kc in range(4):
                                mm(psb[bank][:, t2 * 256:(t2 + 1) * 256],
                                   CKVN[:, kc, s_ * 512 + tt * 128: s_ * 512 + (tt + 1) * 128], wkv[:, kc, 256:512],
                                   kc == 0, kc == 3, [twkv, t_ckvn[s_]], [tps[bank]])
                        P.add("dve", lambda e, bank=bank, s_=s_, half=half: e.tensor_copy(
                            out=VH[:, 4 * s_ + 2 * half:4 * s_ + 2 * half + 2, :],
                            in_=psb[bank].rearrange("p (t d) -> p t d", t=2)),
                            reads=[tps[bank]], writes=[t_vh[s_]])
            for bi in range(2):
                if bi == 0:
                    units = [(0, 0), (1, 1), (2, 2), (3, None)]
                else:
                    units = [(0, None), (1, None), (2, None), (3, None), (4, 3), (5, 4), (6, 5), (7, None)]
                diag_slot = 3 if bi == 0 else 7
                kts = []
                for (s, gc) in units:
                    for j in range(4):
                        kts.append((s, j, gc, s == diag_slot))
                po, pl = 4 + 2 * bi, 5 + 2 * bi
                n = len(kts)

                def QK(i):
                    s, j, gc, dg = kts[i]
                    c0 = 128 * j if dg else 0
                    bank = 2 + (i % 2)
                    kcol = (4 * s + j) * 128
                    mm(psb[bank][:, c0:512], KH[hb][:, kcol:kcol + 128], QN[hb][:, bi, c0:512], True, False,
                       [t_kh[hb][s], t_qn[hb][bi]], [tps[bank]])
                    mm(psb[bank][:, c0:512], K2[:, kcol:kcol + 128], TQ[hb][:, bi, c0:512], False, True,
                       [t_k2[s], t_tq[hb][bi]], [tps[bank]])

                QK(0)
                for i in range(n):
                    if i + 1 < n:
                        QK(i + 1)
                    s, j, gc, dg = kts[i]
                    c0 = 128 * j if dg else 0
                    bank = 2 + (i % 2)
                    pb = npt % 4
                    npt += 1
                    if gc is None:
                        P.add("act", lambda e, bank=bank, pb=pb, c0=c0: e.activation(
                            out=PT[pb][:, c0:512], in_=psb[bank][:, c0:512], func=AF.Exp, scale=ATTN_SCALE),
                            reads=[tps[bank]], writes=[t_pt[pb]])
                    else:
                        P.add("act", lambda e, bank=bank, pb=pb, gc=gc: e.activation(
                            out=PT[pb], in_=psb[bank], func=AF.Exp, scale=ATTN_SCALE, bias=gate[:, gc:gc + 1]),
                            reads=[tps[bank], t_const], writes=[t_pt[pb]])
                    if dg:
                        P.add("dve", lambda e, pb=pb, c0=c0: e.memset(PT[pb][64:128, c0:c0 + 64], 0.0),
                              reads=[t_pt[pb]], writes=[t_pt[pb]])
                    mm(psb[po][:, c0:512], VH[:, 4 * s + j, hp * 128:(hp + 1) * 128], PT[pb][:, c0:512], i == 0, i == n - 1,
                       [t_vh[s], t_pt[pb]], [tps[po]])
                    mm(psb[pl][:, c0:512], ones_bf, PT[pb][:, c0:512], i == 0, i == n - 1,
                       [t_const, t_pt[pb]], [tps[pl]])
                P.add("dve", lambda e, pl=pl, bi=bi: e.reciprocal(out=rc[bi], in_=psb[pl]), reads=[tps[pl]], writes=[t_rc[bi]])
                P.add("dve", lambda e, po=po, bi=bi: e.tensor_tensor(out=rc[bi], in0=psb[po], in1=rc[bi], op=ALU.mult),
                      reads=[tps[po], t_rc[bi]], writes=[t_rc[bi]])
                P.add("dve", lambda e, bi=bi, hh=hh: e.tensor_tensor(
                    out=Y[:, 16 + hh, bi * 512:(bi + 1) * 512], in0=rc[bi], in1=Y[:, 16 + hh, bi * 512:(bi + 1) * 512],
                    op=ALU.mult), reads=[t_rc[bi], t_y[16 + hh][bi]], writes=[t_y[16 + hh][bi]])
                if bi == 0:
                    pf_mul(2 * hh)
                    pf_dma(2 * hh + 1)
            if hh + 2 < 16:
                hload(hh + 2)
            if hp == 1 and pr + 2 < 8:
                pload(pr + 2)
            pf_mul(2 * hh + 1)
        P.barrier()

        WS, t_ws = wstream(R_CKVN)
        o3 = R_CKVN + 24576
        T1 = [v32(o3 + b * 8320, 512) for b in range(2)]
        UU = [v32(o3 + b * 8320 + 2048, 516) for b in range(2)]
        CC = [v32(o3 + b * 8320 + 4112, 512) for b in range(2)]
        GG = [v32(o3 + b * 8320 + 6160, 512) for b in range(2)]
        t_t1 = [Tr(), Tr()]
        t_uu = [Tr(), Tr()]
        t_cc = [Tr(), Tr()]
        t_gg = [Tr(), Tr()]
        o3b = o3 + 2 * 8320
        rb2 = [v32(o3b + b * 2048, 512) for b in range(2)]
        t_rb2 = [Tr(), Tr()]
        o3c = o3b + 4096
        xst1 = [v32(R_XG + b * 4096, 1024).rearrange("p (k t) -> p k t", k=2) for b in range(8)]
        t_xst1 = [Tr() for _ in range(8)]
        o3d = o3c + 8192
        xhs = v32(o3d, 128).rearrange("p (k t) -> p k t", k=32)
        xgh = v16(o3d + 512, 128).rearrange("p (k t) -> p k t", k=32)
        xhq = v16(o3d + 768, 128).rearrange("p (k t) -> p k t", k=32)
        th = v32(o3d + 1024, 8)
        assert o3d + 1024 + 32 <= R_MISC
        t_h = Tr()
        mts = list(range(29, 93))
        for idx in range(3):
            wload(idx, mts[idx])
        Yc = v16(R_XG, 16384).rearrange("p (f t) -> p f t", f=16)
        nr = 0
        for bi, s in ((1, 7),):
            for kg in range(16):
                stb = nr % 8
                nr += 1
                P.dma("sp", lambda e, s=s, kg=kg, stb=stb: e.dma_start(out=xst1[stb], in_=xT[s, :, 2 * kg:2 * kg + 2, :]),
                      writes=[t_xst1[stb]])
                for j in range(2):
                    kc = 2 * kg + j
                    P.add("dve", lambda e, bi=bi, kc=kc, j=j, stb=stb: e.tensor_scalar(
                        out=XG[bi][:, kc, :], in0=xst1[stb][:, j, :], scalar1=g_in_t[:, kc:kc + 1], scalar2=None,
                        op0=ALU.mult), reads=[t_xst1[stb], t_const], writes=[t_xg[bi][kc // 4]])
        for bi in range(2):
            P.add("dve", lambda e, bi=bi: e.tensor_tensor(out=rb2[bi], in0=rb[bi], in1=rb[bi], op=ALU.mult),
                  reads=[t_rb[bi]], writes=[t_rb2[bi]])
        P.dma("sp", lambda e: e.dma_start(out=xhs, in_=xh), writes=[t_h])
        P.add("dve", lambda e: e.tensor_tensor(out=xgh, in0=xhs, in1=g_in_t.unsqueeze(2).to_broadcast([128, 32, 4]),
                                               op=ALU.mult), reads=[t_h, t_const], writes=[t_h])
        P.add("act", lambda e: e.activation(out=xhq, in_=xhs, func=AF.Square), reads=[t_h], writes=[t_h])
        for kc in range(32):
            mm(psb[7][:, 0:4], ones_bf, xhq[:, kc, :], kc == 0, kc == 31, [t_h, t_const], [tps[7]])
        rstd_from_psum(psb[7], tps[7], 1.0 / D, sqt, t_sqt, rbh, t_small, n=4)
        P.add("dve", lambda e: e.tensor_tensor(out=rbh2, in0=rbh, in1=rbh, op=ALU.mult), reads=[t_small], writes=[t_small])
        P.barrier()

        for idx, m in enumerate(mts):
            f, which = divmod(idx, 4)
            ws, tw = WS[idx % 3], t_ws[idx % 3]
            banks = [(2 * idx) % 6, (2 * idx + 1) % 6]
            hc = 4 * which
            for kc in range(32):
                mm(psb[banks[0]], ws[:, kc, :], XGA[:, kc, :], kc == 0, kc == 31, [tw, t_xga[kc]], [tps[banks[0]]])
                mm(psb[banks[1]], ws[:, kc, :], XG[1][:, kc, :], kc == 0, kc == 31, [tw, t_xg[1][kc // 4]], [tps[banks[1]]])
                if which < 2:
                    mm(psb[6 + (f % 2)][:, hc:hc + 4], ws[:, kc, :], xgh[:, kc, :], kc == 0, kc == 31, [tw, t_h],
                       [tps[6 + (f % 2)]])
            if idx + 3 < len(mts):
                wload(idx + 3, mts[idx + 3])
            hb_ = 6 + (f % 2)
            for bi in range(2):
                bank = banks[bi]
                tb = bi
                if which == 0:
                    P.add("act", lambda e, bank=bank, tb=tb: e.activation(out=T1[tb], in_=psb[bank], func=AF.Copy),
                          reads=[tps[bank]], writes=[t_t1[tb]])
                elif which == 1:
                    P.add("dve", lambda e, bank=bank, tb=tb: e.tensor_tensor(out=T1[tb], in0=psb[bank], in1=T1[tb], op=ALU.mult),
                          reads=[tps[bank], t_t1[tb]], writes=[t_t1[tb]])
                    P.add("dve", lambda e, tb=tb, bi=bi: e.tensor_tensor(out=UU[tb][:, 2:514], in0=T1[tb], in1=rb2[bi], op=ALU.mult),
                          reads=[t_t1[tb], t_rb2[bi]], writes=[t_uu[tb]])
                    if bi == 0:
                        P.add("act", lambda e, hb_=hb_: e.activation(out=th[:, 0:4], in_=psb[hb_][:, 0:4], func=AF.Copy),
                              reads=[tps[hb_]], writes=[t_small])
                        P.add("dve", lambda e, hb_=hb_: e.tensor_tensor(out=th[:, 0:4], in0=psb[hb_][:, 4:8], in1=th[:, 0:4], op=ALU.mult),
                              reads=[tps[hb_], t_small], writes=[t_small])
                        P.add("dve", lambda e: e.tensor_tensor(out=uh, in0=th[:, 0:4], in1=rbh2, op=ALU.mult),
                              reads=[t_small], writes=[t_small])
                    P.add("dve", lambda e, tb=tb, bi=bi: e.tensor_copy(out=UU[tb][:, 0:2], in_=uh[:, 2 * bi:2 * bi + 2]),
                          reads=[t_small, t_uu[tb]], writes=[t_uu[tb]])
                    cw = 3 * f
                    P.add("act", lambda e, tb=tb, cw=cw: e.activation(out=CC[tb], in_=UU[tb][:, 0:512], func=AF.Copy,
                                                                       scale=convw[:, cw:cw + 1]),
                          reads=[t_uu[tb], t_const], writes=[t_cc[tb]])
                    P.add("dve", lambda e, tb=tb, cw=cw: e.scalar_tensor_tensor(
                        out=CC[tb], in0=UU[tb][:, 1:513], scalar=convw[:, cw + 1:cw + 2], in1=CC[tb], op0=ALU.mult, op1=ALU.add),
                        reads=[t_uu[tb], t_cc[tb], t_const], writes=[t_cc[tb]])
                    P.add("dve", lambda e, tb=tb, cw=cw: e.scalar_tensor_tensor(
                        out=CC[tb], in0=UU[tb][:, 2:514], scalar=convw[:, cw + 2:cw + 3], in1=CC[tb], op0=ALU.mult, op1=ALU.add),
                        reads=[t_uu[tb], t_cc[tb], t_const], writes=[t_cc[tb]])
                elif which == 2:
                    P.add("dve", lambda e, bank=bank, tb=tb, bi=bi: e.tensor_tensor(out=GG[tb], in0=psb[bank], in1=rb[bi], op=ALU.mult),
                          reads=[tps[bank], t_rb[bi]], writes=[t_gg[tb]])
                    P.add("dve", lambda e, tb=tb: e.tensor_tensor(out=GG[tb], in0=GG[tb], in1=CC[tb], op=ALU.mult),
                          reads=[t_gg[tb], t_cc[tb]], writes=[t_gg[tb]])
                else:
                    P.add("dve", lambda e, bank=bank, tb=tb, bi=bi: e.tensor_tensor(out=T1[tb], in0=psb[bank], in1=rb[bi], op=ALU.mult),
                          reads=[tps[bank], t_rb[bi]], writes=[t_t1[tb]])
                    P.add("act", lambda e, tb=tb: e.activation(out=T1[tb], in_=T1[tb], func=AF.Silu),
                          reads=[t_t1[tb]], writes=[t_t1[tb]])
                    P.add("dve", lambda e, tb=tb, f=f, bi=bi: e.tensor_tensor(
                        out=Yc[:, f, bi * 512:(bi + 1) * 512], in0=GG[tb], in1=T1[tb], op=ALU.mult),
                        reads=[t_gg[tb], t_t1[tb]], writes=[t_y[f][bi]])
        P.barrier()

        Ht = [v32(R_Y + tt * 16384, 4096) for tt in range(2)] + [v32(R_XG + 32768 + tt * 16384, 4096) for tt in range(2)]
        t_hh = [Tr() for _ in range(4)]
        WO = [v16(R_CKVN + b * 4096, 2048).rearrange("p (k c) -> p k c", k=4) for b in range(4)]
        t_wo = [Tr() for _ in range(4)]
        XR = [v32(R_CKVN + 16384 + b * 8192, 2048).rearrange("p (t c) -> p t c", t=4) for b in range(2)]
        t_xr = [Tr(), Tr()]
        GF = v32(R_CKVN + 32768, 4096)
        t_gf = Tr()
        JUNK = v16(R_CKVN + 49152, 512)
        assert R_CKVN + 49152 + 1024 <= R_MISC
        t_out = [Tr("out%d" % i) for i in range(4)]
        t_ssq = [[Tr() for _ in range(8)] for _ in range(4)]
        P.dma("sp", lambda e: e.dma_start(out=GF, in_=gfin_d), writes=[t_gf])
        wlist = [(grp, ct, kg) for grp in range(2) for ct in range(8) for kg in range(8)]

        def woload(i):
            grp, ct, kg = wlist[i]
            P.dma("pool", lambda e: e.dma_start(out=WO[i % 4], in_=w_out_t[ct, kg], max_dma_last_dim=8192), writes=[t_wo[i % 4]])

        for i in range(4):
            woload(i)
        wi = 0
        for grp in range(2):
            for ct in range(8):
                xb = ct % 2
                P.dma("sp", lambda e, grp=grp, ct=ct, xb=xb: e.dma_start(
                    out=XR[xb], in_=xres[grp * 512:(grp + 1) * 512, ct * 512:(ct + 1) * 512].rearrange("(t p) c -> p t c", p=128)),
                    writes=[t_xr[xb]])
                pbase = 4 * (ct % 2)
                for kg in range(8):
                    wo, two = WO[wi % 4], t_wo[wi % 4]
                    for tt in range(4):
                        for kcc in range(4):
                            kc = 4 * kg + kcc
                            tok0 = grp * 512 + tt * 128
                            ysrc = Yc[:, kc, tok0:tok0 + 128] if kc < 16 else Y[:, kc, tok0:tok0 + 128]
                            mm(psb[pbase + tt], ysrc, wo[:, kcc, :], kc == 0, kc == 31,
                               [two, t_y[kc][grp]], [tps[pbase + tt]])
                    if wi + 4 < len(wlist):
                        woload(wi + 4)
                    wi += 1
                for tt in range(4):
                    P.add("dve", lambda e, pbase=pbase, tt=tt, ct=ct, xb=xb: e.tensor_tensor(
                        out=Ht[tt][:, ct * 512:(ct + 1) * 512], in0=psb[pbase + tt], in1=XR[xb][:, tt, :], op=ALU.add),
                        reads=[tps[pbase + tt], t_xr[xb]], writes=[t_hh[tt]])
                    P.add("act", lambda e, tt=tt, ct=ct: e.activation(
                        out=JUNK, in_=Ht[tt][:, ct * 512:(ct + 1) * 512], func=AF.Square,
                        accum_out=ssq[:, tt * 8 + ct: tt * 8 + ct + 1]),
                        reads=[t_hh[tt]], writes=[t_ssq[tt][ct]])
            P.add("dve", lambda e: e.reduce_sum(out=tot, in_=ssq.rearrange("p (t c) -> p t c", t=4), axis=AX.X),
                  reads=[t_small] + [t for row in t_ssq for t in row], writes=[t_small])
            P.add("act", lambda e: e.activation(out=tot, in_=tot, func=AF.Sqrt, scale=1.0 / D, bias=eps_ap),
                  reads=[t_small, t_const], writes=[t_small])
            P.add("dve", lambda e: e.reciprocal(out=rfin, in_=tot), reads=[t_small], writes=[t_small])
            for tt in range(4):
                for ct in range(8):
                    P.add("dve", lambda e, tt=tt, ct=ct: e.scalar_tensor_tensor(
                        out=Ht[tt][:, ct * 512:(ct + 1) * 512], in0=Ht[tt][:, ct * 512:(ct + 1) * 512],
                        scalar=rfin[:, tt:tt + 1], in1=GF[:, ct * 512:(ct + 1) * 512], op0=ALU.mult, op1=ALU.mult),
                        reads=[t_hh[tt], t_small, t_gf], writes=[t_hh[tt]])
                r0 = grp * 512 + tt * 128
                P.dma("sp", lambda e, tt=tt, r0=r0: e.dma_start(out=out[r0:r0 + 128, :], in_=Ht[tt]),
                      reads=[t_hh[tt]], sem_tr=t_out[tt])
        P.barrier()
        P.emit(nc, stack)
    return nc


def _tile_w(w, cols, kc):
    sub = w[:, cols]
    return np.ascontiguousarray(sub.reshape(kc, 128, len(cols)).transpose(1, 0, 2))


def _prep_weights(g_in, w_in, conv_w, q_norm_g, w_uq, kv_norm_g, w_ukv, w_out, g_final):
    w_in = w_in[0]
    ar = np.arange
    perm = np.concatenate([ar(0, 32), ar(32, 64), ar(32, 64), ar(0, 32)])
    col_tiles = []
    for m in range(4):
        col_tiles.append(9216 + m * 128 + ar(128))
    col_tiles.append(9728 + perm)
    for m in range(8):
        col_tiles.append(8192 + m * 128 + ar(128))
    for m in range(16):
        col_tiles.append(9792 + m * 128 + ar(128))
    for f in range(16):
        for base in (2048, 4096, 0, 6144):
            col_tiles.append(base + f * 128 + ar(128))
    assert len(col_tiles) == NMT
    w_in_t = np.empty((NMT, 128, 32, 128), np.float32)
    for m, cols in enumerate(col_tiles):
        w_in_t[m] = _tile_w(w_in, cols, 32)
    w_uq_t = np.empty((16, 128, 8, 256), np.float32)
    for h in range(16):
        cols = np.concatenate([h * 192 + ar(128), h * 192 + 128 + perm])
        w_uq_t[h] = _tile_w(w_uq[0], cols, 8)
    w_ukv_t = np.empty((8, 128, 4, 512), np.float32)
    for pr in range(8):
        h0, h1 = 2 * pr, 2 * pr + 1
        cols = np.concatenate([h0 * 256 + ar(128), h1 * 256 + ar(128), h0 * 256 + 128 + ar(128), h1 * 256 + 128 + ar(128)])
        w_ukv_t[pr] = _tile_w(w_ukv[0], cols, 4)
    w_out_t = np.ascontiguousarray(w_out[0].reshape(8, 4, 128, 8, 512).transpose(3, 0, 2, 1, 4))
    dmat = np.zeros((128, 128), np.float32)
    i64 = ar(64)
    for a in (0, 64):
        for b in (0, 64):
            dmat[a + i64, b + i64] = 1.0
    return {
        "w_in_t": w_in_t, "w_uq_t": w_uq_t, "w_ukv_t": w_ukv_t, "w_out_t": w_out_t,
        "g_in_t": np.ascontiguousarray(g_in[0].reshape(32, 128).T),
        "g_q_t": np.ascontiguousarray(q_norm_g[0].reshape(8, 128).T),
        "g_kv_t": np.ascontiguousarray(kv_norm_g[0].reshape(4, 128).T),
        "convw_t": np.ascontiguousarray(conv_w[0].reshape(3, 16, 128).transpose(2, 1, 0).reshape(128, 48)),
        "gfin_b": np.ascontiguousarray(np.broadcast_to(g_final[None, :], (128, D))),
        "dmat": dmat,
    }


def _rope_table():
    pos = np.arange(S, dtype=np.float32)
    inv_freq = (1.0 / (np.float32(10000.0) ** (np.arange(0, 64, 2, dtype=np.float32) / np.float32(64)))).astype(np.float32)
    ang = (pos[:, None] * inv_freq[None, :]).astype(np.float32)
    c = np.cos(ang).astype(np.float32).T
    s = np.sin(ang).astype(np.float32).T
    return np.concatenate([c, c, -s, s], axis=0)


def _slots(k):
    low = [i for i in range(4) if i != k]
    high = [i for i in range(4, 8) if i != 7 - k]
    return low + [k] + high + [7 - k]


def kernel(x, g_in, w_in, conv_w, q_norm_g, w_uq, kv_norm_g, w_ukv, w_out, g_final):
    x = np.asarray(x, np.float32)
    wts = _prep_weights(*(np.asarray(a, np.float32) for a in
                          (g_in, w_in, conv_w, q_norm_g, w_uq, kv_norm_g, w_ukv, w_out, g_final)))
    cs_full = _rope_table()
    in_maps = []
    for c in range(NCORES):
        b, k = divmod(c, 4)
        sl = _slots(k)
        xb = x[b]
        xT = np.empty((8, 128, 32, 512), np.float32)
        cs_tab = np.empty((8, 128, 512), np.float32)
        for si, blk in enumerate(sl):
            xT[si] = xb[blk * BLK:(blk + 1) * BLK, :].reshape(512, 32, 128).transpose(2, 1, 0)
            cs_tab[si] = cs_full[:, blk * BLK:(blk + 1) * BLK]
        A, B = k, 7 - k
        xres = np.concatenate([xb[A * BLK:(A + 1) * BLK], xb[B * BLK:(B + 1) * BLK]], axis=0)
        halo = np.zeros((4, D), np.float32)
        if A > 0:
            halo[0:2] = xb[A * BLK - 2:A * BLK]
        halo[2:4] = xb[B * BLK - 2:B * BLK]
        xh = np.ascontiguousarray(halo.reshape(4, 32, 128).transpose(2, 1, 0))
        gate = np.zeros((128, 6), np.float32)
        for j in range(3):
            gate[:, j] = 0.0 if sl[j] < A else NEG
            gate[:, 3 + j] = 0.0 if sl[4 + j] < B else NEG
        m = {"xT": xT, "xres": np.ascontiguousarray(xres), "xh": xh, "cs_tab": cs_tab, "gate": gate}
        m.update(wts)
        in_maps.append(m)
    nc = build_program()
    res = run_bass_kernel_spmd(nc, in_maps, core_ids=list(range(NCORES)))
    outp = np.empty((2, S, D), np.float32)
    for c in range(NCORES):
        b, k = divmod(c, 4)
        o = res.results[c]["out"]
        outp[b, k * BLK:(k + 1) * BLK] = o[0:512]
        outp[b, (7 - k) * BLK:(8 - k) * BLK] = o[512:1024]
    return outp
```

```python
import contextlib
import numpy as np
import concourse.bass as bass
import concourse.mybir as mybir
from concourse.bass_utils import run_bass_kernel_spmd

F32 = mybir.dt.float32
BF16 = mybir.dt.bfloat16
AF = mybir.ActivationFunctionType
ALU = mybir.AluOpType
AX = mybir.AxisListType

NCORES = 8
D = 4096
S = 4096
BLK = 512
NMT = 93
EPS = 1e-6
ATTN_SCALE = 192 ** -0.5
ORDER = [0, 1, 2, 4, 5, 6, 3, 7]
NEG = -30000.0


class Tr:
    __slots__ = ("lastw", "readers", "sem", "cnt", "name")

    def __init__(self, name=""):
        self.lastw = None
        self.readers = []
        self.sem = None
        self.cnt = 0
        self.name = name


class Ins:
    __slots__ = ("eng", "fn", "deps", "signal", "ordinal", "dma_tr", "dma_val")

    def __init__(self, eng, fn, deps):
        self.eng = eng
        self.fn = fn
        self.deps = deps
        self.signal = False
        self.ordinal = 0
        self.dma_tr = None
        self.dma_val = 0


ENGS = ("pe", "act", "dve", "pool", "sp")


class Prog:
    def __init__(self):
        self.ins = {e: [] for e in ENGS}
        self.last = {e: None for e in ENGS}
        self.dma_trs = []
        self.sync_same_engine = True

    def _deps(self, eng, reads, writes):
        deps = []
        for t in reads:
            if t.lastw is not None:
                deps.append(t.lastw)
        for t in writes:
            if t.lastw is not None:
                deps.append(t.lastw)
            deps.extend(t.readers)
        out = []
        for d in deps:
            if d[0] == "c":
                i = d[1]
                if i.eng == "pe" and eng == "pe":
                    continue
                if i.eng == eng and not self.sync_same_engine:
                    continue
                i.signal = True
            out.append(d)
        return out

    def add(self, eng, fn, reads=(), writes=()):
        ins = Ins(eng, fn, self._deps(eng, reads, writes))
        ev = ("c", ins)
        for t in reads:
            t.readers.append(ev)
        for t in writes:
            t.lastw = ev
            t.readers = []
        self.ins[eng].append(ins)
        self.last[eng] = ins
        return ins

    def dma(self, eng, fn, reads=(), writes=(), sem_tr=None):
        ins = Ins(eng, fn, self._deps(eng, reads, writes))
        if sem_tr is None:
            sem_tr = writes[0]
        if sem_tr not in self.dma_trs:
            self.dma_trs.append(sem_tr)
        sem_tr.cnt += 16
        ins.dma_tr = sem_tr
        ins.dma_val = sem_tr.cnt
        ev = ("d", sem_tr, sem_tr.cnt)
        for t in reads:
            t.readers.append(ev)
        for t in writes:
            t.lastw = ev
            t.readers = []
        self.ins[eng].append(ins)
        return ins

    def barrier(self):
        evs = []
        for e in ("pe", "act", "dve", "pool"):
            if self.last[e] is not None:
                self.last[e].signal = True
                evs.append(("c", self.last[e]))
        for t in self.dma_trs:
            evs.append(("d", t, t.cnt))
        for e in ENGS:
            self.ins[e].append(Ins(e, None, list(evs)))

    def emit(self, nc, stack):
        esem = {e: stack.enter_context(nc.semaphore("sem_" + e)) for e in ("pe", "act", "dve", "pool")}
        for i, t in enumerate(self.dma_trs):
            t.sem = stack.enter_context(nc.semaphore("dsem%d" % i))
        for e in ("pe", "act", "dve", "pool"):
            n = 0
            for i in self.ins[e]:
                if i.signal:
                    n += 1
                    i.ordinal = n
        block = stack.enter_context(nc.Block())

        def run(ename, eng):
            waited = {}
            for i in self.ins[ename]:
                for d in i.deps:
                    if d[0] == "c":
                        sem, val, key = esem[d[1].eng], d[1].ordinal, d[1].eng
                    else:
                        sem, val, key = d[1].sem, d[2], id(d[1])
                    if waited.get(key, 0) >= val:
                        continue
                    waited[key] = val
                    eng.wait_ge(sem, val)
                if i.fn is None:
                    continue
                r = i.fn(eng)
                if i.dma_tr is not None:
                    r.then_inc(i.dma_tr.sem, 16)
                elif i.signal:
                    r.then_inc(esem[ename], 1)

        @block.tensor
        def _(e):
            run("pe", e)

        @block.scalar
        def _(e):
            run("act", e)

        @block.vector
        def _(e):
            run("dve", e)

        @block.gpsimd
        def _(e):
            run("pool", e)

        @block.sync
        def _(e):
            run("sp", e)


R_XG = 0
R_Y = 65536
R_CKVN = 131072
R_K2 = 163840
R_CQG = 172032
R_MISC = 188416
ARENA = 188416 + 20480 + 2048


def build_program(dbg=False):
    nc = bass.Bass("TRN2", target_bir_lowering=False)
    P = Prog()

    def din(name, shape):
        return nc.dram_tensor(name, list(shape), F32, kind="ExternalInput").ap()

    xT = din("xT", [8, 128, 32, 512])
    xres = din("xres", [1024, D])
    xh = din("xh", [128, 32, 4])
    cs_tab = din("cs_tab", [8, 128, 512])
    gate_d = din("gate", [128, 6])
    w_in_t = din("w_in_t", [NMT, 128, 32, 128])
    w_uq_t = din("w_uq_t", [16, 128, 8, 256])
    w_ukv_t = din("w_ukv_t", [8, 128, 4, 512])
    w_out_t = din("w_out_t", [8, 8, 128, 4, 512])
    g_in_d = din("g_in_t", [128, 32])
    g_q_d = din("g_q_t", [128, 8])
    g_kv_d = din("g_kv_t", [128, 4])
    convw_d = din("convw_t", [128, 48])
    gfin_d = din("gfin_b", [128, D])
    dmat_d = din("dmat", [128, 128])
    out = nc.dram_tensor("out", [1024, D], F32, kind="ExternalOutput").ap()

    stack = contextlib.ExitStack()
    with stack:
        arena = stack.enter_context(nc.sbuf_tensor("arena", [128, ARENA // 4], F32))
        psb_t = [stack.enter_context(nc.psum_tensor("psb%d" % i, [128, 512], F32)) for i in range(8)]
        psb = [t[:, :] for t in psb_t]
        tps = [Tr("ps%d" % i) for i in range(8)]

        def v32(off, n):
            return arena[:, off // 4: off // 4 + n]

        def v16(off, n):
            return arena[:, off // 4: off // 4 + n // 2].bitcast(BF16)

        mo = [R_MISC]

        def misc32(n):
            a = v32(mo[0], n)
            mo[0] += 4 * n
            return a

        def misc16(n):
            a = v16(mo[0], n)
            mo[0] += 2 * n
            return a

        g_in_t = misc32(32)
        g_q_t = misc32(8)
        g_kv_t = misc32(4)
        convw = misc32(48)
        gate = misc32(6)
        ssq = misc32(32)
        tot = misc32(4)
        rfin = misc32(4)
        rbh = misc32(4)
        rbh2 = misc32(4)
        uh = misc32(4)
        dmat = misc32(128)
        ones_bf = misc16(128)
        rb = [misc32(512) for _ in range(2)]
        cs = [misc32(512) for _ in range(2)]
        rqb = [misc32(512) for _ in range(2)]
        csq = [misc32(512) for _ in range(2)]
        rkvb = misc32(512)
        assert mo[0] <= ARENA, mo[0]
        t_const = Tr("const")
        t_rb = [Tr(), Tr()]
        t_cs = [Tr(), Tr()]
        t_rqb = [Tr(), Tr()]
        t_csq = [Tr(), Tr()]
        t_rkvb = Tr()
        t_small = Tr("small")

        XG = [v16(R_XG + b * 32768, 16384).rearrange("p (k t) -> p k t", k=32) for b in range(2)]
        t_xg = [[Tr() for _ in range(8)] for _ in range(2)]
        Y = v16(R_Y, 32768).rearrange("p (f t) -> p f t", f=32)
        t_y = [[Tr() for _ in range(2)] for _ in range(32)]
        CKVN = v16(R_CKVN, 16384).rearrange("p (k t) -> p k t", k=4)
        t_ckvn = [Tr() for _ in range(8)]
        K2 = v16(R_K2, 4096)
        t_k2 = [Tr() for _ in range(8)]
        CQG = v16(R_CQG, 8192).rearrange("p (k t) -> p k t", k=8)
        t_cqg = [[Tr() for _ in range(2)] for _ in range(8)]

        for dst, src in ((g_in_t, g_in_d), (g_q_t, g_q_d), (g_kv_t, g_kv_d), (convw, convw_d),
                         (gate, gate_d), (dmat, dmat_d)):
            P.dma("sp", lambda e, dst=dst, src=src: e.dma_start(out=dst, in_=src), writes=[Tr()])
        P.add("dve", lambda e: e.memset(ones_bf, 1.0), writes=[Tr()])

        def mm(out_, lhsT, rhs, start, stop, reads, writes):
            P.add("pe", lambda e: e.matmul(out_, lhsT, rhs, start=start, stop=stop), reads=reads, writes=writes)

        def rstd_from_psum(bank, tbank, scale, tmp, ttmp, dst, tdst, n=512):
            P.add("act", lambda e: e.activation(out=tmp[:, 0:n], in_=bank[:, 0:n], func=AF.Sqrt, scale=scale, bias=eps_ap),
                  reads=[tbank, t_const], writes=[ttmp])
            P.add("dve", lambda e: e.reciprocal(out=dst[:, 0:n], in_=tmp[:, 0:n]), reads=[ttmp], writes=[tdst])

        eps_ap = misc32(1) if False else None
        eps_ap = v32(mo[0], 1)
        mo[0] += 4
        P.add("dve", lambda e: e.memset(eps_ap, EPS), writes=[Tr()])
        P.barrier()

        WCKV = v16(R_Y, 32 * 640).rearrange("p (k c) -> p k c", k=32)
        t_wckv = Tr()
        o1 = R_Y + 40960
        xst = [v32(o1 + b * 4096, 1024).rearrange("p (k t) -> p k t", k=2) for b in range(4)]
        t_xst = [Tr() for _ in range(4)]
        ckvf = v32(o1 + 16384, 2048).rearrange("p (k t) -> p k t", k=4)
        t_ckvf = [Tr() for _ in range(4)]
        xsq = [v16(R_CQG + b * 2048, 1024).rearrange("p (k t) -> p k t", k=2) for b in range(4)]
        t_xsq = [Tr() for _ in range(4)]
        tk = v32(R_CQG + 8192, 512)
        t_tk = Tr()
        ckvsq = v16(R_CQG + 10240, 2048).rearrange("p (k t) -> p k t", k=4)
        t_ckvsq = Tr()
        sqt = v32(R_CQG + 14336, 512)
        t_sqt = Tr()
        t_xgk = [[Tr() for _ in range(32)] for _ in range(2)]
        for b_ in range(2):
            for g_ in range(8):
                t_xg[b_][g_] = None

        for m in range(5):
            P.dma("pool", lambda e, m=m: e.dma_start(out=WCKV[:, :, m * 128:(m + 1) * 128], in_=w_in_t[m],
                                                       max_dma_last_dim=8192), writes=[t_wckv])
        nchunk = [0]

        def load_slot(i):
            s = ORDER[i]
            xb = i % 2
            for kg in range(16):
                stb = nchunk[0] % 4
                nchunk[0] += 1
                P.dma("sp", lambda e, s=s, kg=kg, stb=stb: e.dma_start(out=xst[stb], in_=xT[s, :, 2 * kg:2 * kg + 2, :]),
                      writes=[t_xst[stb]])
                for j in range(2):
                    kc = 2 * kg + j
                    P.add("dve", lambda e, xb=xb, kc=kc, stb=stb, j=j: e.tensor_scalar(
                        out=XG[xb][:, kc, :], in0=xst[stb][:, j, :], scalar1=g_in_t[:, kc:kc + 1], scalar2=None,
                        op0=ALU.mult), reads=[t_xst[stb], t_const], writes=[t_xgk[xb][kc]])
                P.add("act", lambda e, stb=stb: e.activation(out=xsq[stb], in_=xst[stb], func=AF.Square),
                      reads=[t_xst[stb]], writes=[t_xsq[stb]])
                for j in range(2):
                    mm(psb[0], ones_bf, xsq[stb][:, j, :], kg == 0 and j == 0, kg == 15 and j == 1,
                       [t_xsq[stb], t_const], [tps[0]])
                yield
            P.add("act", lambda e, xb=xb: e.activation(out=rb[xb], in_=psb[0], func=AF.Sqrt, scale=1.0 / D, bias=eps_ap),
                  reads=[tps[0], t_const], writes=[t_rb[xb]])
            P.add("dve", lambda e, xb=xb: e.reciprocal(out=rb[xb], in_=rb[xb]), reads=[t_rb[xb]], writes=[t_rb[xb]])
            P.dma("sp", lambda e, s=s, xb=xb: e.dma_start(out=cs[xb], in_=cs_tab[s]), writes=[t_cs[xb]])
            yield

        def compute_slot(i):
            s = ORDER[i]
            xb = i % 2
            for m in range(5):
                bank = 1 + (m % 2)
                for kc in range(32):
                    mm(psb[bank], WCKV[:, kc, m * 128:(m + 1) * 128], XG[xb][:, kc, :], kc == 0, kc == 31,
                       [t_wckv, t_xgk[xb][kc]], [tps[bank]])
                    if kc % 16 == 15:
                        yield
                if m < 4:
                    P.add("dve", lambda e, bank=bank, m=m, xb=xb: e.tensor_tensor(
                        out=ckvf[:, m, :], in0=psb[bank], in1=rb[xb], op=ALU.mult),
                        reads=[tps[bank], t_rb[xb]], writes=[t_ckvf[m]])
                else:
                    P.add("dve", lambda e, bank=bank, xb=xb: e.tensor_tensor(
                        out=tk, in0=psb[bank], in1=rb[xb], op=ALU.mult), reads=[tps[bank], t_rb[xb]], writes=[t_tk])
                    P.add("dve", lambda e, xb=xb: e.tensor_tensor(out=tk, in0=tk, in1=cs[xb], op=ALU.mult),
                          reads=[t_tk, t_cs[xb]], writes=[t_tk])
            P.add("act", lambda e: e.activation(out=ckvsq, in_=ckvf, func=AF.Square), reads=t_ckvf, writes=[t_ckvsq])
            for j in range(4):
                mm(psb[3], ones_bf, ckvsq[:, j, :], j == 0, j == 3, [t_ckvsq, t_const], [tps[3]])
            P.add("act", lambda e: e.activation(out=rkvb, in_=psb[3], func=AF.Sqrt, scale=1.0 / 512, bias=eps_ap),
                  reads=[tps[3], t_const], writes=[t_rkvb])
            P.add("dve", lambda e: e.reciprocal(out=rkvb, in_=rkvb), reads=[t_rkvb], writes=[t_rkvb])
            mm(psb[4], dmat, tk, True, True, [t_tk, t_const], [tps[4]])
            yield
            for m in range(4):
                P.add("dve", lambda e, m=m, s=s: e.scalar_tensor_tensor(
                    out=CKVN[:, m, s * 512:(s + 1) * 512], in0=ckvf[:, m, :], scalar=g_kv_t[:, m:m + 1], in1=rkvb,
                    op0=ALU.mult, op1=ALU.mult), reads=[t_ckvf[m], t_rkvb, t_const], writes=[t_ckvn[s]])
            P.add("act", lambda e, s=s: e.activation(out=K2[:, s * 512:(s + 1) * 512], in_=psb[4], func=AF.Copy),
                  reads=[tps[4]], writes=[t_k2[s]])
            yield

        for _ in load_slot(0):
            pass
        for i in range(8):
            gens = [compute_slot(i)]
            if i + 1 < 8:
                gens.append(load_slot(i + 1))
            while gens:
                for g in list(gens):
                    try:
                        next(g)
                    except StopIteration:
                        gens.remove(g)
        P.barrier()
        for b_ in range(2):
            for g_ in range(8):
                t_xg[b_][g_] = Tr()

        def wstream(region_off, nslots=3):
            return ([v16(region_off + b * 8192, 4096).rearrange("p (k c) -> p k c", k=32) for b in range(nslots)],
                    [Tr() for _ in range(nslots)])

        WS, t_ws = wstream(R_Y)
        o2 = R_Y + 24576
        ev = [v32(o2 + b * 2048, 512) for b in range(2)]
        t_ev = [Tr(), Tr()]
        sq2 = [v16(o2 + 4096 + b * 1024, 512) for b in range(2)]
        t_sq2 = [Tr(), Tr()]
        sqt2a = v32(o2 + 6144, 512)
        mts = list(range(5, 29))

        def wload(idx, m):
            dst, tdst = WS[idx % 3], t_ws[idx % 3]
            P.dma("pool", lambda e: e.dma_start(out=dst, in_=w_in_t[m], max_dma_last_dim=8192), writes=[tdst])

        for idx in range(3):
            wload(idx, mts[idx])
        nev = 0
        for idx, m in enumerate(mts):
            ws, tw = WS[idx % 3], t_ws[idx % 3]
            for bi in range(2):
                bank = (2 * idx + bi) % 4
                for kc in range(32):
                    mm(psb[bank], ws[:, kc, :], XG[bi][:, kc, :], kc == 0, kc == 31, [tw, t_xg[bi][kc // 4]], [tps[bank]])
                eb = nev % 2
                nev += 1
                P.add("dve", lambda e, bank=bank, bi=bi, eb=eb: e.tensor_tensor(
                    out=ev[eb], in0=psb[bank], in1=rb[bi], op=ALU.mult), reads=[tps[bank], t_rb[bi]], writes=[t_ev[eb]])
                if m < 13:
                    j = m - 5
                    P.add("act", lambda e, eb=eb: e.activation(out=sq2[eb], in_=ev[eb], func=AF.Square),
                          reads=[t_ev[eb]], writes=[t_sq2[eb]])
                    mm(psb[4 + bi], ones_bf, sq2[eb], j == 0, j == 7, [t_sq2[eb], t_const], [tps[4 + bi]])
                    P.add("dve", lambda e, eb=eb, j=j, bi=bi: e.tensor_scalar(
                        out=CQG[:, j, bi * 512:(bi + 1) * 512], in0=ev[eb], scalar1=g_q_t[:, j:j + 1], scalar2=None,
                        op0=ALU.mult), reads=[t_ev[eb], t_const], writes=[t_cqg[j][bi]])
                else:
                    hh = m - 13
                    P.add("act", lambda e, eb=eb, hh=hh, bi=bi: e.activation(
                        out=Y[:, 16 + hh, bi * 512:(bi + 1) * 512], in_=ev[eb], func=AF.Silu),
                        reads=[t_ev[eb]], writes=[t_y[16 + hh][bi]])
            if idx + 3 < len(mts):
                wload(idx + 3, mts[idx + 3])
            if m == 12:
                for bi in range(2):
                    rstd_from_psum(psb[4 + bi], tps[4 + bi], 1.0 / 1024, sqt2a, t_sqt, rqb[bi], t_rqb[bi])
                    P.add("dve", lambda e, bi=bi: e.tensor_tensor(out=csq[bi], in0=cs[bi], in1=rqb[bi], op=ALU.mult),
                          reads=[t_cs[bi], t_rqb[bi]], writes=[t_csq[bi]])
        P.barrier()

        o4 = R_XG
        KH = [v16(o4 + b * 8192, 4096) for b in range(2)]
        VH = v16(o4 + 16384, 8192).rearrange("p (t d) -> p t d", t=32)
        QN = [v16(o4 + 32768 + b * 2048, 1024).rearrange("p (b t) -> p b t", b=2) for b in range(2)]
        TQ = [v16(o4 + 36864 + b * 2048, 1024).rearrange("p (b t) -> p b t", b=2) for b in range(2)]
        PT = [v16(o4 + 40960 + b * 1024, 512) for b in range(4)]
        rc = [v32(o4 + 45056 + b * 2048, 512) for b in range(2)]
        WQ = [v16(o4 + 49152 + b * 4096, 2048).rearrange("p (k c) -> p k c", k=8) for b in range(2)]
        WKV = [v16(o4 + 57344 + b * 4096, 2048).rearrange("p (k c) -> p k c", k=4) for b in range(2)]
        t_kh = [[Tr() for _ in range(8)] for _ in range(2)]
        t_vh = [Tr() for _ in range(8)]
        t_qn = [[Tr(), Tr()] for _ in range(2)]
        t_tq = [[Tr(), Tr()] for _ in range(2)]
        t_pt = [Tr() for _ in range(4)]
        t_rc = [Tr(), Tr()]
        t_wq = [Tr(), Tr()]
        t_wkv = [Tr(), Tr()]

        def hload(hh):
            P.dma("pool", lambda e: e.dma_start(out=WQ[hh % 2], in_=w_uq_t[hh], max_dma_last_dim=8192), writes=[t_wq[hh % 2]])

        def pload(pr):
            P.dma("pool", lambda e: e.dma_start(out=WKV[pr % 2], in_=w_ukv_t[pr], max_dma_last_dim=8192), writes=[t_wkv[pr % 2]])

        hload(0)
        hload(1)
        pload(0)
        pload(1)
        stg = v32(188416 + 20480, 512)
        t_stg = Tr()
        XGA = v16(R_Y, 16384).rearrange("p (k t) -> p k t", k=32)
        t_xga = [Tr() for _ in range(32)]
        npt = 0
        nmisc = 0
        def pf_dma(kc):
            P.dma("sp", lambda e: e.dma_start(out=stg, in_=xT[3, :, kc, :]), writes=[t_stg])

        def pf_mul(kc):
            P.add("dve", lambda e: e.tensor_scalar(
                out=XGA[:, kc, :], in0=stg, scalar1=g_in_t[:, kc:kc + 1], scalar2=None, op0=ALU.mult),
                reads=[t_stg, t_const], writes=[t_xga[kc]])

        for hh in range(16):
            hb = hh % 2
            pr, hp = divmod(hh, 2)
            pf_dma(2 * hh)
            wq, wkv, twkv = WQ[hb], WKV[pr % 2], t_wkv[pr % 2]
            for bi in range(2):
                for part in range(2):
                    bank = nmisc % 2
                    nmisc += 1
                    for kc in range(8):
                        mm(psb[bank], wq[:, kc, part * 128:(part + 1) * 128], CQG[:, kc, bi * 512:(bi + 1) * 512],
                           kc == 0, kc == 7, [t_wq[hb], t_cqg[kc][bi]], [tps[bank]])
                    if part == 0:
                        P.add("dve", lambda e, bank=bank, hb=hb, bi=bi: e.tensor_tensor(
                            out=QN[hb][:, bi, :], in0=psb[bank], in1=rqb[bi], op=ALU.mult),
                            reads=[tps[bank], t_rqb[bi]], writes=[t_qn[hb][bi]])
                    else:
                        P.add("dve", lambda e, bank=bank, hb=hb, bi=bi: e.tensor_tensor(
                            out=TQ[hb][:, bi, :], in0=psb[bank], in1=csq[bi], op=ALU.mult),
                            reads=[tps[bank], t_csq[bi]], writes=[t_tq[hb][bi]])
            for s in range(8):
                bank = nmisc % 2
                nmisc += 1
                for kc in range(4):
                    mm(psb[bank], wkv[:, kc, hp * 128:(hp + 1) * 128], CKVN[:, kc, s * 512:(s + 1) * 512], kc == 0, kc == 3,
                       [twkv, t_ckvn[s]], [tps[bank]])
                P.add("act", lambda e, bank=bank, hb=hb, s=s: e.activation(
                    out=KH[hb][:, s * 512:(s + 1) * 512], in_=psb[bank], func=AF.Copy),
                    reads=[tps[bank]], writes=[t_kh[hb][s]])
            if hp == 0:
                for s_ in range(8):
                    for half in range(2):
                        bank = nmisc % 2
                        nmisc += 1
                        for t2 in range(2):
                            tt = 2 * half + t2
                            for kc in range(4):
                                mm(psb[bank][:, t2 * 256:(t2 + 1) * 256],
                                   CKVN[:, kc, s_ * 512 + tt * 128: s_ * 512 + (tt + 1) * 128], wkv[:, kc, 256:512],
                                   kc == 0, kc == 3, [twkv, t_ckvn[s_]], [tps[bank]])
                        P.add("dve", lambda e, bank=bank, s_=s_, half=half: e.tensor_copy(
                            out=VH[:, 4 * s_ + 2 * half:4 * s_ + 2 * half + 2, :],
                            in_=psb[bank].rearrange("p (t d) -> p t d", t=2)),
                            reads=[tps[bank]], writes=[t_vh[s_]])
            for bi in range(2):
                if bi == 0:
                    units = [(0, 0), (1, 1), (2, 2), (3, None)]
                else:
                    units = [(0, None), (1, None), (2, None), (3, None), (4, 3), (5, 4), (6, 5), (7, None)]
                diag_slot = 3 if bi == 0 else 7
                kts = []
                for (s, gc) in units:
                    for j in range(4):
                        kts.append((s, j, gc, s == diag_slot))
                po, pl = 4 + 2 * bi, 5 + 2 * bi
                n = len(kts)

                def QK(i):
                    s, j, gc, dg = kts[i]
                    c0 = 128 * j if dg else 0
                    bank = 2 + (i % 2)
                    kcol = (4 * s + j) * 128
                    mm(psb[bank][:, c0:512], KH[hb][:, kcol:kcol + 128], QN[hb][:, bi, c0:512], True, False,
                       [t_kh[hb][s], t_qn[hb][bi]], [tps[bank]])
                    mm(psb[bank][:, c0:512], K2[:, kcol:kcol + 128], TQ[hb][:, bi, c0:512], False, True,
                       [t_k2[s], t_tq[hb][bi]], [tps[bank]])

                QK(0)
                for i in range(n):
                    if i + 1 < n:
                        QK(i + 1)
                    s, j, gc, dg = kts[i]
                    c0 = 128 * j if dg else 0
                    bank = 2 + (i % 2)
                    pb = npt % 4
                    npt += 1
                    if gc is None:
                        P.add("act", lambda e, bank=bank, pb=pb, c0=c0: e.activation(
                            out=PT[pb][:, c0:512], in_=psb[bank][:, c0:512], func=AF.Exp, scale=ATTN_SCALE),
                            reads=[tps[bank]], writes=[t_pt[pb]])
                    else:
                        P.add("act", lambda e, bank=bank, pb=pb, gc=gc: e.activation(
                            out=PT[pb], in_=psb[bank], func=AF.Exp, scale=ATTN_SCALE, bias=gate[:, gc:gc + 1]),
                            reads=[tps[bank], t_const], writes=[t_pt[pb]])
                    if dg:
                        P.add("dve", lambda e, pb=pb, c0=c0: e.memset(PT[pb][64:128, c0:c0 + 64], 0.0),
                              reads=[t_pt[pb]], writes=[t_pt[pb]])
                    mm(psb[po][:, c0:512], VH[:, 4 * s + j, hp * 128:(hp + 1) * 128], PT[pb][:, c0:512], i == 0, i == n - 1,
                       [t_vh[s], t_pt[pb]], [tps[po]])
                    mm(psb[pl][:, c0:512], ones_bf, PT[pb][:, c0:512], i == 0, i == n - 1,
                       [t_const, t_pt[pb]], [tps[pl]])
                P.add("dve", lambda e, pl=pl, bi=bi: e.reciprocal(out=rc[bi], in_=psb[pl]), reads=[tps[pl]], writes=[t_rc[bi]])
                P.add("dve", lambda e, po=po, bi=bi: e.tensor_tensor(out=rc[bi], in0=psb[po], in1=rc[bi], op=ALU.mult),
                      reads=[tps[po], t_rc[bi]], writes=[t_rc[bi]])
                P.add("dve", lambda e, bi=bi, hh=hh: e.tensor_tensor(
                    out=Y[:, 16 + hh, bi * 512:(bi + 1) * 512], in0=rc[bi], in1=Y[:, 16 + hh, bi * 512:(bi + 1) * 512],
                    op=ALU.mult), reads=[t_rc[bi], t_y[16 + hh][bi]], writes=[t_y[16 + hh][bi]])
                if bi == 0:
                    pf_mul(2 * hh)
                    pf_dma(2 * hh + 1)
            if hh + 2 < 16:
                hload(hh + 2)
            if hp == 1 and pr + 2 < 8:
                pload(pr + 2)
            pf_mul(2 * hh + 1)
        P.barrier()

        WS, t_ws = wstream(R_CKVN)
        o3 = R_CKVN + 24576
        T1 = [v32(o3 + b * 8320, 512) for b in range(2)]
        UU = [v32(o3 + b * 8320 + 2048, 516) for b in range(2)]
        CC = [v32(o3 + b * 8320 + 4112, 512) for b in range(2)]
        GG = [v32(o3 + b * 8320 + 6160, 512) for b in range(2)]
        t_t1 = [Tr(), Tr()]
        t_uu = [Tr(), Tr()]
        t_cc = [Tr(), Tr()]
        t_gg = [Tr(), Tr()]
        o3b = o3 + 2 * 8320
        rb2 = [v32(o3b + b * 2048, 512) for b in range(2)]
        t_rb2 = [Tr(), Tr()]
        o3c = o3b + 4096
        xst1 = [v32(R_XG + b * 4096, 1024).rearrange("p (k t) -> p k t", k=2) for b in range(8)]
        t_xst1 = [Tr() for _ in range(8)]
        o3d = o3c + 8192
        xhs = v32(o3d, 128).rearrange("p (k t) -> p k t", k=32)
        xgh = v16(o3d + 512, 128).rearrange("p (k t) -> p k t", k=32)
        xhq = v16(o3d + 768, 128).rearrange("p (k t) -> p k t", k=32)
        th = v32(o3d + 1024, 8)
        assert o3d + 1024 + 32 <= R_MISC
        t_h = Tr()
        mts = list(range(29, 93))
        for idx in range(3):
            wload(idx, mts[idx])
        Yc = v16(R_XG, 16384).rearrange("p (f t) -> p f t", f=16)
        nr = 0
        for bi, s in ((1, 7),):
            for kg in range(16):
                stb = nr % 8
                nr += 1
                P.dma("sp", lambda e, s=s, kg=kg, stb=stb: e.dma_start(out=xst1[stb], in_=xT[s, :, 2 * kg:2 * kg + 2, :]),
                      writes=[t_xst1[stb]])
                for j in range(2):
                    kc = 2 * kg + j
                    P.add("dve", lambda e, bi=bi, kc=kc, j=j, stb=stb: e.tensor_scalar(
                        out=XG[bi][:, kc, :], in0=xst1[stb][:, j, :], scalar1=g_in_t[:, kc:kc + 1], scalar2=None,
                        op0=ALU.mult), reads=[t_xst1[stb], t_const], writes=[t_xg[bi][kc // 4]])
        for bi in range(2):
            P.add("dve", lambda e, bi=bi: e.tensor_tensor(out=rb2[bi], in0=rb[bi], in1=rb[bi], op=ALU.mult),
                  reads=[t_rb[bi]], writes=[t_rb2[bi]])
        P.dma("sp", lambda e: e.dma_start(out=xhs, in_=xh), writes=[t_h])
        P.add("dve", lambda e: e.tensor_tensor(out=xgh, in0=xhs, in1=g_in_t.unsqueeze(2).to_broadcast([128, 32, 4]),
                                               op=ALU.mult), reads=[t_h, t_const], writes=[t_h])
        P.add("act", lambda e: e.activation(out=xhq, in_=xhs, func=AF.Square), reads=[t_h], writes=[t_h])
        for kc in range(32):
            mm(psb[7][:, 0:4], ones_bf, xhq[:, kc, :], kc == 0, kc == 31, [t_h, t_const], [tps[7]])
        rstd_from_psum(psb[7], tps[7], 1.0 / D, sqt, t_sqt, rbh, t_small, n=4)
        P.add("dve", lambda e: e.tensor_tensor(out=rbh2, in0=rbh, in1=rbh, op=ALU.mult), reads=[t_small], writes=[t_small])
        P.barrier()

        for idx, m in enumerate(mts):
            f, which = divmod(idx, 4)
            ws, tw = WS[idx % 3], t_ws[idx % 3]
            banks = [(2 * idx) % 6, (2 * idx + 1) % 6]
            hc = 4 * which
            for kc in range(32):
                mm(psb[banks[0]], ws[:, kc, :], XGA[:, kc, :], kc == 0, kc == 31, [tw, t_xga[kc]], [tps[banks[0]]])
                mm(psb[banks[1]], ws[:, kc, :], XG[1][:, kc, :], kc == 0, kc == 31, [tw, t_xg[1][kc // 4]], [tps[banks[1]]])
                if which < 2:
                    mm(psb[6 + (f % 2)][:, hc:hc + 4], ws[:, kc, :], xgh[:, kc, :], kc == 0, kc == 31, [tw, t_h],
                       [tps[6 + (f % 2)]])
            if idx + 3 < len(mts):
                wload(idx + 3, mts[idx + 3])
            hb_ = 6 + (f % 2)
            for bi in range(2):
                bank = banks[bi]
                tb = bi
                if which == 0:
                    P.add("act", lambda e, bank=bank, tb=tb: e.activation(out=T1[tb], in_=psb[bank], func=AF.Copy),
                          reads=[tps[bank]], writes=[t_t1[tb]])
                elif which == 1:
                    P.add("dve", lambda e, bank=bank, tb=tb: e.tensor_tensor(out=T1[tb], in0=psb[bank], in1=T1[tb], op=ALU.mult),
                          reads=[tps[bank], t_t1[tb]], writes=[t_t1[tb]])
                    P.add("dve", lambda e, tb=tb, bi=bi: e.tensor_tensor(out=UU[tb][:, 2:514], in0=T1[tb], in1=rb2[bi], op=ALU.mult),
                          reads=[t_t1[tb], t_rb2[bi]], writes=[t_uu[tb]])
                    if bi == 0:
                        P.add("act", lambda e, hb_=hb_: e.activation(out=th[:, 0:4], in_=psb[hb_][:, 0:4], func=AF.Copy),
                              reads=[tps[hb_]], writes=[t_small])
                        P.add("dve", lambda e, hb_=hb_: e.tensor_tensor(out=th[:, 0:4], in0=psb[hb_][:, 4:8], in1=th[:, 0:4], op=ALU.mult),
                              reads=[tps[hb_], t_small], writes=[t_small])
                        P.add("dve", lambda e: e.tensor_tensor(out=uh, in0=th[:, 0:4], in1=rbh2, op=ALU.mult),
                              reads=[t_small], writes=[t_small])
                    P.add("dve", lambda e, tb=tb, bi=bi: e.tensor_copy(out=UU[tb][:, 0:2], in_=uh[:, 2 * bi:2 * bi + 2]),
                          reads=[t_small, t_uu[tb]], writes=[t_uu[tb]])
                    cw = 3 * f
                    P.add("act", lambda e, tb=tb, cw=cw: e.activation(out=CC[tb], in_=UU[tb][:, 0:512], func=AF.Copy,
                                                                       scale=convw[:, cw:cw + 1]),
                          reads=[t_uu[tb], t_const], writes=[t_cc[tb]])
                    P.add("dve", lambda e, tb=tb, cw=cw: e.scalar_tensor_tensor(
                        out=CC[tb], in0=UU[tb][:, 1:513], scalar=convw[:, cw + 1:cw + 2], in1=CC[tb], op0=ALU.mult, op1=ALU.add),
                        reads=[t_uu[tb], t_cc[tb], t_const], writes=[t_cc[tb]])
                    P.add("dve", lambda e, tb=tb, cw=cw: e.scalar_tensor_tensor(
                        out=CC[tb], in0=UU[tb][:, 2:514], scalar=convw[:, cw + 2:cw + 3], in1=CC[tb], op0=ALU.mult, op1=ALU.add),
                        reads=[t_uu[tb], t_cc[tb], t_const], writes=[t_cc[tb]])
                elif which == 2:
                    P.add("dve", lambda e, bank=bank, tb=tb, bi=bi: e.tensor_tensor(out=GG[tb], in0=psb[bank], in1=rb[bi], op=ALU.mult),
                          reads=[tps[bank], t_rb[bi]], writes=[t_gg[tb]])
                    P.add("dve", lambda e, tb=tb: e.tensor_tensor(out=GG[tb], in0=GG[tb], in1=CC[tb], op=ALU.mult),
                          reads=[t_gg[tb], t_cc[tb]], writes=[t_gg[tb]])
                else:
                    P.add("dve", lambda e, bank=bank, tb=tb, bi=bi: e.tensor_tensor(out=T1[tb], in0=psb[bank], in1=rb[bi], op=ALU.mult),
                          reads=[tps[bank], t_rb[bi]], writes=[t_t1[tb]])
                    P.add("act", lambda e, tb=tb: e.activation(out=T1[tb], in_=T1[tb], func=AF.Silu),
                          reads=[t_t1[tb]], writes=[t_t1[tb]])
                    P.add("dve", lambda e, tb=tb, f=f, bi=bi: e.tensor_tensor(
                        out=Yc[:, f, bi * 512:(bi + 1) * 512], in0=GG[tb], in1=T1[tb], op=ALU.mult),
                        reads=[t_gg[tb], t_t1[tb]], writes=[t_y[f][bi]])
        P.barrier()

        Ht = [v32(R_Y + tt * 16384, 4096) for tt in range(2)] + [v32(R_XG + 32768 + tt * 16384, 4096) for tt in range(2)]
        t_hh = [Tr() for _ in range(4)]
        WO = [v16(R_CKVN + b * 4096, 2048).rearrange("p (k c) -> p k c", k=4) for b in range(4)]
        t_wo = [Tr() for _ in range(4)]
        XR = [v32(R_CKVN + 16384 + b * 8192, 2048).rearrange("p (t c) -> p t c", t=4) for b in range(2)]
        t_xr = [Tr(), Tr()]
        GF = v32(R_CKVN + 32768, 4096)
        t_gf = Tr()
        JUNK = v16(R_CKVN + 49152, 512)
        t_junk = Tr()
        assert R_CKVN + 49152 + 1024 <= R_MISC
        t_out = [Tr("out%d" % i) for i in range(4)]
        t_ssq = [[Tr() for _ in range(8)] for _ in range(4)]
        P.dma("sp", lambda e: e.dma_start(out=GF, in_=gfin_d), writes=[t_gf])
        wlist = [(grp, ct, kg) for grp in range(2) for ct in range(8) for kg in range(8)]

        def woload(i):
            grp, ct, kg = wlist[i]
            P.dma("pool", lambda e: e.dma_start(out=WO[i % 4], in_=w_out_t[ct, kg], max_dma_last_dim=8192), writes=[t_wo[i % 4]])

        for i in range(4):
            woload(i)
        wi = 0
        for grp in range(2):
            for ct in range(8):
                xb = ct % 2
                P.dma("sp", lambda e, grp=grp, ct=ct, xb=xb: e.dma_start(
                    out=XR[xb], in_=xres[grp * 512:(grp + 1) * 512, ct * 512:(ct + 1) * 512].rearrange("(t p) c -> p t c", p=128)),
                    writes=[t_xr[xb]])
                pbase = 4 * (ct % 2)
                for kg in range(8):
                    wo, two = WO[wi % 4], t_wo[wi % 4]
                    for tt in range(4):
                        for kcc in range(4):
                            kc = 4 * kg + kcc
                            tok0 = grp * 512 + tt * 128
                            ysrc = Yc[:, kc, tok0:tok0 + 128] if kc < 16 else Y[:, kc, tok0:tok0 + 128]
                            mm(psb[pbase + tt], ysrc, wo[:, kcc, :], kc == 0, kc == 31,
                               [two, t_y[kc][grp]], [tps[pbase + tt]])
                    if wi + 4 < len(wlist):
                        woload(wi + 4)
                    wi += 1
                for tt in range(4):
                    P.add("dve", lambda e, pbase=pbase, tt=tt, ct=ct, xb=xb: e.tensor_tensor(
                        out=Ht[tt][:, ct * 512:(ct + 1) * 512], in0=psb[pbase + tt], in1=XR[xb][:, tt, :], op=ALU.add),
                        reads=[tps[pbase + tt], t_xr[xb]], writes=[t_hh[tt]])
                    P.add("act", lambda e, tt=tt, ct=ct: e.activation(
                        out=JUNK, in_=Ht[tt][:, ct * 512:(ct + 1) * 512], func=AF.Square,
                        accum_out=ssq[:, tt * 8 + ct: tt * 8 + ct + 1]),
                        reads=[t_hh[tt]], writes=[t_ssq[tt][ct], t_junk])
            P.add("dve", lambda e: e.reduce_sum(out=tot, in_=ssq.rearrange("p (t c) -> p t c", t=4), axis=AX.X),
                  reads=[t_small] + [t for row in t_ssq for t in row], writes=[t_small])
            P.add("act", lambda e: e.activation(out=tot, in_=tot, func=AF.Sqrt, scale=1.0 / D, bias=eps_ap),
                  reads=[t_small, t_const], writes=[t_small])
            P.add("dve", lambda e: e.reciprocal(out=rfin, in_=tot), reads=[t_small], writes=[t_small])
            for tt in range(4):
                for ct in range(8):
                    P.add("dve", lambda e, tt=tt, ct=ct: e.scalar_tensor_tensor(
                        out=Ht[tt][:, ct * 512:(ct + 1) * 512], in0=Ht[tt][:, ct * 512:(ct + 1) * 512],
                        scalar=rfin[:, tt:tt + 1], in1=GF[:, ct * 512:(ct + 1) * 512], op0=ALU.mult, op1=ALU.mult),
                        reads=[t_hh[tt], t_small, t_gf], writes=[t_hh[tt]])
                r0 = grp * 512 + tt * 128
                P.dma("sp", lambda e, tt=tt, r0=r0: e.dma_start(out=out[r0:r0 + 128, :], in_=Ht[tt]),
                      reads=[t_hh[tt]], sem_tr=t_out[tt])
        P.barrier()
        P.emit(nc, stack)
    return nc


def _tile_w(w, cols, kc):
    sub = w[:, cols]
    return np.ascontiguousarray(sub.reshape(kc, 128, len(cols)).transpose(1, 0, 2))


def _prep_weights(g_in, w_in, conv_w, q_norm_g, w_uq, kv_norm_g, w_ukv, w_out, g_final):
    w_in = w_in[0]
    ar = np.arange
    perm = np.concatenate([ar(0, 32), ar(32, 64), ar(32, 64), ar(0, 32)])
    col_tiles = []
    for m in range(4):
        col_tiles.append(9216 + m * 128 + ar(128))
    col_tiles.append(9728 + perm)
    for m in range(8):
        col_tiles.append(8192 + m * 128 + ar(128))
    for m in range(16):
        col_tiles.append(9792 + m * 128 + ar(128))
    for f in range(16):
        for base in (2048, 4096, 0, 6144):
            col_tiles.append(base + f * 128 + ar(128))
    assert len(col_tiles) == NMT
    w_in_t = np.empty((NMT, 128, 32, 128), np.float32)
    for m, cols in enumerate(col_tiles):
        w_in_t[m] = _tile_w(w_in, cols, 32)
    w_uq_t = np.empty((16, 128, 8, 256), np.float32)
    for h in range(16):
        cols = np.concatenate([h * 192 + ar(128), h * 192 + 128 + perm])
        w_uq_t[h] = _tile_w(w_uq[0], cols, 8)
    w_ukv_t = np.empty((8, 128, 4, 512), np.float32)
    for pr in range(8):
        h0, h1 = 2 * pr, 2 * pr + 1
        cols = np.concatenate([h0 * 256 + ar(128), h1 * 256 + ar(128), h0 * 256 + 128 + ar(128), h1 * 256 + 128 + ar(128)])
        w_ukv_t[pr] = _tile_w(w_ukv[0], cols, 4)
    w_out_t = np.ascontiguousarray(w_out[0].reshape(8, 4, 128, 8, 512).transpose(3, 0, 2, 1, 4))
    dmat = np.zeros((128, 128), np.float32)
    i64 = ar(64)
    for a in (0, 64):
        for b in (0, 64):
            dmat[a + i64, b + i64] = 1.0
    return {
        "w_in_t": w_in_t, "w_uq_t": w_uq_t, "w_ukv_t": w_ukv_t, "w_out_t": w_out_t,
        "g_in_t": np.ascontiguousarray(g_in[0].reshape(32, 128).T),
        "g_q_t": np.ascontiguousarray(q_norm_g[0].reshape(8, 128).T),
        "g_kv_t": np.ascontiguousarray(kv_norm_g[0].reshape(4, 128).T),
        "convw_t": np.ascontiguousarray(conv_w[0].reshape(3, 16, 128).transpose(2, 1, 0).reshape(128, 48)),
        "gfin_b": np.ascontiguousarray(np.broadcast_to(g_final[None, :], (128, D))),
        "dmat": dmat,
    }


def _rope_table():
    pos = np.arange(S, dtype=np.float32)
    inv_freq = (1.0 / (np.float32(10000.0) ** (np.arange(0, 64, 2, dtype=np.float32) / np.float32(64)))).astype(np.float32)
    ang = (pos[:, None] * inv_freq[None, :]).astype(np.float32)
    c = np.cos(ang).astype(np.float32).T
    s = np.sin(ang).astype(np.float32).T
    return np.concatenate([c, c, -s, s], axis=0)


def _slots(k):
    low = [i for i in range(4) if i != k]
    high = [i for i in range(4, 8) if i != 7 - k]
    return low + [k] + high + [7 - k]


def kernel(x, g_in, w_in, conv_w, q_norm_g, w_uq, kv_norm_g, w_ukv, w_out, g_final):
    x = np.asarray(x, np.float32)
    wts = _prep_weights(*(np.asarray(a, np.float32) for a in
                          (g_in, w_in, conv_w, q_norm_g, w_uq, kv_norm_g, w_ukv, w_out, g_final)))
    cs_full = _rope_table()
    in_maps = []
    for c in range(NCORES):
        b, k = divmod(c, 4)
        sl = _slots(k)
        xb = x[b]
        xT = np.empty((8, 128, 32, 512), np.float32)
        cs_tab = np.empty((8, 128, 512), np.float32)
        for si, blk in enumerate(sl):
            xT[si] = xb[blk * BLK:(blk + 1) * BLK, :].reshape(512, 32, 128).transpose(2, 1, 0)
            cs_tab[si] = cs_full[:, blk * BLK:(blk + 1) * BLK]
        A, B = k, 7 - k
        xres = np.concatenate([xb[A * BLK:(A + 1) * BLK], xb[B * BLK:(B + 1) * BLK]], axis=0)
        halo = np.zeros((4, D), np.float32)
        if A > 0:
            halo[0:2] = xb[A * BLK - 2:A * BLK]
        halo[2:4] = xb[B * BLK - 2:B * BLK]
        xh = np.ascontiguousarray(halo.reshape(4, 32, 128).transpose(2, 1, 0))
        gate = np.zeros((128, 6), np.float32)
        for j in range(3):
            gate[:, j] = 0.0 if sl[j] < A else NEG
            gate[:, 3 + j] = 0.0 if sl[4 + j] < B else NEG
        m = {"xT": xT, "xres": np.ascontiguousarray(xres), "xh": xh, "cs_tab": cs_tab, "gate": gate}
        m.update(wts)
        in_maps.append(m)
    nc = build_program()
    res = run_bass_kernel_spmd(nc, in_maps, core_ids=list(range(NCORES)))
    outp = np.empty((2, S, D), np.float32)
    for c in range(NCORES):
        b, k = divmod(c, 4)
        o = res.results[c]["out"]
        outp[b, k * BLK:(k + 1) * BLK] = o[0:512]
        outp[b, (7 - k) * BLK:(8 - k) * BLK] = o[512:1024]
    return outp
```

```python
import contextlib
import numpy as np
import concourse.bass as bass
import concourse.mybir as mybir
from concourse.bass_utils import run_bass_kernel_spmd

F32 = mybir.dt.float32
BF16 = mybir.dt.bfloat16
AF = mybir.ActivationFunctionType
ALU = mybir.AluOpType
AX = mybir.AxisListType

NCORES = 8
D = 4096
S = 4096
BLK = 512
NMT = 93
EPS = 1e-6
ATTN_SCALE = 192 ** -0.5
ORDER = [0, 1, 2, 4, 5, 6, 3, 7]
NEG = -30000.0


class Tr:
    __slots__ = ("lastw", "readers", "sem", "cnt", "name")

    def __init__(self, name=""):
        self.lastw = None
        self.readers = []
        self.sem = None
        self.cnt = 0
        self.name = name


class Ins:
    __slots__ = ("eng", "fn", "deps", "signal", "ordinal", "dma_tr", "dma_val")

    def __init__(self, eng, fn, deps):
        self.eng = eng
        self.fn = fn
        self.deps = deps
        self.signal = False
        self.ordinal = 0
        self.dma_tr = None
        self.dma_val = 0


ENGS = ("pe", "act", "dve", "pool", "sp")


class Prog:
    def __init__(self):
        self.ins = {e: [] for e in ENGS}
        self.last = {e: None for e in ENGS}
        self.dma_trs = []
        self.sync_same_engine = True

    def _deps(self, eng, reads, writes):
        deps = []
        for t in reads:
            if t.lastw is not None:
                deps.append(t.lastw)
        for t in writes:
            if t.lastw is not None:
                deps.append(t.lastw)
            deps.extend(t.readers)
        out = []
        for d in deps:
            if d[0] == "c":
                i = d[1]
                if i.eng == "pe" and eng == "pe":
                    continue
                if i.eng == eng and not self.sync_same_engine:
                    continue
                i.signal = True
            out.append(d)
        return out

    def add(self, eng, fn, reads=(), writes=()):
        ins = Ins(eng, fn, self._deps(eng, reads, writes))
        ev = ("c", ins)
        for t in reads:
            t.readers.append(ev)
        for t in writes:
            t.lastw = ev
            t.readers = []
        self.ins[eng].append(ins)
        self.last[eng] = ins
        return ins

    def dma(self, eng, fn, reads=(), writes=(), sem_tr=None):
        ins = Ins(eng, fn, self._deps(eng, reads, writes))
        if sem_tr is None:
            sem_tr = writes[0]
        if sem_tr not in self.dma_trs:
            self.dma_trs.append(sem_tr)
        sem_tr.cnt += 16
        ins.dma_tr = sem_tr
        ins.dma_val = sem_tr.cnt
        ev = ("d", sem_tr, sem_tr.cnt)
        for t in reads:
            t.readers.append(ev)
        for t in writes:
            t.lastw = ev
            t.readers = []
        self.ins[eng].append(ins)
        return ins

    def barrier(self):
        evs = []
        for e in ("pe", "act", "dve", "pool"):
            if self.last[e] is not None:
                self.last[e].signal = True
                evs.append(("c", self.last[e]))
        for t in self.dma_trs:
            evs.append(("d", t, t.cnt))
        for e in ENGS:
            self.ins[e].append(Ins(e, None, list(evs)))

    def emit(self, nc, stack):
        esem = {e: stack.enter_context(nc.semaphore("sem_" + e)) for e in ("pe", "act", "dve", "pool")}
        for i, t in enumerate(self.dma_trs):
            t.sem = stack.enter_context(nc.semaphore("dsem%d" % i))
        for e in ("pe", "act", "dve", "pool"):
            n = 0
            for i in self.ins[e]:
                if i.signal:
                    n += 1
                    i.ordinal = n
        block = stack.enter_context(nc.Block())

        def run(ename, eng):
            waited = {}
            for i in self.ins[ename]:
                for d in i.deps:
                    if d[0] == "c":
                        sem, val, key = esem[d[1].eng], d[1].ordinal, d[1].eng
                    else:
                        sem, val, key = d[1].sem, d[2], id(d[1])
                    if waited.get(key, 0) >= val:
                        continue
                    waited[key] = val
                    eng.wait_ge(sem, val)
                if i.fn is None:
                    continue
                r = i.fn(eng)
                if i.dma_tr is not None:
                    r.then_inc(i.dma_tr.sem, 16)
                elif i.signal:
                    r.then_inc(esem[ename], 1)

        @block.tensor
        def _(e):
            run("pe", e)

        @block.scalar
        def _(e):
            run("act", e)

        @block.vector
        def _(e):
            run("dve", e)

        @block.gpsimd
        def _(e):
            run("pool", e)

        @block.sync
        def _(e):
            run("sp", e)


R_XG = 0
R_Y = 65536
R_CKVN = 131072
R_K2 = 163840
R_CQG = 172032
R_MISC = 188416
ARENA = 188416 + 20480 + 2048


def build_program(dbg=False):
    nc = bass.Bass("TRN2", target_bir_lowering=False)
    P = Prog()

    def din(name, shape):
        return nc.dram_tensor(name, list(shape), F32, kind="ExternalInput").ap()

    xT = din("xT", [8, 128, 32, 512])
    xres = din("xres", [1024, D])
    xh = din("xh", [128, 32, 4])
    cs_tab = din("cs_tab", [8, 128, 512])
    gate_d = din("gate", [128, 6])
    w_in_t = din("w_in_t", [NMT, 128, 32, 128])
    w_uq_t = din("w_uq_t", [16, 128, 8, 256])
    w_ukv_t = din("w_ukv_t", [8, 128, 4, 512])
    w_out_t = din("w_out_t", [8, 8, 128, 4, 512])
    g_in_d = din("g_in_t", [128, 32])
    g_q_d = din("g_q_t", [128, 8])
    g_kv_d = din("g_kv_t", [128, 4])
    convw_d = din("convw_t", [128, 48])
    gfin_d = din("gfin_b", [128, D])
    dmat_d = din("dmat", [128, 128])
    out = nc.dram_tensor("out", [1024, D], F32, kind="ExternalOutput").ap()

    stack = contextlib.ExitStack()
    with stack:
        arena = stack.enter_context(nc.sbuf_tensor("arena", [128, ARENA // 4], F32))
        psb_t = [stack.enter_context(nc.psum_tensor("psb%d" % i, [128, 512], F32)) for i in range(8)]
        psb = [t[:, :] for t in psb_t]
        tps = [Tr("ps%d" % i) for i in range(8)]

        def v32(off, n):
            return arena[:, off // 4: off // 4 + n]

        def v16(off, n):
            return arena[:, off // 4: off // 4 + n // 2].bitcast(BF16)

        mo = [R_MISC]

        def misc32(n):
            a = v32(mo[0], n)
            mo[0] += 4 * n
            return a

        def misc16(n):
            a = v16(mo[0], n)
            mo[0] += 2 * n
            return a

        g_in_t = misc32(32)
        g_q_t = misc32(8)
        g_kv_t = misc32(4)
        convw = misc32(48)
        gate = misc32(6)
        ssq = misc32(32)
        tot = misc32(4)
        rfin = misc32(4)
        rbh = misc32(4)
        rbh2 = misc32(4)
        uh = misc32(4)
        dmat = misc32(128)
        ones_bf = misc16(128)
        rb = [misc32(512) for _ in range(2)]
        cs = [misc32(512) for _ in range(2)]
        rqb = [misc32(512) for _ in range(2)]
        csq = [misc32(512) for _ in range(2)]
        rkvb = misc32(512)
        assert mo[0] <= ARENA, mo[0]
        t_const = Tr("const")
        t_rb = [Tr(), Tr()]
        t_cs = [Tr(), Tr()]
        t_rqb = [Tr(), Tr()]
        t_csq = [Tr(), Tr()]
        t_rkvb = Tr()
        t_small = Tr("small")

        XG = [v16(R_XG + b * 32768, 16384).rearrange("p (k t) -> p k t", k=32) for b in range(2)]
        t_xg = [[Tr() for _ in range(8)] for _ in range(2)]
        Y = v16(R_Y, 32768).rearrange("p (f t) -> p f t", f=32)
        t_y = [[Tr() for _ in range(2)] for _ in range(32)]
        CKVN = v16(R_CKVN, 16384).rearrange("p (k t) -> p k t", k=4)
        t_ckvn = [Tr() for _ in range(8)]
        K2 = v16(R_K2, 4096)
        t_k2 = [Tr() for _ in range(8)]
        CQG = v16(R_CQG, 8192).rearrange("p (k t) -> p k t", k=8)
        t_cqg = [[Tr() for _ in range(2)] for _ in range(8)]

        for dst, src in ((g_in_t, g_in_d), (g_q_t, g_q_d), (g_kv_t, g_kv_d), (convw, convw_d),
                         (gate, gate_d), (dmat, dmat_d)):
            P.dma("sp", lambda e, dst=dst, src=src: e.dma_start(out=dst, in_=src), writes=[Tr()])
        P.add("dve", lambda e: e.memset(ones_bf, 1.0), writes=[Tr()])

        def mm(out_, lhsT, rhs, start, stop, reads, writes):
            P.add("pe", lambda e: e.matmul(out_, lhsT, rhs, start=start, stop=stop), reads=reads, writes=writes)

        def rstd_from_psum(bank, tbank, scale, tmp, ttmp, dst, tdst, n=512):
            P.add("act", lambda e: e.activation(out=tmp[:, 0:n], in_=bank[:, 0:n], func=AF.Sqrt, scale=scale, bias=eps_ap),
                  reads=[tbank, t_const], writes=[ttmp])
            P.add("dve", lambda e: e.reciprocal(out=dst[:, 0:n], in_=tmp[:, 0:n]), reads=[ttmp], writes=[tdst])

        eps_ap = misc32(1) if False else None
        eps_ap = v32(mo[0], 1)
        mo[0] += 4
        P.add("dve", lambda e: e.memset(eps_ap, EPS), writes=[Tr()])
        P.barrier()

        WCKV = v16(R_Y, 32 * 640).rearrange("p (k c) -> p k c", k=32)
        t_wckv = Tr()
        o1 = R_Y + 40960
        xst = [v32(o1 + b * 4096, 1024).rearrange("p (k t) -> p k t", k=2) for b in range(4)]
        t_xst = [Tr() for _ in range(4)]
        ckvf = v32(o1 + 16384, 2048).rearrange("p (k t) -> p k t", k=4)
        t_ckvf = [Tr() for _ in range(4)]
        xsq = [v16(R_CQG + b * 2048, 1024).rearrange("p (k t) -> p k t", k=2) for b in range(4)]
        t_xsq = [Tr() for _ in range(4)]
        tk = v32(R_CQG + 8192, 512)
        t_tk = Tr()
        ckvsq = v16(R_CQG + 10240, 2048).rearrange("p (k t) -> p k t", k=4)
        t_ckvsq = Tr()
        sqt = v32(R_CQG + 14336, 512)
        t_sqt = Tr()
        t_xgk = [[Tr() for _ in range(32)] for _ in range(2)]
        for b_ in range(2):
            for g_ in range(8):
                t_xg[b_][g_] = None

        for m in range(5):
            P.dma("pool", lambda e, m=m: e.dma_start(out=WCKV[:, :, m * 128:(m + 1) * 128], in_=w_in_t[m],
                                                       max_dma_last_dim=8192), writes=[t_wckv])
        nchunk = [0]

        def load_slot(i):
            s = ORDER[i]
            xb = i % 2
            for kg in range(16):
                stb = nchunk[0] % 4
                nchunk[0] += 1
                P.dma("sp", lambda e, s=s, kg=kg, stb=stb: e.dma_start(out=xst[stb], in_=xT[s, :, 2 * kg:2 * kg + 2, :]),
                      writes=[t_xst[stb]])
                for j in range(2):
                    kc = 2 * kg + j
                    P.add("dve", lambda e, xb=xb, kc=kc, stb=stb, j=j: e.tensor_scalar(
                        out=XG[xb][:, kc, :], in0=xst[stb][:, j, :], scalar1=g_in_t[:, kc:kc + 1], scalar2=None,
                        op0=ALU.mult), reads=[t_xst[stb], t_const], writes=[t_xgk[xb][kc]])
                P.add("act", lambda e, stb=stb: e.activation(out=xsq[stb], in_=xst[stb], func=AF.Square),
                      reads=[t_xst[stb]], writes=[t_xsq[stb]])
                for j in range(2):
                    mm(psb[0], ones_bf, xsq[stb][:, j, :], kg == 0 and j == 0, kg == 15 and j == 1,
                       [t_xsq[stb], t_const], [tps[0]])
                yield
            P.add("act", lambda e, xb=xb: e.activation(out=rb[xb], in_=psb[0], func=AF.Sqrt, scale=1.0 / D, bias=eps_ap),
                  reads=[tps[0], t_const], writes=[t_rb[xb]])
            P.add("dve", lambda e, xb=xb: e.reciprocal(out=rb[xb], in_=rb[xb]), reads=[t_rb[xb]], writes=[t_rb[xb]])
            P.dma("sp", lambda e, s=s, xb=xb: e.dma_start(out=cs[xb], in_=cs_tab[s]), writes=[t_cs[xb]])
            yield

        def compute_slot(i):
            s = ORDER[i]
            xb = i % 2
            for m in range(5):
                bank = (1, 2, 5, 6)[m % 4]
                for kc in range(32):
                    mm(psb[bank], WCKV[:, kc, m * 128:(m + 1) * 128], XG[xb][:, kc, :], kc == 0, kc == 31,
                       [t_wckv, t_xgk[xb][kc]], [tps[bank]])
                    if kc % 16 == 15:
                        yield
                if m < 4:
                    P.add("dve", lambda e, bank=bank, m=m, xb=xb: e.tensor_tensor(
                        out=ckvf[:, m, :], in0=psb[bank], in1=rb[xb], op=ALU.mult),
                        reads=[tps[bank], t_rb[xb]], writes=[t_ckvf[m]])
                else:
                    P.add("dve", lambda e, bank=bank, xb=xb: e.tensor_tensor(
                        out=tk, in0=psb[bank], in1=rb[xb], op=ALU.mult), reads=[tps[bank], t_rb[xb]], writes=[t_tk])
                    P.add("dve", lambda e, xb=xb: e.tensor_tensor(out=tk, in0=tk, in1=cs[xb], op=ALU.mult),
                          reads=[t_tk, t_cs[xb]], writes=[t_tk])
            P.add("act", lambda e: e.activation(out=ckvsq, in_=ckvf, func=AF.Square), reads=t_ckvf, writes=[t_ckvsq])
            for j in range(4):
                mm(psb[3], ones_bf, ckvsq[:, j, :], j == 0, j == 3, [t_ckvsq, t_const], [tps[3]])
            P.add("act", lambda e: e.activation(out=rkvb, in_=psb[3], func=AF.Sqrt, scale=1.0 / 512, bias=eps_ap),
                  reads=[tps[3], t_const], writes=[t_rkvb])
            P.add("dve", lambda e: e.reciprocal(out=rkvb, in_=rkvb), reads=[t_rkvb], writes=[t_rkvb])
            mm(psb[4], dmat, tk, True, True, [t_tk, t_const], [tps[4]])
            yield
            for m in range(4):
                P.add("dve", lambda e, m=m, s=s: e.scalar_tensor_tensor(
                    out=CKVN[:, m, s * 512:(s + 1) * 512], in0=ckvf[:, m, :], scalar=g_kv_t[:, m:m + 1], in1=rkvb,
                    op0=ALU.mult, op1=ALU.mult), reads=[t_ckvf[m], t_rkvb, t_const], writes=[t_ckvn[s]])
            P.add("act", lambda e, s=s: e.activation(out=K2[:, s * 512:(s + 1) * 512], in_=psb[4], func=AF.Copy),
                  reads=[tps[4]], writes=[t_k2[s]])
            yield

        for _ in load_slot(0):
            pass
        for i in range(8):
            gens = [compute_slot(i)]
            if i + 1 < 8:
                gens.append(load_slot(i + 1))
            while gens:
                for g in list(gens):
                    try:
                        next(g)
                    except StopIteration:
                        gens.remove(g)
        P.barrier()
        for b_ in range(2):
            for g_ in range(8):
                t_xg[b_][g_] = Tr()

        def wstream(region_off, nslots=3):
            return ([v16(region_off + b * 8192, 4096).rearrange("p (k c) -> p k c", k=32) for b in range(nslots)],
                    [Tr() for _ in range(nslots)])

        WS, t_ws = wstream(R_Y)
        o2 = R_Y + 24576
        ev = [v32(o2 + b * 2048, 512) for b in range(2)]
        t_ev = [Tr(), Tr()]
        sq2 = [v16(o2 + 4096 + b * 1024, 512) for b in range(2)]
        t_sq2 = [Tr(), Tr()]
        sqt2a = v32(o2 + 6144, 512)
        mts = list(range(5, 29))

        def wload(idx, m):
            dst, tdst = WS[idx % 3], t_ws[idx % 3]
            P.dma("pool", lambda e: e.dma_start(out=dst, in_=w_in_t[m], max_dma_last_dim=8192), writes=[tdst])

        for idx in range(3):
            wload(idx, mts[idx])
        nev = 0
        for idx, m in enumerate(mts):
            ws, tw = WS[idx % 3], t_ws[idx % 3]
            for bi in range(2):
                bank = (2 * idx + bi) % 4
                for kc in range(32):
                    mm(psb[bank], ws[:, kc, :], XG[bi][:, kc, :], kc == 0, kc == 31, [tw, t_xg[bi][kc // 4]], [tps[bank]])
                eb = nev % 2
                nev += 1
                P.add("dve", lambda e, bank=bank, bi=bi, eb=eb: e.tensor_tensor(
                    out=ev[eb], in0=psb[bank], in1=rb[bi], op=ALU.mult), reads=[tps[bank], t_rb[bi]], writes=[t_ev[eb]])
                if m < 13:
                    j = m - 5
                    P.add("act", lambda e, eb=eb: e.activation(out=sq2[eb], in_=ev[eb], func=AF.Square),
                          reads=[t_ev[eb]], writes=[t_sq2[eb]])
                    mm(psb[4 + bi], ones_bf, sq2[eb], j == 0, j == 7, [t_sq2[eb], t_const], [tps[4 + bi]])
                    P.add("dve", lambda e, eb=eb, j=j, bi=bi: e.tensor_scalar(
                        out=CQG[:, j, bi * 512:(bi + 1) * 512], in0=ev[eb], scalar1=g_q_t[:, j:j + 1], scalar2=None,
                        op0=ALU.mult), reads=[t_ev[eb], t_const], writes=[t_cqg[j][bi]])
                else:
                    hh = m - 13
                    P.add("act", lambda e, eb=eb, hh=hh, bi=bi: e.activation(
                        out=Y[:, 16 + hh, bi * 512:(bi + 1) * 512], in_=ev[eb], func=AF.Silu),
                        reads=[t_ev[eb]], writes=[t_y[16 + hh][bi]])
            if idx + 3 < len(mts):
                wload(idx + 3, mts[idx + 3])
            if m == 12:
                for bi in range(2):
                    rstd_from_psum(psb[4 + bi], tps[4 + bi], 1.0 / 1024, sqt2a, t_sqt, rqb[bi], t_rqb[bi])
                    P.add("dve", lambda e, bi=bi: e.tensor_tensor(out=csq[bi], in0=cs[bi], in1=rqb[bi], op=ALU.mult),
                          reads=[t_cs[bi], t_rqb[bi]], writes=[t_csq[bi]])
        P.barrier()

        o4 = R_XG
        KH = [v16(o4 + b * 8192, 4096) for b in range(2)]
        VH = v16(o4 + 16384, 8192).rearrange("p (t d) -> p t d", t=32)
        QN = [v16(o4 + 32768 + b * 2048, 1024).rearrange("p (b t) -> p b t", b=2) for b in range(2)]
        TQ = [v16(o4 + 36864 + b * 2048, 1024).rearrange("p (b t) -> p b t", b=2) for b in range(2)]
        PT = [v16(o4 + 40960 + b * 1024, 512) for b in range(4)]
        rc = [v32(o4 + 45056 + b * 2048, 512) for b in range(2)]
        WQ = [v16(o4 + 49152 + b * 4096, 2048).rearrange("p (k c) -> p k c", k=8) for b in range(2)]
        WKV = [v16(o4 + 57344 + b * 4096, 2048).rearrange("p (k c) -> p k c", k=4) for b in range(2)]
        t_kh = [[Tr() for _ in range(8)] for _ in range(2)]
        t_vh = [Tr() for _ in range(8)]
        t_qn = [[Tr(), Tr()] for _ in range(2)]
        t_tq = [[Tr(), Tr()] for _ in range(2)]
        t_pt = [Tr() for _ in range(4)]
        t_rc = [Tr(), Tr()]
        t_wq = [Tr(), Tr()]
        t_wkv = [Tr(), Tr()]

        def hload(hh):
            P.dma("pool", lambda e: e.dma_start(out=WQ[hh % 2], in_=w_uq_t[hh], max_dma_last_dim=8192), writes=[t_wq[hh % 2]])

        def pload(pr):
            P.dma("pool", lambda e: e.dma_start(out=WKV[pr % 2], in_=w_ukv_t[pr], max_dma_last_dim=8192), writes=[t_wkv[pr % 2]])

        hload(0)
        hload(1)
        pload(0)
        pload(1)
        stg = v32(188416 + 20480, 512)
        t_stg = Tr()
        XGA = v16(R_Y, 16384).rearrange("p (k t) -> p k t", k=32)
        t_xga = [Tr() for _ in range(32)]
        npt = 0
        nmisc = 0
        def pf_dma(kc):
            P.dma("sp", lambda e: e.dma_start(out=stg, in_=xT[3, :, kc, :]), writes=[t_stg])

        def pf_mul(kc):
            P.add("dve", lambda e: e.tensor_scalar(
                out=XGA[:, kc, :], in0=stg, scalar1=g_in_t[:, kc:kc + 1], scalar2=None, op0=ALU.mult),
                reads=[t_stg, t_const], writes=[t_xga[kc]])

        for hh in range(16):
            hb = hh % 2
            pr, hp = divmod(hh, 2)
            pf_dma(2 * hh)
            wq, wkv, twkv = WQ[hb], WKV[pr % 2], t_wkv[pr % 2]
            for bi in range(2):
                for part in range(2):
                    bank = nmisc % 4
                    nmisc += 1
                    for kc in range(8):
                        mm(psb[bank], wq[:, kc, part * 128:(part + 1) * 128], CQG[:, kc, bi * 512:(bi + 1) * 512],
                           kc == 0, kc == 7, [t_wq[hb], t_cqg[kc][bi]], [tps[bank]])
                    if part == 0:
                        P.add("dve", lambda e, bank=bank, hb=hb, bi=bi: e.tensor_tensor(
                            out=QN[hb][:, bi, :], in0=psb[bank], in1=rqb[bi], op=ALU.mult),
                            reads=[tps[bank], t_rqb[bi]], writes=[t_qn[hb][bi]])
                    else:
                        P.add("dve", lambda e, bank=bank, hb=hb, bi=bi: e.tensor_tensor(
                            out=TQ[hb][:, bi, :], in0=psb[bank], in1=csq[bi], op=ALU.mult),
                            reads=[tps[bank], t_csq[bi]], writes=[t_tq[hb][bi]])
            for s in range(8):
                bank = nmisc % 4
                nmisc += 1
                for kc in range(4):
                    mm(psb[bank], wkv[:, kc, hp * 128:(hp + 1) * 128], CKVN[:, kc, s * 512:(s + 1) * 512], kc == 0, kc == 3,
                       [twkv, t_ckvn[s]], [tps[bank]])
                P.add("act", lambda e, bank=bank, hb=hb, s=s: e.activation(
                    out=KH[hb][:, s * 512:(s + 1) * 512], in_=psb[bank], func=AF.Copy),
                    reads=[tps[bank]], writes=[t_kh[hb][s]])
            if hp == 0:
                for s_ in range(8):
                    for half in range(2):
                        bank = nmisc % 4
                        nmisc += 1
                        for t2 in range(2):
                            tt = 2 * half + t2
                            for kc in range(4):
                                mm(psb[bank][:, t2 * 256:(t2 + 1) * 256],
                                   CKVN[:, kc, s_ * 512 + tt * 128: s_ * 512 + (tt + 1) * 128], wkv[:, kc, 256:512],
                                   kc == 0, kc == 3, [twkv, t_ckvn[s_]], [tps[bank]])
                        P.add("dve", lambda e, bank=bank, s_=s_, half=half: e.tensor_copy(
                            out=VH[:, 4 * s_ + 2 * half:4 * s_ + 2 * half + 2, :],
                            in_=psb[bank].rearrange("p (t d) -> p t d", t=2)),
                            reads=[tps[bank]], writes=[t_vh[s_]])
            for bi in range(2):
                if bi == 0:
                    units = [(0, 0), (1, 1), (2, 2), (3, None)]
                else:
                    units = [(0, None), (1, None), (2, None), (3, None), (4, 3), (5, 4), (6, 5), (7, None)]
                diag_slot = 3 if bi == 0 else 7
                kts = []
                for (s, gc) in units:
                    for j in range(4):
                        kts.append((s, j, gc, s == diag_slot))
                po, pl = 4 + 2 * bi, 5 + 2 * bi
                n = len(kts)

                def QK(i):
                    s, j, gc, dg = kts[i]
                    c0 = 128 * j if dg else 0
                    bank = 2 + (i % 2)
                    kcol = (4 * s + j) * 128
                    mm(psb[bank][:, c0:512], KH[hb][:, kcol:kcol + 128], QN[hb][:, bi, c0:512], True, False,
                       [t_kh[hb][s], t_qn[hb][bi]], [tps[bank]])
                    mm(psb[bank][:, c0:512], K2[:, kcol:kcol + 128], TQ[hb][:, bi, c0:512], False, True,
                       [t_k2[s], t_tq[hb][bi]], [tps[bank]])

                QK(0)
                for i in range(n):
                    if i + 1 < n:
                        QK(i + 1)
                    s, j, gc, dg = kts[i]
                    c0 = 128 * j if dg else 0
                    bank = 2 + (i % 2)
                    pb = npt % 4
                    npt += 1
                    if gc is None:
                        P.add("act", lambda e, bank=bank, pb=pb, c0=c0: e.activation(
                            out=PT[pb][:, c0:512], in_=psb[bank][:, c0:512], func=AF.Exp, scale=ATTN_SCALE),
                            reads=[tps[bank]], writes=[t_pt[pb]])
                    else:
                        P.add("act", lambda e, bank=bank, pb=pb, gc=gc: e.activation(
                            out=PT[pb], in_=psb[bank], func=AF.Exp, scale=ATTN_SCALE, bias=gate[:, gc:gc + 1]),
                            reads=[tps[bank], t_const], writes=[t_pt[pb]])
                    if dg:
                        P.add("dve", lambda e, pb=pb, c0=c0: e.memset(PT[pb][64:128, c0:c0 + 64], 0.0),
                              reads=[t_pt[pb]], writes=[t_pt[pb]])
                    mm(psb[po][:, c0:512], VH[:, 4 * s + j, hp * 128:(hp + 1) * 128], PT[pb][:, c0:512], i == 0, i == n - 1,
                       [t_vh[s], t_pt[pb]], [tps[po]])
                    mm(psb[pl][:, c0:512], ones_bf, PT[pb][:, c0:512], i == 0, i == n - 1,
                       [t_const, t_pt[pb]], [tps[pl]])
                P.add("dve", lambda e, pl=pl, bi=bi: e.reciprocal(out=rc[bi], in_=psb[pl]), reads=[tps[pl]], writes=[t_rc[bi]])
                P.add("dve", lambda e, po=po, bi=bi: e.tensor_tensor(out=rc[bi], in0=psb[po], in1=rc[bi], op=ALU.mult),
                      reads=[tps[po], t_rc[bi]], writes=[t_rc[bi]])
                P.add("dve", lambda e, bi=bi, hh=hh: e.tensor_tensor(
                    out=Y[:, 16 + hh, bi * 512:(bi + 1) * 512], in0=rc[bi], in1=Y[:, 16 + hh, bi * 512:(bi + 1) * 512],
                    op=ALU.mult), reads=[t_rc[bi], t_y[16 + hh][bi]], writes=[t_y[16 + hh][bi]])
                if bi == 0:
                    pf_mul(2 * hh)
                    pf_dma(2 * hh + 1)
            if hh + 2 < 16:
                hload(hh + 2)
            if hp == 1 and pr + 2 < 8:
                pload(pr + 2)
            pf_mul(2 * hh + 1)
        P.barrier()

        WS, t_ws = wstream(R_CKVN)
        o3 = R_CKVN + 24576
        T1 = [v32(o3 + b * 8320, 512) for b in range(2)]
        UU = [v32(o3 + b * 8320 + 2048, 516) for b in range(2)]
        CC = [v32(o3 + b * 8320 + 4112, 512) for b in range(2)]
        GG = [v32(o3 + b * 8320 + 6160, 512) for b in range(2)]
        t_t1 = [Tr(), Tr()]
        t_uu = [Tr(), Tr()]
        t_cc = [Tr(), Tr()]
        t_gg = [Tr(), Tr()]
        o3b = o3 + 2 * 8320
        rb2 = [v32(o3b + b * 2048, 512) for b in range(2)]
        t_rb2 = [Tr(), Tr()]
        o3c = o3b + 4096
        xst1 = [v32(R_XG + b * 4096, 1024).rearrange("p (k t) -> p k t", k=2) for b in range(8)]
        t_xst1 = [Tr() for _ in range(8)]
        o3d = o3c + 8192
        xhs = v32(o3d, 128).rearrange("p (k t) -> p k t", k=32)
        xgh = v16(o3d + 512, 128).rearrange("p (k t) -> p k t", k=32)
        xhq = v16(o3d + 768, 128).rearrange("p (k t) -> p k t", k=32)
        th = v32(o3d + 1024, 8)
        assert o3d + 1024 + 32 <= R_MISC
        t_h = Tr()
        mts = list(range(29, 93))
        for idx in range(3):
            wload(idx, mts[idx])
        Yc = v16(R_XG, 16384).rearrange("p (f t) -> p f t", f=16)
        nr = 0
        for bi, s in ((1, 7),):
            for kg in range(16):
                stb = nr % 8
                nr += 1
                P.dma("sp", lambda e, s=s, kg=kg, stb=stb: e.dma_start(out=xst1[stb], in_=xT[s, :, 2 * kg:2 * kg + 2, :]),
                      writes=[t_xst1[stb]])
                for j in range(2):
                    kc = 2 * kg + j
                    P.add("dve", lambda e, bi=bi, kc=kc, j=j, stb=stb: e.tensor_scalar(
                        out=XG[bi][:, kc, :], in0=xst1[stb][:, j, :], scalar1=g_in_t[:, kc:kc + 1], scalar2=None,
                        op0=ALU.mult), reads=[t_xst1[stb], t_const], writes=[t_xg[bi][kc // 4]])
        for bi in range(2):
            P.add("dve", lambda e, bi=bi: e.tensor_tensor(out=rb2[bi], in0=rb[bi], in1=rb[bi], op=ALU.mult),
                  reads=[t_rb[bi]], writes=[t_rb2[bi]])
        P.dma("sp", lambda e: e.dma_start(out=xhs, in_=xh), writes=[t_h])
        P.add("dve", lambda e: e.tensor_tensor(out=xgh, in0=xhs, in1=g_in_t.unsqueeze(2).to_broadcast([128, 32, 4]),
                                               op=ALU.mult), reads=[t_h, t_const], writes=[t_h])
        P.add("act", lambda e: e.activation(out=xhq, in_=xhs, func=AF.Square), reads=[t_h], writes=[t_h])
        for kc in range(32):
            mm(psb[7][:, 0:4], ones_bf, xhq[:, kc, :], kc == 0, kc == 31, [t_h, t_const], [tps[7]])
        rstd_from_psum(psb[7], tps[7], 1.0 / D, sqt, t_sqt, rbh, t_small, n=4)
        P.add("dve", lambda e: e.tensor_tensor(out=rbh2, in0=rbh, in1=rbh, op=ALU.mult), reads=[t_small], writes=[t_small])
        P.barrier()

        for idx, m in enumerate(mts):
            f, which = divmod(idx, 4)
            ws, tw = WS[idx % 3], t_ws[idx % 3]
            banks = [(2 * idx) % 6, (2 * idx + 1) % 6]
            hc = 4 * which
            for kc in range(32):
                mm(psb[banks[0]], ws[:, kc, :], XGA[:, kc, :], kc == 0, kc == 31, [tw, t_xga[kc]], [tps[banks[0]]])
                mm(psb[banks[1]], ws[:, kc, :], XG[1][:, kc, :], kc == 0, kc == 31, [tw, t_xg[1][kc // 4]], [tps[banks[1]]])
                if which < 2:
                    mm(psb[6 + (f % 2)][:, hc:hc + 4], ws[:, kc, :], xgh[:, kc, :], kc == 0, kc == 31, [tw, t_h],
                       [tps[6 + (f % 2)]])
            if idx + 3 < len(mts):
                wload(idx + 3, mts[idx + 3])
            hb_ = 6 + (f % 2)
            for bi in range(2):
                bank = banks[bi]
                tb = bi
                if which == 0:
                    P.add("act", lambda e, bank=bank, tb=tb: e.activation(out=T1[tb], in_=psb[bank], func=AF.Copy),
                          reads=[tps[bank]], writes=[t_t1[tb]])
                elif which == 1:
                    P.add("dve", lambda e, bank=bank, tb=tb: e.tensor_tensor(out=T1[tb], in0=psb[bank], in1=T1[tb], op=ALU.mult),
                          reads=[tps[bank], t_t1[tb]], writes=[t_t1[tb]])
                    P.add("dve", lambda e, tb=tb, bi=bi: e.tensor_tensor(out=UU[tb][:, 2:514], in0=T1[tb], in1=rb2[bi], op=ALU.mult),
                          reads=[t_t1[tb], t_rb2[bi]], writes=[t_uu[tb]])
                    if bi == 0:
                        P.add("act", lambda e, hb_=hb_: e.activation(out=th[:, 0:4], in_=psb[hb_][:, 0:4], func=AF.Copy),
                              reads=[tps[hb_]], writes=[t_small])
                        P.add("dve", lambda e, hb_=hb_: e.tensor_tensor(out=th[:, 0:4], in0=psb[hb_][:, 4:8], in1=th[:, 0:4], op=ALU.mult),
                              reads=[tps[hb_], t_small], writes=[t_small])
                        P.add("dve", lambda e: e.tensor_tensor(out=uh, in0=th[:, 0:4], in1=rbh2, op=ALU.mult),
                              reads=[t_small], writes=[t_small])
                    P.add("dve", lambda e, tb=tb, bi=bi: e.tensor_copy(out=UU[tb][:, 0:2], in_=uh[:, 2 * bi:2 * bi + 2]),
                          reads=[t_small, t_uu[tb]], writes=[t_uu[tb]])
                    cw = 3 * f
                    P.add("act", lambda e, tb=tb, cw=cw: e.activation(out=CC[tb], in_=UU[tb][:, 0:512], func=AF.Copy,
                                                                       scale=convw[:, cw:cw + 1]),
                          reads=[t_uu[tb], t_const], writes=[t_cc[tb]])
                    P.add("dve", lambda e, tb=tb, cw=cw: e.scalar_tensor_tensor(
                        out=CC[tb], in0=UU[tb][:, 1:513], scalar=convw[:, cw + 1:cw + 2], in1=CC[tb], op0=ALU.mult, op1=ALU.add),
                        reads=[t_uu[tb], t_cc[tb], t_const], writes=[t_cc[tb]])
                    P.add("dve", lambda e, tb=tb, cw=cw: e.scalar_tensor_tensor(
                        out=CC[tb], in0=UU[tb][:, 2:514], scalar=convw[:, cw + 2:cw + 3], in1=CC[tb], op0=ALU.mult, op1=ALU.add),
                        reads=[t_uu[tb], t_cc[tb], t_const], writes=[t_cc[tb]])
                elif which == 2:
                    P.add("dve", lambda e, bank=bank, tb=tb, bi=bi: e.tensor_tensor(out=GG[tb], in0=psb[bank], in1=rb[bi], op=ALU.mult),
                          reads=[tps[bank], t_rb[bi]], writes=[t_gg[tb]])
                    P.add("dve", lambda e, tb=tb: e.tensor_tensor(out=GG[tb], in0=GG[tb], in1=CC[tb], op=ALU.mult),
                          reads=[t_gg[tb], t_cc[tb]], writes=[t_gg[tb]])
                else:
                    P.add("dve", lambda e, bank=bank, tb=tb, bi=bi: e.tensor_tensor(out=T1[tb], in0=psb[bank], in1=rb[bi], op=ALU.mult),
                          reads=[tps[bank], t_rb[bi]], writes=[t_t1[tb]])
                    P.add("act", lambda e, tb=tb: e.activation(out=T1[tb], in_=T1[tb], func=AF.Silu),
                          reads=[t_t1[tb]], writes=[t_t1[tb]])
                    P.add("dve", lambda e, tb=tb, f=f, bi=bi: e.tensor_tensor(
                        out=Yc[:, f, bi * 512:(bi + 1) * 512], in0=GG[tb], in1=T1[tb], op=ALU.mult),
                        reads=[t_gg[tb], t_t1[tb]], writes=[t_y[f][bi]])
        P.barrier()

        Ht = [v32(R_Y + tt * 16384, 4096) for tt in range(2)] + [v32(R_XG + 32768 + tt * 16384, 4096) for tt in range(2)]
        t_hh = [Tr() for _ in range(4)]
        WO = [v16(R_CKVN + b * 4096, 2048).rearrange("p (k c) -> p k c", k=4) for b in range(4)]
        t_wo = [Tr() for _ in range(4)]
        XR = [v32(R_CKVN + 16384 + b * 8192, 2048).rearrange("p (t c) -> p t c", t=4) for b in range(2)]
        t_xr = [Tr(), Tr()]
        GF = v32(R_CKVN + 32768, 4096)
        t_gf = Tr()
        JUNK = v16(R_CKVN + 49152, 512)
        t_junk = Tr()
        assert R_CKVN + 49152 + 1024 <= R_MISC
        t_out = [Tr("out%d" % i) for i in range(4)]
        t_ssq = [[Tr() for _ in range(8)] for _ in range(4)]
        P.dma("sp", lambda e: e.dma_start(out=GF, in_=gfin_d), writes=[t_gf])
        wlist = [(grp, ct, kg) for grp in range(2) for ct in range(8) for kg in range(8)]

        def woload(i):
            grp, ct, kg = wlist[i]
            P.dma("pool", lambda e: e.dma_start(out=WO[i % 4], in_=w_out_t[ct, kg], max_dma_last_dim=8192), writes=[t_wo[i % 4]])

        for i in range(4):
            woload(i)
        wi = 0
        for grp in range(2):
            for ct in range(8):
                xb = ct % 2
                P.dma("sp", lambda e, grp=grp, ct=ct, xb=xb: e.dma_start(
                    out=XR[xb], in_=xres[grp * 512:(grp + 1) * 512, ct * 512:(ct + 1) * 512].rearrange("(t p) c -> p t c", p=128)),
                    writes=[t_xr[xb]])
                pbase = 4 * (ct % 2)
                for kg in range(8):
                    wo, two = WO[wi % 4], t_wo[wi % 4]
                    for tt in range(4):
                        for kcc in range(4):
                            kc = 4 * kg + kcc
                            tok0 = grp * 512 + tt * 128
                            ysrc = Yc[:, kc, tok0:tok0 + 128] if kc < 16 else Y[:, kc, tok0:tok0 + 128]
                            mm(psb[pbase + tt], ysrc, wo[:, kcc, :], kc == 0, kc == 31,
                               [two, t_y[kc][grp]], [tps[pbase + tt]])
                    if wi + 4 < len(wlist):
                        woload(wi + 4)
                    wi += 1
                for tt in range(4):
                    P.add("dve", lambda e, pbase=pbase, tt=tt, ct=ct, xb=xb: e.tensor_tensor(
                        out=Ht[tt][:, ct * 512:(ct + 1) * 512], in0=psb[pbase + tt], in1=XR[xb][:, tt, :], op=ALU.add),
                        reads=[tps[pbase + tt], t_xr[xb]], writes=[t_hh[tt]])
                    P.add("act", lambda e, tt=tt, ct=ct: e.activation(
                        out=JUNK, in_=Ht[tt][:, ct * 512:(ct + 1) * 512], func=AF.Square,
                        accum_out=ssq[:, tt * 8 + ct: tt * 8 + ct + 1]),
                        reads=[t_hh[tt]], writes=[t_ssq[tt][ct], t_junk])
            P.add("dve", lambda e: e.reduce_sum(out=tot, in_=ssq.rearrange("p (t c) -> p t c", t=4), axis=AX.X),
                  reads=[t_small] + [t for row in t_ssq for t in row], writes=[t_small])
            P.add("act", lambda e: e.activation(out=tot, in_=tot, func=AF.Sqrt, scale=1.0 / D, bias=eps_ap),
                  reads=[t_small, t_const], writes=[t_small])
            P.add("dve", lambda e: e.reciprocal(out=rfin, in_=tot), reads=[t_small], writes=[t_small])
            for tt in range(4):
                for ct in range(8):
                    P.add("dve", lambda e, tt=tt, ct=ct: e.scalar_tensor_tensor(
                        out=Ht[tt][:, ct * 512:(ct + 1) * 512], in0=Ht[tt][:, ct * 512:(ct + 1) * 512],
                        scalar=rfin[:, tt:tt + 1], in1=GF[:, ct * 512:(ct + 1) * 512], op0=ALU.mult, op1=ALU.mult),
                        reads=[t_hh[tt], t_small, t_gf], writes=[t_hh[tt]])
                r0 = grp * 512 + tt * 128
                P.dma("sp", lambda e, tt=tt, r0=r0: e.dma_start(out=out[r0:r0 + 128, :], in_=Ht[tt]),
                      reads=[t_hh[tt]], sem_tr=t_out[tt])
        P.barrier()
        P.emit(nc, stack)
    return nc


def _tile_w(w, cols, kc):
    sub = w[:, cols]
    return np.ascontiguousarray(sub.reshape(kc, 128, len(cols)).transpose(1, 0, 2))


def _prep_weights(g_in, w_in, conv_w, q_norm_g, w_uq, kv_norm_g, w_ukv, w_out, g_final):
    w_in = w_in[0]
    ar = np.arange
    perm = np.concatenate([ar(0, 32), ar(32, 64), ar(32, 64), ar(0, 32)])
    col_tiles = []
    for m in range(4):
        col_tiles.append(9216 + m * 128 + ar(128))
    col_tiles.append(9728 + perm)
    for m in range(8):
        col_tiles.append(8192 + m * 128 + ar(128))
    for m in range(16):
        col_tiles.append(9792 + m * 128 + ar(128))
    for f in range(16):
        for base in (2048, 4096, 0, 6144):
            col_tiles.append(base + f * 128 + ar(128))
    assert len(col_tiles) == NMT
    w_in_t = np.empty((NMT, 128, 32, 128), np.float32)
    for m, cols in enumerate(col_tiles):
        w_in_t[m] = _tile_w(w_in, cols, 32)
    w_uq_t = np.empty((16, 128, 8, 256), np.float32)
    for h in range(16):
        cols = np.concatenate([h * 192 + ar(128), h * 192 + 128 + perm])
        w_uq_t[h] = _tile_w(w_uq[0], cols, 8)
    w_ukv_t = np.empty((8, 128, 4, 512), np.float32)
    for pr in range(8):
        h0, h1 = 2 * pr, 2 * pr + 1
        cols = np.concatenate([h0 * 256 + ar(128), h1 * 256 + ar(128), h0 * 256 + 128 + ar(128), h1 * 256 + 128 + ar(128)])
        w_ukv_t[pr] = _tile_w(w_ukv[0], cols, 4)
    w_out_t = np.ascontiguousarray(w_out[0].reshape(8, 4, 128, 8, 512).transpose(3, 0, 2, 1, 4))
    dmat = np.zeros((128, 128), np.float32)
    i64 = ar(64)
    for a in (0, 64):
        for b in (0, 64):
            dmat[a + i64, b + i64] = 1.0
    return {
        "w_in_t": w_in_t, "w_uq_t": w_uq_t, "w_ukv_t": w_ukv_t, "w_out_t": w_out_t,
        "g_in_t": np.ascontiguousarray(g_in[0].reshape(32, 128).T),
        "g_q_t": np.ascontiguousarray(q_norm_g[0].reshape(8, 128).T),
        "g_kv_t": np.ascontiguousarray(kv_norm_g[0].reshape(4, 128).T),
        "convw_t": np.ascontiguousarray(conv_w[0].reshape(3, 16, 128).transpose(2, 1, 0).reshape(128, 48)),
        "gfin_b": np.ascontiguousarray(np.broadcast_to(g_final[None, :], (128, D))),
        "dmat": dmat,
    }


def _rope_table():
    pos = np.arange(S, dtype=np.float32)
    inv_freq = (1.0 / (np.float32(10000.0) ** (np.arange(0, 64, 2, dtype=np.float32) / np.float32(64)))).astype(np.float32)
    ang = (pos[:, None] * inv_freq[None, :]).astype(np.float32)
    c = np.cos(ang).astype(np.float32).T
    s = np.sin(ang).astype(np.float32).T
    return np.concatenate([c, c, -s, s], axis=0)


def _slots(k):
    low = [i for i in range(4) if i != k]
    high = [i for i in range(4, 8) if i != 7 - k]
    return low + [k] + high + [7 - k]


def kernel(x, g_in, w_in, conv_w, q_norm_g, w_uq, kv_norm_g, w_ukv, w_out, g_final):
    x = np.asarray(x, np.float32)
    wts = _prep_weights(*(np.asarray(a, np.float32) for a in
                          (g_in, w_in, conv_w, q_norm_g, w_uq, kv_norm_g, w_ukv, w_out, g_final)))
    cs_full = _rope_table()
    in_maps = []
    for c in range(NCORES):
        b, k = divmod(c, 4)
        sl = _slots(k)
        xb = x[b]
        xT = np.empty((8, 128, 32, 512), np.float32)
        cs_tab = np.empty((8, 128, 512), np.float32)
        for si, blk in enumerate(sl):
            xT[si] = xb[blk * BLK:(blk + 1) * BLK, :].reshape(512, 32, 128).transpose(2, 1, 0)
            cs_tab[si] = cs_full[:, blk * BLK:(blk + 1) * BLK]
        A, B = k, 7 - k
        xres = np.concatenate([xb[A * BLK:(A + 1) * BLK], xb[B * BLK:(B + 1) * BLK]], axis=0)
        halo = np.zeros((4, D), np.float32)
        if A > 0:
            halo[0:2] = xb[A * BLK - 2:A * BLK]
        halo[2:4] = xb[B * BLK - 2:B * BLK]
        xh = np.ascontiguousarray(halo.reshape(4, 32, 128).transpose(2, 1, 0))
        gate = np.zeros((128, 6), np.float32)
        for j in range(3):
            gate[:, j] = 0.0 if sl[j] < A else NEG
            gate[:, 3 + j] = 0.0 if sl[4 + j] < B else NEG
        m = {"xT": xT, "xres": np.ascontiguousarray(xres), "xh": xh, "cs_tab": cs_tab, "gate": gate}
        m.update(wts)
        in_maps.append(m)
    nc = build_program()
    res = run_bass_kernel_spmd(nc, in_maps, core_ids=list(range(NCORES)))
    outp = np.empty((2, S, D), np.float32)
    for c in range(NCORES):
        b, k = divmod(c, 4)
        o = res.results[c]["out"]
        outp[b, k * BLK:(k + 1) * BLK] = o[0:512]
        outp[b, (7 - k) * BLK:(8 - k) * BLK] = o[512:1024]
    return outp
```

```python
import contextlib
import numpy as np
import concourse.bass as bass
import concourse.mybir as mybir
from concourse.bass_utils import run_bass_kernel_spmd

F32 = mybir.dt.float32
BF16 = mybir.dt.bfloat16
AF = mybir.ActivationFunctionType
ALU = mybir.AluOpType
AX = mybir.AxisListType

NCORES = 8
D = 4096
S = 4096
BLK = 512
NMT = 93
EPS = 1e-6
ATTN_SCALE = 192 ** -0.5
ORDER = [0, 1, 2, 4, 5, 6, 3, 7]
NEG = -30000.0


class Tr:
    __slots__ = ("lastw", "readers", "sem", "cnt", "name")

    def __init__(self, name=""):
        self.lastw = None
        self.readers = []
        self.sem = None
        self.cnt = 0
        self.name = name


class Ins:
    __slots__ = ("eng", "fn", "deps", "signal", "ordinal", "dma_tr", "dma_val")

    def __init__(self, eng, fn, deps):
        self.eng = eng
        self.fn = fn
        self.deps = deps
        self.signal = False
        self.ordinal = 0
        self.dma_tr = None
        self.dma_val = 0


ENGS = ("pe", "act", "dve", "pool", "sp")


class Prog:
    def __init__(self):
        self.ins = {e: [] for e in ENGS}
        self.last = {e: None for e in ENGS}
        self.dma_trs = []
        self.sync_same_engine = True

    def _deps(self, eng, reads, writes):
        deps = []
        for t in reads:
            if t.lastw is not None:
                deps.append(t.lastw)
        for t in writes:
            if t.lastw is not None:
                deps.append(t.lastw)
            deps.extend(t.readers)
        out = []
        for d in deps:
            if d[0] == "c":
                i = d[1]
                if i.eng == "pe" and eng == "pe":
                    continue
                if i.eng == eng and not self.sync_same_engine:
                    continue
                i.signal = True
            out.append(d)
        return out

    def add(self, eng, fn, reads=(), writes=()):
        ins = Ins(eng, fn, self._deps(eng, reads, writes))
        ev = ("c", ins)
        for t in reads:
            t.readers.append(ev)
        for t in writes:
            t.lastw = ev
            t.readers = []
        self.ins[eng].append(ins)
        self.last[eng] = ins
        return ins

    def dma(self, eng, fn, reads=(), writes=(), sem_tr=None):
        ins = Ins(eng, fn, self._deps(eng, reads, writes))
        if sem_tr is None:
            sem_tr = writes[0]
        if sem_tr not in self.dma_trs:
            self.dma_trs.append(sem_tr)
        sem_tr.cnt += 16
        ins.dma_tr = sem_tr
        ins.dma_val = sem_tr.cnt
        ev = ("d", sem_tr, sem_tr.cnt)
        for t in reads:
            t.readers.append(ev)
        for t in writes:
            t.lastw = ev
            t.readers = []
        self.ins[eng].append(ins)
        return ins

    def barrier(self):
        evs = []
        for e in ("pe", "act", "dve", "pool"):
            if self.last[e] is not None:
                self.last[e].signal = True
                evs.append(("c", self.last[e]))
        for t in self.dma_trs:
            evs.append(("d", t, t.cnt))
        for e in ENGS:
            self.ins[e].append(Ins(e, None, list(evs)))

    def emit(self, nc, stack):
        esem = {e: stack.enter_context(nc.semaphore("sem_" + e)) for e in ("pe", "act", "dve", "pool")}
        for i, t in enumerate(self.dma_trs):
            t.sem = stack.enter_context(nc.semaphore("dsem%d" % i))
        for e in ("pe", "act", "dve", "pool"):
            n = 0
            for i in self.ins[e]:
                if i.signal:
                    n += 1
                    i.ordinal = n
        block = stack.enter_context(nc.Block())

        def run(ename, eng):
            waited = {}
            for i in self.ins[ename]:
                for d in i.deps:
                    if d[0] == "c":
                        sem, val, key = esem[d[1].eng], d[1].ordinal, d[1].eng
                    else:
                        sem, val, key = d[1].sem, d[2], id(d[1])
                    if waited.get(key, 0) >= val:
                        continue
                    waited[key] = val
                    eng.wait_ge(sem, val)
                if i.fn is None:
                    continue
                r = i.fn(eng)
                if i.dma_tr is not None:
                    r.then_inc(i.dma_tr.sem, 16)
                elif i.signal:
                    r.then_inc(esem[ename], 1)

        @block.tensor
        def _(e):
            run("pe", e)

        @block.scalar
        def _(e):
            run("act", e)

        @block.vector
        def _(e):
            run("dve", e)

        @block.gpsimd
        def _(e):
            run("pool", e)

        @block.sync
        def _(e):
            run("sp", e)


R_XG = 0
R_Y = 65536
R_CKVN = 131072
R_K2 = 163840
R_CQG = 172032
R_MISC = 188416
ARENA = 188416 + 20480 + 2048


def build_program(dbg=False):
    nc = bass.Bass("TRN2", target_bir_lowering=False)
    P = Prog()

    def din(name, shape):
        return nc.dram_tensor(name, list(shape), F32, kind="ExternalInput").ap()

    xT = din("xT", [8, 128, 32, 512])
    xres = din("xres", [1024, D])
    xh = din("xh", [128, 32, 4])
    cs_tab = din("cs_tab", [8, 128, 512])
    gate_d = din("gate", [128, 6])
    w_in_t = din("w_in_t", [NMT, 128, 32, 128])
    w_uq_t = din("w_uq_t", [16, 128, 8, 256])
    w_ukv_t = din("w_ukv_t", [8, 128, 4, 512])
    w_out_t = din("w_out_t", [8, 8, 128, 4, 512])
    g_in_d = din("g_in_t", [128, 32])
    g_q_d = din("g_q_t", [128, 8])
    g_kv_d = din("g_kv_t", [128, 4])
    convw_d = din("convw_t", [128, 48])
    gfin_d = din("gfin_b", [128, D])
    dmat_d = din("dmat", [128, 128])
    out = nc.dram_tensor("out", [1024, D], F32, kind="ExternalOutput").ap()

    stack = contextlib.ExitStack()
    with stack:
        arena = stack.enter_context(nc.sbuf_tensor("arena", [128, ARENA // 4], F32))
        psb_t = [stack.enter_context(nc.psum_tensor("psb%d" % i, [128, 512], F32)) for i in range(8)]
        psb = [t[:, :] for t in psb_t]
        tps = [Tr("ps%d" % i) for i in range(8)]

        def v32(off, n):
            return arena[:, off // 4: off // 4 + n]

        def v16(off, n):
            return arena[:, off // 4: off // 4 + n // 2].bitcast(BF16)

        mo = [R_MISC]

        def misc32(n):
            a = v32(mo[0], n)
            mo[0] += 4 * n
            return a

        def misc16(n):
            a = v16(mo[0], n)
            mo[0] += 2 * n
            return a

        g_in_t = misc32(32)
        g_q_t = misc32(8)
        g_kv_t = misc32(4)
        convw = misc32(48)
        gate = misc32(6)
        ssq = misc32(32)
        tot = misc32(4)
        rfin = misc32(4)
        rbh = misc32(4)
        rbh2 = misc32(4)
        uh = misc32(4)
        dmat = misc32(128)
        ones_bf = misc16(128)
        rb = [misc32(512) for _ in range(2)]
        cs = [misc32(512) for _ in range(2)]
        rqb = [misc32(512) for _ in range(2)]
        csq = [misc32(512) for _ in range(2)]
        rkvb = misc32(512)
        assert mo[0] <= ARENA, mo[0]
        t_const = Tr("const")
        t_rb = [Tr(), Tr()]
        t_cs = [Tr(), Tr()]
        t_rqb = [Tr(), Tr()]
        t_csq = [Tr(), Tr()]
        t_rkvb = Tr()
        t_small = Tr("small")

        XG = [v16(R_XG + b * 32768, 16384).rearrange("p (k t) -> p k t", k=32) for b in range(2)]
        t_xg = [[Tr() for _ in range(8)] for _ in range(2)]
        Y = v16(R_Y, 32768).rearrange("p (f t) -> p f t", f=32)
        t_y = [[Tr() for _ in range(2)] for _ in range(32)]
        CKVN = v16(R_CKVN, 16384).rearrange("p (k t) -> p k t", k=4)
        t_ckvn = [Tr() for _ in range(8)]
        K2 = v16(R_K2, 4096)
        t_k2 = [Tr() for _ in range(8)]
        CQG = v16(R_CQG, 8192).rearrange("p (k t) -> p k t", k=8)
        t_cqg = [[Tr() for _ in range(2)] for _ in range(8)]

        for dst, src in ((g_in_t, g_in_d), (g_q_t, g_q_d), (g_kv_t, g_kv_d), (convw, convw_d),
                         (gate, gate_d), (dmat, dmat_d)):
            P.dma("sp", lambda e, dst=dst, src=src: e.dma_start(out=dst, in_=src), writes=[Tr()])
        P.add("dve", lambda e: e.memset(ones_bf, 1.0), writes=[Tr()])

        def mm(out_, lhsT, rhs, start, stop, reads, writes):
            P.add("pe", lambda e: e.matmul(out_, lhsT, rhs, start=start, stop=stop), reads=reads, writes=writes)

        def rstd_from_psum(bank, tbank, scale, tmp, ttmp, dst, tdst, n=512):
            P.add("act", lambda e: e.activation(out=tmp[:, 0:n], in_=bank[:, 0:n], func=AF.Sqrt, scale=scale, bias=eps_ap),
                  reads=[tbank, t_const], writes=[ttmp])
            P.add("dve", lambda e: e.reciprocal(out=dst[:, 0:n], in_=tmp[:, 0:n]), reads=[ttmp], writes=[tdst])

        eps_ap = misc32(1) if False else None
        eps_ap = v32(mo[0], 1)
        mo[0] += 4
        P.add("dve", lambda e: e.memset(eps_ap, EPS), writes=[Tr()])
        P.barrier()

        WCKV = v16(R_Y, 32 * 640).rearrange("p (k c) -> p k c", k=32)
        t_wckv = Tr()
        o1 = R_Y + 40960
        xst = [v32(o1 + b * 4096, 1024).rearrange("p (k t) -> p k t", k=2) for b in range(4)]
        t_xst = [Tr() for _ in range(4)]
        ckvf = v32(o1 + 16384, 2048).rearrange("p (k t) -> p k t", k=4)
        t_ckvf = [Tr() for _ in range(4)]
        xsq = [v16(R_CQG + b * 2048, 1024).rearrange("p (k t) -> p k t", k=2) for b in range(4)]
        t_xsq = [Tr() for _ in range(4)]
        tk = v32(R_CQG + 8192, 512)
        t_tk = Tr()
        ckvsq = v16(R_CQG + 10240, 2048).rearrange("p (k t) -> p k t", k=4)
        t_ckvsq = Tr()
        sqt = v32(R_CQG + 14336, 512)
        t_sqt = Tr()
        t_xgk = [[Tr() for _ in range(32)] for _ in range(2)]
        for b_ in range(2):
            for g_ in range(8):
                t_xg[b_][g_] = None

        for m in range(5):
            P.dma("pool", lambda e, m=m: e.dma_start(out=WCKV[:, :, m * 128:(m + 1) * 128], in_=w_in_t[m],
                                                       max_dma_last_dim=8192), writes=[t_wckv])
        nchunk = [0]

        def load_slot(i):
            s = ORDER[i]
            xb = i % 2
            for kg in range(16):
                stb = nchunk[0] % 4
                nchunk[0] += 1
                P.dma("sp", lambda e, s=s, kg=kg, stb=stb: e.dma_start(out=xst[stb], in_=xT[s, :, 2 * kg:2 * kg + 2, :]),
                      writes=[t_xst[stb]])
                for j in range(2):
                    kc = 2 * kg + j
                    P.add("dve", lambda e, xb=xb, kc=kc, stb=stb, j=j: e.tensor_scalar(
                        out=XG[xb][:, kc, :], in0=xst[stb][:, j, :], scalar1=g_in_t[:, kc:kc + 1], scalar2=None,
                        op0=ALU.mult), reads=[t_xst[stb], t_const], writes=[t_xgk[xb][kc]])
                P.add("act", lambda e, stb=stb: e.activation(out=xsq[stb], in_=xst[stb], func=AF.Square),
                      reads=[t_xst[stb]], writes=[t_xsq[stb]])
                for j in range(2):
                    mm(psb[0], ones_bf, xsq[stb][:, j, :], kg == 0 and j == 0, kg == 15 and j == 1,
                       [t_xsq[stb], t_const], [tps[0]])
                yield
            P.add("act", lambda e, xb=xb: e.activation(out=rb[xb], in_=psb[0], func=AF.Sqrt, scale=1.0 / D, bias=eps_ap),
                  reads=[tps[0], t_const], writes=[t_rb[xb]])
            P.add("dve", lambda e, xb=xb: e.reciprocal(out=rb[xb], in_=rb[xb]), reads=[t_rb[xb]], writes=[t_rb[xb]])
            P.dma("sp", lambda e, s=s, xb=xb: e.dma_start(out=cs[xb], in_=cs_tab[s]), writes=[t_cs[xb]])
            yield

        def compute_slot(i):
            s = ORDER[i]
            xb = i % 2
            for m in range(5):
                bank = (1, 2, 5, 6)[m % 4]
                for kc in range(32):
                    mm(psb[bank], WCKV[:, kc, m * 128:(m + 1) * 128], XG[xb][:, kc, :], kc == 0, kc == 31,
                       [t_wckv, t_xgk[xb][kc]], [tps[bank]])
                    if kc % 16 == 15:
                        yield
                if m < 4:
                    P.add("dve", lambda e, bank=bank, m=m, xb=xb: e.tensor_tensor(
                        out=ckvf[:, m, :], in0=psb[bank], in1=rb[xb], op=ALU.mult),
                        reads=[tps[bank], t_rb[xb]], writes=[t_ckvf[m]])
                else:
                    P.add("dve", lambda e, bank=bank, xb=xb: e.tensor_tensor(
                        out=tk, in0=psb[bank], in1=rb[xb], op=ALU.mult), reads=[tps[bank], t_rb[xb]], writes=[t_tk])
                    P.add("dve", lambda e, xb=xb: e.tensor_tensor(out=tk, in0=tk, in1=cs[xb], op=ALU.mult),
                          reads=[t_tk, t_cs[xb]], writes=[t_tk])
            P.add("act", lambda e: e.activation(out=ckvsq, in_=ckvf, func=AF.Square), reads=t_ckvf, writes=[t_ckvsq])
            for j in range(4):
                mm(psb[3], ones_bf, ckvsq[:, j, :], j == 0, j == 3, [t_ckvsq, t_const], [tps[3]])
            P.add("act", lambda e: e.activation(out=rkvb, in_=psb[3], func=AF.Sqrt, scale=1.0 / 512, bias=eps_ap),
                  reads=[tps[3], t_const], writes=[t_rkvb])
            P.add("dve", lambda e: e.reciprocal(out=rkvb, in_=rkvb), reads=[t_rkvb], writes=[t_rkvb])
            mm(psb[4], dmat, tk, True, True, [t_tk, t_const], [tps[4]])
            yield
            for m in range(4):
                P.add("dve", lambda e, m=m, s=s: e.scalar_tensor_tensor(
                    out=CKVN[:, m, s * 512:(s + 1) * 512], in0=ckvf[:, m, :], scalar=g_kv_t[:, m:m + 1], in1=rkvb,
                    op0=ALU.mult, op1=ALU.mult), reads=[t_ckvf[m], t_rkvb, t_const], writes=[t_ckvn[s]])
            P.add("act", lambda e, s=s: e.activation(out=K2[:, s * 512:(s + 1) * 512], in_=psb[4], func=AF.Copy),
                  reads=[tps[4]], writes=[t_k2[s]])
            yield

        for _ in load_slot(0):
            pass
        for i in range(8):
            gens = [compute_slot(i)]
            if i + 1 < 8:
                gens.append(load_slot(i + 1))
            while gens:
                for g in list(gens):
                    try:
                        next(g)
                    except StopIteration:
                        gens.remove(g)
        P.barrier()
        for b_ in range(2):
            for g_ in range(8):
                t_xg[b_][g_] = Tr()

        def wstream(region_off, nslots=3):
            return ([v16(region_off + b * 8192, 4096).rearrange("p (k c) -> p k c", k=32) for b in range(nslots)],
                    [Tr() for _ in range(nslots)])

        WS, t_ws = wstream(R_Y)
        o2 = R_Y + 24576
        ev = [v32(o2 + b * 2048, 512) for b in range(2)]
        t_ev = [Tr(), Tr()]
        sq2 = [v16(o2 + 4096 + b * 1024, 512) for b in range(2)]
        t_sq2 = [Tr(), Tr()]
        sqt2a = v32(o2 + 6144, 512)
        mts = list(range(5, 29))

        def wload(idx, m):
            dst, tdst = WS[idx % 3], t_ws[idx % 3]
            P.dma("pool", lambda e: e.dma_start(out=dst, in_=w_in_t[m], max_dma_last_dim=8192), writes=[tdst])

        for idx in range(3):
            wload(idx, mts[idx])
        nev = 0
        for idx, m in enumerate(mts):
            ws, tw = WS[idx % 3], t_ws[idx % 3]
            for bi in range(2):
                bank = (2 * idx + bi) % 4
                for kc in range(32):
                    mm(psb[bank], ws[:, kc, :], XG[bi][:, kc, :], kc == 0, kc == 31, [tw, t_xg[bi][kc // 4]], [tps[bank]])
                eb = nev % 2
                nev += 1
                P.add("dve", lambda e, bank=bank, bi=bi, eb=eb: e.tensor_tensor(
                    out=ev[eb], in0=psb[bank], in1=rb[bi], op=ALU.mult), reads=[tps[bank], t_rb[bi]], writes=[t_ev[eb]])
                if m < 13:
                    j = m - 5
                    P.add("act", lambda e, eb=eb: e.activation(out=sq2[eb], in_=ev[eb], func=AF.Square),
                          reads=[t_ev[eb]], writes=[t_sq2[eb]])
                    mm(psb[4 + bi], ones_bf, sq2[eb], j == 0, j == 7, [t_sq2[eb], t_const], [tps[4 + bi]])
                    P.add("dve", lambda e, eb=eb, j=j, bi=bi: e.tensor_scalar(
                        out=CQG[:, j, bi * 512:(bi + 1) * 512], in0=ev[eb], scalar1=g_q_t[:, j:j + 1], scalar2=None,
                        op0=ALU.mult), reads=[t_ev[eb], t_const], writes=[t_cqg[j][bi]])
                else:
                    hh = m - 13
                    P.add("act", lambda e, eb=eb, hh=hh, bi=bi: e.activation(
                        out=Y[:, 16 + hh, bi * 512:(bi + 1) * 512], in_=ev[eb], func=AF.Silu),
                        reads=[t_ev[eb]], writes=[t_y[16 + hh][bi]])
            if idx + 3 < len(mts):
                wload(idx + 3, mts[idx + 3])
            if m == 12:
                for bi in range(2):
                    rstd_from_psum(psb[4 + bi], tps[4 + bi], 1.0 / 1024, sqt2a, t_sqt, rqb[bi], t_rqb[bi])
                    P.add("dve", lambda e, bi=bi: e.tensor_tensor(out=csq[bi], in0=cs[bi], in1=rqb[bi], op=ALU.mult),
                          reads=[t_cs[bi], t_rqb[bi]], writes=[t_csq[bi]])
        P.barrier()

        o4 = R_XG
        KH = [v16(o4 + b * 8192, 4096) for b in range(2)]
        VH = v16(o4 + 16384, 8192).rearrange("p (t d) -> p t d", t=32)
        QN = [v16(o4 + 32768 + b * 2048, 1024).rearrange("p (b t) -> p b t", b=2) for b in range(2)]
        TQ = [v16(o4 + 36864 + b * 2048, 1024).rearrange("p (b t) -> p b t", b=2) for b in range(2)]
        PT = [v16(o4 + 40960 + b * 1024, 512) for b in range(4)]
        rc = [v32(o4 + 45056 + b * 2048, 512) for b in range(2)]
        WQ = [v16(o4 + 49152 + b * 4096, 2048).rearrange("p (k c) -> p k c", k=8) for b in range(2)]
        WKV = [v16(o4 + 57344 + b * 4096, 2048).rearrange("p (k c) -> p k c", k=4) for b in range(2)]
        t_kh = [[Tr() for _ in range(8)] for _ in range(2)]
        t_vh = [Tr() for _ in range(8)]
        t_qn = [[Tr(), Tr()] for _ in range(2)]
        t_tq = [[Tr(), Tr()] for _ in range(2)]
        t_pt = [Tr() for _ in range(4)]
        t_rc = [Tr(), Tr()]
        t_wq = [Tr(), Tr()]
        t_wkv = [Tr(), Tr()]

        def hload(hh):
            P.dma("pool", lambda e: e.dma_start(out=WQ[hh % 2], in_=w_uq_t[hh], max_dma_last_dim=8192), writes=[t_wq[hh % 2]])

        def pload(pr):
            P.dma("pool", lambda e: e.dma_start(out=WKV[pr % 2], in_=w_ukv_t[pr], max_dma_last_dim=8192), writes=[t_wkv[pr % 2]])

        hload(0)
        hload(1)
        pload(0)
        pload(1)
        stg = v32(188416 + 20480, 512)
        t_stg = Tr()
        XGA = v16(R_Y, 16384).rearrange("p (k t) -> p k t", k=32)
        t_xga = [Tr() for _ in range(32)]
        npt = 0
        nmisc = 0
        def pf_dma(kc):
            P.dma("sp", lambda e: e.dma_start(out=stg, in_=xT[3, :, kc, :]), writes=[t_stg])

        def pf_mul(kc):
            P.add("dve", lambda e: e.tensor_scalar(
                out=XGA[:, kc, :], in0=stg, scalar1=g_in_t[:, kc:kc + 1], scalar2=None, op0=ALU.mult),
                reads=[t_stg, t_const], writes=[t_xga[kc]])

        for hh in range(16):
            hb = hh % 2
            pr, hp = divmod(hh, 2)
            pf_dma(2 * hh)
            wq, wkv, twkv = WQ[hb], WKV[pr % 2], t_wkv[pr % 2]
            for bi in range(2):
                for part in range(2):
                    bank = nmisc % 4
                    nmisc += 1
                    for kc in range(8):
                        mm(psb[bank], wq[:, kc, part * 128:(part + 1) * 128], CQG[:, kc, bi * 512:(bi + 1) * 512],
                           kc == 0, kc == 7, [t_wq[hb], t_cqg[kc][bi]], [tps[bank]])
                    if part == 0:
                        P.add("dve", lambda e, bank=bank, hb=hb, bi=bi: e.tensor_tensor(
                            out=QN[hb][:, bi, :], in0=psb[bank], in1=rqb[bi], op=ALU.mult),
                            reads=[tps[bank], t_rqb[bi]], writes=[t_qn[hb][bi]])
                    else:
                        P.add("dve", lambda e, bank=bank, hb=hb, bi=bi: e.tensor_tensor(
                            out=TQ[hb][:, bi, :], in0=psb[bank], in1=csq[bi], op=ALU.mult),
                            reads=[tps[bank], t_csq[bi]], writes=[t_tq[hb][bi]])
            for s in range(8):
                bank = nmisc % 4
                nmisc += 1
                for kc in range(4):
                    mm(psb[bank], wkv[:, kc, hp * 128:(hp + 1) * 128], CKVN[:, kc, s * 512:(s + 1) * 512], kc == 0, kc == 3,
                       [twkv, t_ckvn[s]], [tps[bank]])
                P.add("act", lambda e, bank=bank, hb=hb, s=s: e.activation(
                    out=KH[hb][:, s * 512:(s + 1) * 512], in_=psb[bank], func=AF.Copy),
                    reads=[tps[bank]], writes=[t_kh[hb][s]])
            if hp == 0:
                for s_ in range(8):
                    for half in range(2):
                        bank = nmisc % 4
                        nmisc += 1
                        for t2 in range(2):
                            tt = 2 * half + t2
                            for kc in range(4):
                                mm(psb[bank][:, t2 * 256:(t2 + 1) * 256],
                                   CKVN[:, kc, s_ * 512 + tt * 128: s_ * 512 + (tt + 1) * 128], wkv[:, kc, 256:512],
                                   kc == 0, kc == 3, [twkv, t_ckvn[s_]], [tps[bank]])
                        P.add("dve", lambda e, bank=bank, s_=s_, half=half: e.tensor_copy(
                            out=VH[:, 4 * s_ + 2 * half:4 * s_ + 2 * half + 2, :],
                            in_=psb[bank].rearrange("p (t d) -> p t d", t=2)),
                            reads=[tps[bank]], writes=[t_vh[s_]])
            for bi in range(2):
                if bi == 0:
                    units = [(0, 0), (1, 1), (2, 2), (3, None)]
                else:
                    units = [(0, None), (1, None), (2, None), (3, None), (4, 3), (5, 4), (6, 5), (7, None)]
                diag_slot = 3 if bi == 0 else 7
                kts = []
                for (s, gc) in units:
                    for j in range(4):
                        kts.append((s, j, gc, s == diag_slot))
                po, pl = 4 + 2 * bi, 5 + 2 * bi
                n = len(kts)

                def QK(i):
                    s, j, gc, dg = kts[i]
                    c0 = 128 * j if dg else 0
                    bank = i % 4
                    kcol = (4 * s + j) * 128
                    mm(psb[bank][:, c0:512], KH[hb][:, kcol:kcol + 128], QN[hb][:, bi, c0:512], True, False,
                       [t_kh[hb][s], t_qn[hb][bi]], [tps[bank]])
                    mm(psb[bank][:, c0:512], K2[:, kcol:kcol + 128], TQ[hb][:, bi, c0:512], False, True,
                       [t_k2[s], t_tq[hb][bi]], [tps[bank]])

                QK(0)
                QK(1)
                for i in range(n):
                    if i + 2 < n:
                        QK(i + 2)
                    s, j, gc, dg = kts[i]
                    c0 = 128 * j if dg else 0
                    bank = i % 4
                    pb = npt % 4
                    npt += 1
                    if gc is None:
                        P.add("act", lambda e, bank=bank, pb=pb, c0=c0: e.activation(
                            out=PT[pb][:, c0:512], in_=psb[bank][:, c0:512], func=AF.Exp, scale=ATTN_SCALE),
                            reads=[tps[bank]], writes=[t_pt[pb]])
                    else:
                        P.add("act", lambda e, bank=bank, pb=pb, gc=gc: e.activation(
                            out=PT[pb], in_=psb[bank], func=AF.Exp, scale=ATTN_SCALE, bias=gate[:, gc:gc + 1]),
                            reads=[tps[bank], t_const], writes=[t_pt[pb]])
                    if dg:
                        P.add("dve", lambda e, pb=pb, c0=c0: e.memset(PT[pb][64:128, c0:c0 + 64], 0.0),
                              reads=[t_pt[pb]], writes=[t_pt[pb]])
                    mm(psb[po][:, c0:512], VH[:, 4 * s + j, hp * 128:(hp + 1) * 128], PT[pb][:, c0:512], i == 0, i == n - 1,
                       [t_vh[s], t_pt[pb]], [tps[po]])
                    mm(psb[pl][:, c0:512], ones_bf, PT[pb][:, c0:512], i == 0, i == n - 1,
                       [t_const, t_pt[pb]], [tps[pl]])
                P.add("dve", lambda e, pl=pl, bi=bi: e.reciprocal(out=rc[bi], in_=psb[pl]), reads=[tps[pl]], writes=[t_rc[bi]])
                P.add("dve", lambda e, po=po, bi=bi: e.tensor_tensor(out=rc[bi], in0=psb[po], in1=rc[bi], op=ALU.mult),
                      reads=[tps[po], t_rc[bi]], writes=[t_rc[bi]])
                P.add("dve", lambda e, bi=bi, hh=hh: e.tensor_tensor(
                    out=Y[:, 16 + hh, bi * 512:(bi + 1) * 512], in0=rc[bi], in1=Y[:, 16 + hh, bi * 512:(bi + 1) * 512],
                    op=ALU.mult), reads=[t_rc[bi], t_y[16 + hh][bi]], writes=[t_y[16 + hh][bi]])
                if bi == 0:
                    pf_mul(2 * hh)
                    pf_dma(2 * hh + 1)
            if hh + 2 < 16:
                hload(hh + 2)
            if hp == 1 and pr + 2 < 8:
                pload(pr + 2)
            pf_mul(2 * hh + 1)
        P.barrier()

        WS, t_ws = wstream(R_CKVN)
        o3 = R_CKVN + 24576
        T1 = [v32(o3 + b * 8320, 512) for b in range(2)]
        UU = [v32(o3 + b * 8320 + 2048, 516) for b in range(2)]
        CC = [v32(o3 + b * 8320 + 4112, 512) for b in range(2)]
        GG = [v32(o3 + b * 8320 + 6160, 512) for b in range(2)]
        t_t1 = [Tr(), Tr()]
        t_uu = [Tr(), Tr()]
        t_cc = [Tr(), Tr()]
        t_gg = [Tr(), Tr()]
        o3b = o3 + 2 * 8320
        rb2 = [v32(o3b + b * 2048, 512) for b in range(2)]
        t_rb2 = [Tr(), Tr()]
        o3c = o3b + 4096
        xst1 = [v32(R_XG + b * 4096, 1024).rearrange("p (k t) -> p k t", k=2) for b in range(8)]
        t_xst1 = [Tr() for _ in range(8)]
        o3d = o3c + 8192
        xhs = v32(o3d, 128).rearrange("p (k t) -> p k t", k=32)
        xgh = v16(o3d + 512, 128).rearrange("p (k t) -> p k t", k=32)
        xhq = v16(o3d + 768, 128).rearrange("p (k t) -> p k t", k=32)
        th = v32(o3d + 1024, 8)
        assert o3d + 1024 + 32 <= R_MISC
        t_h = Tr()
        mts = list(range(29, 93))
        for idx in range(3):
            wload(idx, mts[idx])
        Yc = v16(R_XG, 16384).rearrange("p (f t) -> p f t", f=16)
        nr = 0
        for bi, s in ((1, 7),):
            for kg in range(16):
                stb = nr % 8
                nr += 1
                P.dma("sp", lambda e, s=s, kg=kg, stb=stb: e.dma_start(out=xst1[stb], in_=xT[s, :, 2 * kg:2 * kg + 2, :]),
                      writes=[t_xst1[stb]])
                for j in range(2):
                    kc = 2 * kg + j
                    P.add("dve", lambda e, bi=bi, kc=kc, j=j, stb=stb: e.tensor_scalar(
                        out=XG[bi][:, kc, :], in0=xst1[stb][:, j, :], scalar1=g_in_t[:, kc:kc + 1], scalar2=None,
                        op0=ALU.mult), reads=[t_xst1[stb], t_const], writes=[t_xg[bi][kc // 4]])
        for bi in range(2):
            P.add("dve", lambda e, bi=bi: e.tensor_tensor(out=rb2[bi], in0=rb[bi], in1=rb[bi], op=ALU.mult),
                  reads=[t_rb[bi]], writes=[t_rb2[bi]])
        P.dma("sp", lambda e: e.dma_start(out=xhs, in_=xh), writes=[t_h])
        P.add("dve", lambda e: e.tensor_tensor(out=xgh, in0=xhs, in1=g_in_t.unsqueeze(2).to_broadcast([128, 32, 4]),
                                               op=ALU.mult), reads=[t_h, t_const], writes=[t_h])
        P.add("act", lambda e: e.activation(out=xhq, in_=xhs, func=AF.Square), reads=[t_h], writes=[t_h])
        for kc in range(32):
            mm(psb[7][:, 0:4], ones_bf, xhq[:, kc, :], kc == 0, kc == 31, [t_h, t_const], [tps[7]])
        rstd_from_psum(psb[7], tps[7], 1.0 / D, sqt, t_sqt, rbh, t_small, n=4)
        P.add("dve", lambda e: e.tensor_tensor(out=rbh2, in0=rbh, in1=rbh, op=ALU.mult), reads=[t_small], writes=[t_small])
        P.barrier()

        for idx, m in enumerate(mts):
            f, which = divmod(idx, 4)
            ws, tw = WS[idx % 3], t_ws[idx % 3]
            banks = [(2 * idx) % 6, (2 * idx + 1) % 6]
            hc = 4 * which
            for kc in range(32):
                mm(psb[banks[0]], ws[:, kc, :], XGA[:, kc, :], kc == 0, kc == 31, [tw, t_xga[kc]], [tps[banks[0]]])
                mm(psb[banks[1]], ws[:, kc, :], XG[1][:, kc, :], kc == 0, kc == 31, [tw, t_xg[1][kc // 4]], [tps[banks[1]]])
                if which < 2:
                    mm(psb[6 + (f % 2)][:, hc:hc + 4], ws[:, kc, :], xgh[:, kc, :], kc == 0, kc == 31, [tw, t_h],
                       [tps[6 + (f % 2)]])
            if idx + 3 < len(mts):
                wload(idx + 3, mts[idx + 3])
            hb_ = 6 + (f % 2)
            for bi in range(2):
                bank = banks[bi]
                tb = bi
                if which == 0:
                    P.add("act", lambda e, bank=bank, tb=tb: e.activation(out=T1[tb], in_=psb[bank], func=AF.Copy),
                          reads=[tps[bank]], writes=[t_t1[tb]])
                elif which == 1:
                    P.add("dve", lambda e, bank=bank, tb=tb: e.tensor_tensor(out=T1[tb], in0=psb[bank], in1=T1[tb], op=ALU.mult),
                          reads=[tps[bank], t_t1[tb]], writes=[t_t1[tb]])
                    P.add("dve", lambda e, tb=tb, bi=bi: e.tensor_tensor(out=UU[tb][:, 2:514], in0=T1[tb], in1=rb2[bi], op=ALU.mult),
                          reads=[t_t1[tb], t_rb2[bi]], writes=[t_uu[tb]])
                    if bi == 0:
                        P.add("act", lambda e, hb_=hb_: e.activation(out=th[:, 0:4], in_=psb[hb_][:, 0:4], func=AF.Copy),
                              reads=[tps[hb_]], writes=[t_small])
                        P.add("dve", lambda e, hb_=hb_: e.tensor_tensor(out=th[:, 0:4], in0=psb[hb_][:, 4:8], in1=th[:, 0:4], op=ALU.mult),
                              reads=[tps[hb_], t_small], writes=[t_small])
                        P.add("dve", lambda e: e.tensor_tensor(out=uh, in0=th[:, 0:4], in1=rbh2, op=ALU.mult),
                              reads=[t_small], writes=[t_small])
                    P.add("dve", lambda e, tb=tb, bi=bi: e.tensor_copy(out=UU[tb][:, 0:2], in_=uh[:, 2 * bi:2 * bi + 2]),
                          reads=[t_small, t_uu[tb]], writes=[t_uu[tb]])
                    cw = 3 * f
                    P.add("act", lambda e, tb=tb, cw=cw: e.activation(out=CC[tb], in_=UU[tb][:, 0:512], func=AF.Copy,
                                                                       scale=convw[:, cw:cw + 1]),
                          reads=[t_uu[tb], t_const], writes=[t_cc[tb]])
                    P.add("dve", lambda e, tb=tb, cw=cw: e.scalar_tensor_tensor(
                        out=CC[tb], in0=UU[tb][:, 1:513], scalar=convw[:, cw + 1:cw + 2], in1=CC[tb], op0=ALU.mult, op1=ALU.add),
                        reads=[t_uu[tb], t_cc[tb], t_const], writes=[t_cc[tb]])
                    P.add("dve", lambda e, tb=tb, cw=cw: e.scalar_tensor_tensor(
                        out=CC[tb], in0=UU[tb][:, 2:514], scalar=convw[:, cw + 2:cw + 3], in1=CC[tb], op0=ALU.mult, op1=ALU.add),
                        reads=[t_uu[tb], t_cc[tb], t_const], writes=[t_cc[tb]])
                elif which == 2:
                    P.add("dve", lambda e, bank=bank, tb=tb, bi=bi: e.tensor_tensor(out=GG[tb], in0=psb[bank], in1=rb[bi], op=ALU.mult),
                          reads=[tps[bank], t_rb[bi]], writes=[t_gg[tb]])
                    P.add("dve", lambda e, tb=tb: e.tensor_tensor(out=GG[tb], in0=GG[tb], in1=CC[tb], op=ALU.mult),
                          reads=[t_gg[tb], t_cc[tb]], writes=[t_gg[tb]])
                else:
                    P.add("dve", lambda e, bank=bank, tb=tb, bi=bi: e.tensor_tensor(out=T1[tb], in0=psb[bank], in1=rb[bi], op=ALU.mult),
                          reads=[tps[bank], t_rb[bi]], writes=[t_t1[tb]])
                    P.add("act", lambda e, tb=tb: e.activation(out=T1[tb], in_=T1[tb], func=AF.Silu),
                          reads=[t_t1[tb]], writes=[t_t1[tb]])
                    P.add("dve", lambda e, tb=tb, f=f, bi=bi: e.tensor_tensor(
                        out=Yc[:, f, bi * 512:(bi + 1) * 512], in0=GG[tb], in1=T1[tb], op=ALU.mult),
                        reads=[t_gg[tb], t_t1[tb]], writes=[t_y[f][bi]])
        P.barrier()

        Ht = [v32(R_Y + tt * 16384, 4096) for tt in range(2)] + [v32(R_XG + 32768 + tt * 16384, 4096) for tt in range(2)]
        t_hh = [Tr() for _ in range(4)]
        WO = [v16(R_CKVN + b * 4096, 2048).rearrange("p (k c) -> p k c", k=4) for b in range(4)]
        t_wo = [Tr() for _ in range(4)]
        XR = [v32(R_CKVN + 16384 + b * 8192, 2048).rearrange("p (t c) -> p t c", t=4) for b in range(2)]
        t_xr = [Tr(), Tr()]
        GF = v32(R_CKVN + 32768, 4096)
        t_gf = Tr()
        JUNK = v16(R_CKVN + 49152, 512)
        t_junk = Tr()
        assert R_CKVN + 49152 + 1024 <= R_MISC
        t_out = [Tr("out%d" % i) for i in range(4)]
        t_ssq = [[Tr() for _ in range(8)] for _ in range(4)]
        P.dma("sp", lambda e: e.dma_start(out=GF, in_=gfin_d), writes=[t_gf])
        wlist = [(grp, ct, kg) for grp in range(2) for ct in range(8) for kg in range(8)]

        def woload(i):
            grp, ct, kg = wlist[i]
            P.dma("pool", lambda e: e.dma_start(out=WO[i % 4], in_=w_out_t[ct, kg], max_dma_last_dim=8192), writes=[t_wo[i % 4]])

        for i in range(4):
            woload(i)
        wi = 0
        for grp in range(2):
            for ct in range(8):
                xb = ct % 2
                P.dma("sp", lambda e, grp=grp, ct=ct, xb=xb: e.dma_start(
                    out=XR[xb], in_=xres[grp * 512:(grp + 1) * 512, ct * 512:(ct + 1) * 512].rearrange("(t p) c -> p t c", p=128)),
                    writes=[t_xr[xb]])
                pbase = 4 * (ct % 2)
                for kg in range(8):
                    wo, two = WO[wi % 4], t_wo[wi % 4]
                    for tt in range(4):
                        for kcc in range(4):
                            kc = 4 * kg + kcc
                            tok0 = grp * 512 + tt * 128
                            ysrc = Yc[:, kc, tok0:tok0 + 128] if kc < 16 else Y[:, kc, tok0:tok0 + 128]
                            mm(psb[pbase + tt], ysrc, wo[:, kcc, :], kc == 0, kc == 31,
                               [two, t_y[kc][grp]], [tps[pbase + tt]])
                    if wi + 4 < len(wlist):
                        woload(wi + 4)
                    wi += 1
                for tt in range(4):
                    P.add("dve", lambda e, pbase=pbase, tt=tt, ct=ct, xb=xb: e.tensor_tensor(
                        out=Ht[tt][:, ct * 512:(ct + 1) * 512], in0=psb[pbase + tt], in1=XR[xb][:, tt, :], op=ALU.add),
                        reads=[tps[pbase + tt], t_xr[xb]], writes=[t_hh[tt]])
                    P.add("act", lambda e, tt=tt, ct=ct: e.activation(
                        out=JUNK, in_=Ht[tt][:, ct * 512:(ct + 1) * 512], func=AF.Square,
                        accum_out=ssq[:, tt * 8 + ct: tt * 8 + ct + 1]),
                        reads=[t_hh[tt]], writes=[t_ssq[tt][ct], t_junk])
            P.add("dve", lambda e: e.reduce_sum(out=tot, in_=ssq.rearrange("p (t c) -> p t c", t=4), axis=AX.X),
                  reads=[t_small] + [t for row in t_ssq for t in row], writes=[t_small])
            P.add("act", lambda e: e.activation(out=tot, in_=tot, func=AF.Sqrt, scale=1.0 / D, bias=eps_ap),
                  reads=[t_small, t_const], writes=[t_small])
            P.add("dve", lambda e: e.reciprocal(out=rfin, in_=tot), reads=[t_small], writes=[t_small])
            for tt in range(4):
                for ct in range(8):
                    P.add("dve", lambda e, tt=tt, ct=ct: e.scalar_tensor_tensor(
                        out=Ht[tt][:, ct * 512:(ct + 1) * 512], in0=Ht[tt][:, ct * 512:(ct + 1) * 512],
                        scalar=rfin[:, tt:tt + 1], in1=GF[:, ct * 512:(ct + 1) * 512], op0=ALU.mult, op1=ALU.mult),
                        reads=[t_hh[tt], t_small, t_gf], writes=[t_hh[tt]])
                r0 = grp * 512 + tt * 128
                P.dma("sp", lambda e, tt=tt, r0=r0: e.dma_start(out=out[r0:r0 + 128, :], in_=Ht[tt]),
                      reads=[t_hh[tt]], sem_tr=t_out[tt])
        P.barrier()
        P.emit(nc, stack)
    return nc


def _tile_w(w, cols, kc):
    sub = w[:, cols]
    return np.ascontiguousarray(sub.reshape(kc, 128, len(cols)).transpose(1, 0, 2))


def _prep_weights(g_in, w_in, conv_w, q_norm_g, w_uq, kv_norm_g, w_ukv, w_out, g_final):
    w_in = w_in[0]
    ar = np.arange
    perm = np.concatenate([ar(0, 32), ar(32, 64), ar(32, 64), ar(0, 32)])
    col_tiles = []
    for m in range(4):
        col_tiles.append(9216 + m * 128 + ar(128))
    col_tiles.append(9728 + perm)
    for m in range(8):
        col_tiles.append(8192 + m * 128 + ar(128))
    for m in range(16):
        col_tiles.append(9792 + m * 128 + ar(128))
    for f in range(16):
        for base in (2048, 4096, 0, 6144):
            col_tiles.append(base + f * 128 + ar(128))
    assert len(col_tiles) == NMT
    w_in_t = np.empty((NMT, 128, 32, 128), np.float32)
    for m, cols in enumerate(col_tiles):
        w_in_t[m] = _tile_w(w_in, cols, 32)
    w_uq_t = np.empty((16, 128, 8, 256), np.float32)
    for h in range(16):
        cols = np.concatenate([h * 192 + ar(128), h * 192 + 128 + perm])
        w_uq_t[h] = _tile_w(w_uq[0], cols, 8)
    w_ukv_t = np.empty((8, 128, 4, 512), np.float32)
    for pr in range(8):
        h0, h1 = 2 * pr, 2 * pr + 1
        cols = np.concatenate([h0 * 256 + ar(128), h1 * 256 + ar(128), h0 * 256 + 128 + ar(128), h1 * 256 + 128 + ar(128)])
        w_ukv_t[pr] = _tile_w(w_ukv[0], cols, 4)
    w_out_t = np.ascontiguousarray(w_out[0].reshape(8, 4, 128, 8, 512).transpose(3, 0, 2, 1, 4))
    dmat = np.zeros((128, 128), np.float32)
    i64 = ar(64)
    for a in (0, 64):
        for b in (0, 64):
            dmat[a + i64, b + i64] = 1.0
    return {
        "w_in_t": w_in_t, "w_uq_t": w_uq_t, "w_ukv_t": w_ukv_t, "w_out_t": w_out_t,
        "g_in_t": np.ascontiguousarray(g_in[0].reshape(32, 128).T),
        "g_q_t": np.ascontiguousarray(q_norm_g[0].reshape(8, 128).T),
        "g_kv_t": np.ascontiguousarray(kv_norm_g[0].reshape(4, 128).T),
        "convw_t": np.ascontiguousarray(conv_w[0].reshape(3, 16, 128).transpose(2, 1, 0).reshape(128, 48)),
        "gfin_b": np.ascontiguousarray(np.broadcast_to(g_final[None, :], (128, D))),
        "dmat": dmat,
    }


def _rope_table():
    pos = np.arange(S, dtype=np.float32)
    inv_freq = (1.0 / (np.float32(10000.0) ** (np.arange(0, 64, 2, dtype=np.float32) / np.float32(64)))).astype(np.float32)
    ang = (pos[:, None] * inv_freq[None, :]).astype(np.float32)
    c = np.cos(ang).astype(np.float32).T
    s = np.sin(ang).astype(np.float32).T
    return np.concatenate([c, c, -s, s], axis=0)


def _slots(k):
    low = [i for i in range(4) if i != k]
    high = [i for i in range(4, 8) if i != 7 - k]
    return low + [k] + high + [7 - k]


def kernel(x, g_in, w_in, conv_w, q_norm_g, w_uq, kv_norm_g, w_ukv, w_out, g_final):
    x = np.asarray(x, np.float32)
    wts = _prep_weights(*(np.asarray(a, np.float32) for a in
                          (g_in, w_in, conv_w, q_norm_g, w_uq, kv_norm_g, w_ukv, w_out, g_final)))
    cs_full = _rope_table()
    in_maps = []
    for c in range(NCORES):
        b, k = divmod(c, 4)
        sl = _slots(k)
        xb = x[b]
        xT = np.empty((8, 128, 32, 512), np.float32)
        cs_tab = np.empty((8, 128, 512), np.float32)
        for si, blk in enumerate(sl):
            xT[si] = xb[blk * BLK:(blk + 1) * BLK, :].reshape(512, 32, 128).transpose(2, 1, 0)
            cs_tab[si] = cs_full[:, blk * BLK:(blk + 1) * BLK]
        A, B = k, 7 - k
        xres = np.concatenate([xb[A * BLK:(A + 1) * BLK], xb[B * BLK:(B + 1) * BLK]], axis=0)
        halo = np.zeros((4, D), np.float32)
        if A > 0:
            halo[0:2] = xb[A * BLK - 2:A * BLK]
        halo[2:4] = xb[B * BLK - 2:B * BLK]
        xh = np.ascontiguousarray(halo.reshape(4, 32, 128).transpose(2, 1, 0))
        gate = np.zeros((128, 6), np.float32)
        for j in range(3):
            gate[:, j] = 0.0 if sl[j] < A else NEG
            gate[:, 3 + j] = 0.0 if sl[4 + j] < B else NEG
        m = {"xT": xT, "xres": np.ascontiguousarray(xres), "xh": xh, "cs_tab": cs_tab, "gate": gate}
        m.update(wts)
        in_maps.append(m)
    nc = build_program()
    res = run_bass_kernel_spmd(nc, in_maps, core_ids=list(range(NCORES)))
    outp = np.empty((2, S, D), np.float32)
    for c in range(NCORES):
        b, k = divmod(c, 4)
        o = res.results[c]["out"]
        outp[b, k * BLK:(k + 1) * BLK] = o[0:512]
        outp[b, (7 - k) * BLK:(8 - k) * BLK] = o[512:1024]
    return outp
```

```python
import contextlib
import numpy as np
import concourse.bass as bass
import concourse.mybir as mybir
from concourse.bass_utils import run_bass_kernel_spmd

F32 = mybir.dt.float32
BF16 = mybir.dt.bfloat16
AF = mybir.ActivationFunctionType
ALU = mybir.AluOpType
AX = mybir.AxisListType

NCORES = 8
D = 4096
S = 4096
BLK = 512
NMT = 93
EPS = 1e-6
ATTN_SCALE = 192 ** -0.5
ORDER = [0, 1, 2, 4, 5, 6, 3, 7]
NEG = -30000.0


class Tr:
    __slots__ = ("lastw", "readers", "sem", "cnt", "name")

    def __init__(self, name=""):
        self.lastw = None
        self.readers = []
        self.sem = None
        self.cnt = 0
        self.name = name


class Ins:
    __slots__ = ("eng", "fn", "deps", "signal", "ordinal", "dma_tr", "dma_val")

    def __init__(self, eng, fn, deps):
        self.eng = eng
        self.fn = fn
        self.deps = deps
        self.signal = False
        self.ordinal = 0
        self.dma_tr = None
        self.dma_val = 0


ENGS = ("pe", "act", "dve", "pool", "sp")


class Prog:
    def __init__(self):
        self.ins = {e: [] for e in ENGS}
        self.last = {e: None for e in ENGS}
        self.dma_trs = []
        self.sync_same_engine = True

    def _deps(self, eng, reads, writes):
        deps = []
        for t in reads:
            if t.lastw is not None:
                deps.append(t.lastw)
        for t in writes:
            if t.lastw is not None:
                deps.append(t.lastw)
            deps.extend(t.readers)
        out = []
        for d in deps:
            if d[0] == "c":
                i = d[1]
                if i.eng == "pe" and eng == "pe":
                    continue
                if i.eng == eng and not self.sync_same_engine:
                    continue
                i.signal = True
            out.append(d)
        return out

    def add(self, eng, fn, reads=(), writes=()):
        ins = Ins(eng, fn, self._deps(eng, reads, writes))
        ev = ("c", ins)
        for t in reads:
            t.readers.append(ev)
        for t in writes:
            t.lastw = ev
            t.readers = []
        self.ins[eng].append(ins)
        self.last[eng] = ins
        return ins

    def dma(self, eng, fn, reads=(), writes=(), sem_tr=None):
        ins = Ins(eng, fn, self._deps(eng, reads, writes))
        if sem_tr is None:
            sem_tr = writes[0]
        if sem_tr not in self.dma_trs:
            self.dma_trs.append(sem_tr)
        sem_tr.cnt += 16
        ins.dma_tr = sem_tr
        ins.dma_val = sem_tr.cnt
        ev = ("d", sem_tr, sem_tr.cnt)
        for t in reads:
            t.readers.append(ev)
        for t in writes:
            t.lastw = ev
            t.readers = []
        self.ins[eng].append(ins)
        return ins

    def barrier(self):
        evs = []
        for e in ("pe", "act", "dve", "pool"):
            if self.last[e] is not None:
                self.last[e].signal = True
                evs.append(("c", self.last[e]))
        for t in self.dma_trs:
            evs.append(("d", t, t.cnt))
        for e in ENGS:
            self.ins[e].append(Ins(e, None, list(evs)))

    def emit(self, nc, stack):
        esem = {e: stack.enter_context(nc.semaphore("sem_" + e)) for e in ("pe", "act", "dve", "pool")}
        for i, t in enumerate(self.dma_trs):
            t.sem = stack.enter_context(nc.semaphore("dsem%d" % i))
        for e in ("pe", "act", "dve", "pool"):
            n = 0
            for i in self.ins[e]:
                if i.signal:
                    n += 1
                    i.ordinal = n
        block = stack.enter_context(nc.Block())

        def run(ename, eng):
            waited = {}
            for i in self.ins[ename]:
                for d in i.deps:
                    if d[0] == "c":
                        sem, val, key = esem[d[1].eng], d[1].ordinal, d[1].eng
                    else:
                        sem, val, key = d[1].sem, d[2], id(d[1])
                    if waited.get(key, 0) >= val:
                        continue
                    waited[key] = val
                    eng.wait_ge(sem, val)
                if i.fn is None:
                    continue
                r = i.fn(eng)
                if i.dma_tr is not None:
                    r.then_inc(i.dma_tr.sem, 16)
                elif i.signal:
                    r.then_inc(esem[ename], 1)

        @block.tensor
        def _(e):
            run("pe", e)

        @block.scalar
        def _(e):
            run("act", e)

        @block.vector
        def _(e):
            run("dve", e)

        @block.gpsimd
        def _(e):
            run("pool", e)

        @block.sync
        def _(e):
            run("sp", e)


R_XG = 0
R_Y = 65536
R_CKVN = 131072
R_K2 = 163840
R_CQG = 172032
R_MISC = 188416
ARENA = 188416 + 20480 + 2048


def build_program(dbg=False):
    nc = bass.Bass("TRN2", target_bir_lowering=False)
    P = Prog()

    def din(name, shape):
        return nc.dram_tensor(name, list(shape), F32, kind="ExternalInput").ap()

    xT = din("xT", [8, 128, 32, 512])
    xres = din("xres", [1024, D])
    xh = din("xh", [128, 32, 4])
    cs_tab = din("cs_tab", [8, 128, 512])
    gate_d = din("gate", [128, 6])
    w_in_t = din("w_in_t", [NMT, 128, 32, 128])
    w_uq_t = din("w_uq_t", [16, 128, 8, 256])
    w_ukv_t = din("w_ukv_t", [8, 128, 4, 512])
    w_out_t = din("w_out_t", [8, 8, 128, 4, 512])
    g_in_d = din("g_in_t", [128, 32])
    g_q_d = din("g_q_t", [128, 8])
    g_kv_d = din("g_kv_t", [128, 4])
    convw_d = din("convw_t", [128, 48])
    gfin_d = din("gfin_b", [128, D])
    dmat_d = din("dmat", [128, 128])
    out = nc.dram_tensor("out", [1024, D], F32, kind="ExternalOutput").ap()

    stack = contextlib.ExitStack()
    with stack:
        arena = stack.enter_context(nc.sbuf_tensor("arena", [128, ARENA // 4], F32))
        psb_t = [stack.enter_context(nc.psum_tensor("psb%d" % i, [128, 512], F32)) for i in range(8)]
        psb = [t[:, :] for t in psb_t]
        tps = [Tr("ps%d" % i) for i in range(8)]

        def v32(off, n):
            return arena[:, off // 4: off // 4 + n]

        def v16(off, n):
            return arena[:, off // 4: off // 4 + n // 2].bitcast(BF16)

        mo = [R_MISC]

        def misc32(n):
            a = v32(mo[0], n)
            mo[0] += 4 * n
            return a

        def misc16(n):
            a = v16(mo[0], n)
            mo[0] += 2 * n
            return a

        g_in_t = misc32(32)
        g_q_t = misc32(8)
        g_kv_t = misc32(4)
        convw = misc32(48)
        gate = misc32(6)
        ssq = misc32(32)
        tot = misc32(4)
        rfin = misc32(4)
        rbh = misc32(4)
        rbh2 = misc32(4)
        uh = misc32(4)
        dmat = misc32(128)
        ones_bf = misc16(128)
        rb = [misc32(512) for _ in range(2)]
        cs = [misc32(512) for _ in range(2)]
        rqb = [misc32(512) for _ in range(2)]
        csq = [misc32(512) for _ in range(2)]
        rkvb = misc32(512)
        assert mo[0] <= ARENA, mo[0]
        t_const = Tr("const")
        t_rb = [Tr(), Tr()]
        t_cs = [Tr(), Tr()]
        t_rqb = [Tr(), Tr()]
        t_csq = [Tr(), Tr()]
        t_rkvb = Tr()
        t_small = Tr("small")

        XG = [v16(R_XG + b * 32768, 16384).rearrange("p (k t) -> p k t", k=32) for b in range(2)]
        t_xg = [[Tr() for _ in range(8)] for _ in range(2)]
        Y = v16(R_Y, 32768).rearrange("p (f t) -> p f t", f=32)
        t_y = [[Tr() for _ in range(2)] for _ in range(32)]
        CKVN = v16(R_CKVN, 16384).rearrange("p (k t) -> p k t", k=4)
        t_ckvn = [Tr() for _ in range(8)]
        K2 = v16(R_K2, 4096)
        t_k2 = [Tr() for _ in range(8)]
        CQG = v16(R_CQG, 8192).rearrange("p (k t) -> p k t", k=8)
        t_cqg = [[Tr() for _ in range(2)] for _ in range(8)]

        for dst, src in ((g_in_t, g_in_d), (g_q_t, g_q_d), (g_kv_t, g_kv_d), (convw, convw_d),
                         (gate, gate_d), (dmat, dmat_d)):
            P.dma("sp", lambda e, dst=dst, src=src: e.dma_start(out=dst, in_=src), writes=[Tr()])
        P.add("dve", lambda e: e.memset(ones_bf, 1.0), writes=[Tr()])

        def mm(out_, lhsT, rhs, start, stop, reads, writes):
            P.add("pe", lambda e: e.matmul(out_, lhsT, rhs, start=start, stop=stop), reads=reads, writes=writes)

        def rstd_from_psum(bank, tbank, scale, tmp, ttmp, dst, tdst, n=512):
            P.add("act", lambda e: e.activation(out=tmp[:, 0:n], in_=bank[:, 0:n], func=AF.Sqrt, scale=scale, bias=eps_ap),
                  reads=[tbank, t_const], writes=[ttmp])
            P.add("dve", lambda e: e.reciprocal(out=dst[:, 0:n], in_=tmp[:, 0:n]), reads=[ttmp], writes=[tdst])

        eps_ap = misc32(1) if False else None
        eps_ap = v32(mo[0], 1)
        mo[0] += 4
        P.add("dve", lambda e: e.memset(eps_ap, EPS), writes=[Tr()])
        P.barrier()

        WCKV = v16(R_Y, 32 * 640).rearrange("p (k c) -> p k c", k=32)
        t_wckv = Tr()
        o1 = R_Y + 40960
        xst = [v32(o1 + b * 4096, 1024).rearrange("p (k t) -> p k t", k=2) for b in range(4)]
        t_xst = [Tr() for _ in range(4)]
        ckvf = v32(o1 + 16384, 2048).rearrange("p (k t) -> p k t", k=4)
        t_ckvf = [Tr() for _ in range(4)]
        xsq = [v16(R_CQG + b * 2048, 1024).rearrange("p (k t) -> p k t", k=2) for b in range(4)]
        t_xsq = [Tr() for _ in range(4)]
        tk = v32(R_CQG + 8192, 512)
        t_tk = Tr()
        ckvsq = v16(R_CQG + 10240, 2048).rearrange("p (k t) -> p k t", k=4)
        t_ckvsq = Tr()
        sqt = v32(R_CQG + 14336, 512)
        t_sqt = Tr()
        t_xgk = [[Tr() for _ in range(32)] for _ in range(2)]
        for b_ in range(2):
            for g_ in range(8):
                t_xg[b_][g_] = None

        for m in range(5):
            P.dma("pool", lambda e, m=m: e.dma_start(out=WCKV[:, :, m * 128:(m + 1) * 128], in_=w_in_t[m],
                                                       max_dma_last_dim=8192), writes=[t_wckv])
        nchunk = [0]

        def load_slot(i):
            s = ORDER[i]
            xb = i % 2
            for kg in range(16):
                stb = nchunk[0] % 4
                nchunk[0] += 1
                P.dma("sp", lambda e, s=s, kg=kg, stb=stb: e.dma_start(out=xst[stb], in_=xT[s, :, 2 * kg:2 * kg + 2, :]),
                      writes=[t_xst[stb]])
                for j in range(2):
                    kc = 2 * kg + j
                    P.add("dve", lambda e, xb=xb, kc=kc, stb=stb, j=j: e.tensor_scalar(
                        out=XG[xb][:, kc, :], in0=xst[stb][:, j, :], scalar1=g_in_t[:, kc:kc + 1], scalar2=None,
                        op0=ALU.mult), reads=[t_xst[stb], t_const], writes=[t_xgk[xb][kc]])
                P.add("act", lambda e, stb=stb: e.activation(out=xsq[stb], in_=xst[stb], func=AF.Square),
                      reads=[t_xst[stb]], writes=[t_xsq[stb]])
                for j in range(2):
                    mm(psb[0], ones_bf, xsq[stb][:, j, :], kg == 0 and j == 0, kg == 15 and j == 1,
                       [t_xsq[stb], t_const], [tps[0]])
                yield
            P.add("act", lambda e, xb=xb: e.activation(out=rb[xb], in_=psb[0], func=AF.Sqrt, scale=1.0 / D, bias=eps_ap),
                  reads=[tps[0], t_const], writes=[t_rb[xb]])
            P.add("dve", lambda e, xb=xb: e.reciprocal(out=rb[xb], in_=rb[xb]), reads=[t_rb[xb]], writes=[t_rb[xb]])
            P.dma("sp", lambda e, s=s, xb=xb: e.dma_start(out=cs[xb], in_=cs_tab[s]), writes=[t_cs[xb]])
            yield

        def compute_slot(i):
            s = ORDER[i]
            xb = i % 2
            for m in range(5):
                bank = (1, 2, 5, 6)[m % 4]
                for kc in range(32):
                    mm(psb[bank], WCKV[:, kc, m * 128:(m + 1) * 128], XG[xb][:, kc, :], kc == 0, kc == 31,
                       [t_wckv, t_xgk[xb][kc]], [tps[bank]])
                    if kc % 10 == 9:
                        yield
                if m < 4:
                    P.add("dve", lambda e, bank=bank, m=m, xb=xb: e.tensor_tensor(
                        out=ckvf[:, m, :], in0=psb[bank], in1=rb[xb], op=ALU.mult),
                        reads=[tps[bank], t_rb[xb]], writes=[t_ckvf[m]])
                else:
                    P.add("dve", lambda e, bank=bank, xb=xb: e.tensor_tensor(
                        out=tk, in0=psb[bank], in1=rb[xb], op=ALU.mult), reads=[tps[bank], t_rb[xb]], writes=[t_tk])
                    P.add("dve", lambda e, xb=xb: e.tensor_tensor(out=tk, in0=tk, in1=cs[xb], op=ALU.mult),
                          reads=[t_tk, t_cs[xb]], writes=[t_tk])
            P.add("act", lambda e: e.activation(out=ckvsq, in_=ckvf, func=AF.Square), reads=t_ckvf, writes=[t_ckvsq])
            for j in range(4):
                mm(psb[3], ones_bf, ckvsq[:, j, :], j == 0, j == 3, [t_ckvsq, t_const], [tps[3]])
            P.add("act", lambda e: e.activation(out=rkvb, in_=psb[3], func=AF.Sqrt, scale=1.0 / 512, bias=eps_ap),
                  reads=[tps[3], t_const], writes=[t_rkvb])
            P.add("dve", lambda e: e.reciprocal(out=rkvb, in_=rkvb), reads=[t_rkvb], writes=[t_rkvb])
            mm(psb[4], dmat, tk, True, True, [t_tk, t_const], [tps[4]])
            yield
            for m in range(4):
                P.add("dve", lambda e, m=m, s=s: e.scalar_tensor_tensor(
                    out=CKVN[:, m, s * 512:(s + 1) * 512], in0=ckvf[:, m, :], scalar=g_kv_t[:, m:m + 1], in1=rkvb,
                    op0=ALU.mult, op1=ALU.mult), reads=[t_ckvf[m], t_rkvb, t_const], writes=[t_ckvn[s]])
            P.add("act", lambda e, s=s: e.activation(out=K2[:, s * 512:(s + 1) * 512], in_=psb[4], func=AF.Copy),
                  reads=[tps[4]], writes=[t_k2[s]])
            yield

        for _ in load_slot(0):
            pass
        for i in range(8):
            gens = [compute_slot(i)]
            if i + 1 < 8:
                gens.append(load_slot(i + 1))
            while gens:
                for g in list(gens):
                    try:
                        next(g)
                    except StopIteration:
                        gens.remove(g)
        P.barrier()
        for b_ in range(2):
            for g_ in range(8):
                t_xg[b_][g_] = Tr()

        def wstream(region_off, nslots=3):
            return ([v16(region_off + b * 8192, 4096).rearrange("p (k c) -> p k c", k=32) for b in range(nslots)],
                    [Tr() for _ in range(nslots)])

        WS, t_ws = wstream(R_Y)
        o2 = R_Y + 24576
        ev = [v32(o2 + b * 2048, 512) for b in range(2)]
        t_ev = [Tr(), Tr()]
        sq2 = [v16(o2 + 4096 + b * 1024, 512) for b in range(2)]
        t_sq2 = [Tr(), Tr()]
        sqt2a = v32(o2 + 6144, 512)
        mts = list(range(5, 29))

        def wload(idx, m):
            dst, tdst = WS[idx % 3], t_ws[idx % 3]
            P.dma("pool", lambda e: e.dma_start(out=dst, in_=w_in_t[m], max_dma_last_dim=8192), writes=[tdst])

        for idx in range(3):
            wload(idx, mts[idx])
        nev = 0
        for idx, m in enumerate(mts):
            ws, tw = WS[idx % 3], t_ws[idx % 3]
            for bi in range(2):
                bank = (2 * idx + bi) % 4
                for kc in range(32):
                    mm(psb[bank], ws[:, kc, :], XG[bi][:, kc, :], kc == 0, kc == 31, [tw, t_xg[bi][kc // 4]], [tps[bank]])
                eb = nev % 2
                nev += 1
                P.add("dve", lambda e, bank=bank, bi=bi, eb=eb: e.tensor_tensor(
                    out=ev[eb], in0=psb[bank], in1=rb[bi], op=ALU.mult), reads=[tps[bank], t_rb[bi]], writes=[t_ev[eb]])
                if m < 13:
                    j = m - 5
                    P.add("act", lambda e, eb=eb: e.activation(out=sq2[eb], in_=ev[eb], func=AF.Square),
                          reads=[t_ev[eb]], writes=[t_sq2[eb]])
                    mm(psb[4 + bi], ones_bf, sq2[eb], j == 0, j == 7, [t_sq2[eb], t_const], [tps[4 + bi]])
                    P.add("dve", lambda e, eb=eb, j=j, bi=bi: e.tensor_scalar(
                        out=CQG[:, j, bi * 512:(bi + 1) * 512], in0=ev[eb], scalar1=g_q_t[:, j:j + 1], scalar2=None,
                        op0=ALU.mult), reads=[t_ev[eb], t_const], writes=[t_cqg[j][bi]])
                else:
                    hh = m - 13
                    P.add("act", lambda e, eb=eb, hh=hh, bi=bi: e.activation(
                        out=Y[:, 16 + hh, bi * 512:(bi + 1) * 512], in_=ev[eb], func=AF.Silu),
                        reads=[t_ev[eb]], writes=[t_y[16 + hh][bi]])
            if idx + 3 < len(mts):
                wload(idx + 3, mts[idx + 3])
            if m == 12:
                for bi in range(2):
                    rstd_from_psum(psb[4 + bi], tps[4 + bi], 1.0 / 1024, sqt2a, t_sqt, rqb[bi], t_rqb[bi])
                    P.add("dve", lambda e, bi=bi: e.tensor_tensor(out=csq[bi], in0=cs[bi], in1=rqb[bi], op=ALU.mult),
                          reads=[t_cs[bi], t_rqb[bi]], writes=[t_csq[bi]])
        P.barrier()

        o4 = R_XG
        KH = [v16(o4 + b * 8192, 4096) for b in range(2)]
        VH = v16(o4 + 16384, 8192).rearrange("p (t d) -> p t d", t=32)
        QN = [v16(o4 + 32768 + b * 2048, 1024).rearrange("p (b t) -> p b t", b=2) for b in range(2)]
        TQ = [v16(o4 + 36864 + b * 2048, 1024).rearrange("p (b t) -> p b t", b=2) for b in range(2)]
        PT = [v16(o4 + 40960 + b * 1024, 512) for b in range(4)]
        rc = [v32(o4 + 45056 + b * 2048, 512) for b in range(2)]
        WQ = [v16(o4 + 49152 + b * 4096, 2048).rearrange("p (k c) -> p k c", k=8) for b in range(2)]
        WKV = [v16(o4 + 57344 + b * 4096, 2048).rearrange("p (k c) -> p k c", k=4) for b in range(2)]
        t_kh = [[Tr() for _ in range(8)] for _ in range(2)]
        t_vh = [Tr() for _ in range(8)]
        t_qn = [[Tr(), Tr()] for _ in range(2)]
        t_tq = [[Tr(), Tr()] for _ in range(2)]
        t_pt = [Tr() for _ in range(4)]
        t_rc = [Tr(), Tr()]
        t_wq = [Tr(), Tr()]
        t_wkv = [Tr(), Tr()]

        def hload(hh):
            P.dma("pool", lambda e: e.dma_start(out=WQ[hh % 2], in_=w_uq_t[hh], max_dma_last_dim=8192), writes=[t_wq[hh % 2]])

        def pload(pr):
            P.dma("pool", lambda e: e.dma_start(out=WKV[pr % 2], in_=w_ukv_t[pr], max_dma_last_dim=8192), writes=[t_wkv[pr % 2]])

        hload(0)
        hload(1)
        pload(0)
        pload(1)
        stg = v32(188416 + 20480, 512)
        t_stg = Tr()
        XGA = v16(R_Y, 16384).rearrange("p (k t) -> p k t", k=32)
        t_xga = [Tr() for _ in range(32)]
        npt = 0
        nmisc = 0
        def pf_dma(kc):
            P.dma("sp", lambda e: e.dma_start(out=stg, in_=xT[3, :, kc, :]), writes=[t_stg])

        def pf_mul(kc):
            P.add("dve", lambda e: e.tensor_scalar(
                out=XGA[:, kc, :], in0=stg, scalar1=g_in_t[:, kc:kc + 1], scalar2=None, op0=ALU.mult),
                reads=[t_stg, t_const], writes=[t_xga[kc]])

        for hh in range(16):
            hb = hh % 2
            pr, hp = divmod(hh, 2)
            pf_dma(2 * hh)
            wq, wkv, twkv = WQ[hb], WKV[pr % 2], t_wkv[pr % 2]
            for bi in range(2):
                for part in range(2):
                    bank = nmisc % 4
                    nmisc += 1
                    for kc in range(8):
                        mm(psb[bank], wq[:, kc, part * 128:(part + 1) * 128], CQG[:, kc, bi * 512:(bi + 1) * 512],
                           kc == 0, kc == 7, [t_wq[hb], t_cqg[kc][bi]], [tps[bank]])
                    if part == 0:
                        P.add("dve", lambda e, bank=bank, hb=hb, bi=bi: e.tensor_tensor(
                            out=QN[hb][:, bi, :], in0=psb[bank], in1=rqb[bi], op=ALU.mult),
                            reads=[tps[bank], t_rqb[bi]], writes=[t_qn[hb][bi]])
                    else:
                        P.add("dve", lambda e, bank=bank, hb=hb, bi=bi: e.tensor_tensor(
                            out=TQ[hb][:, bi, :], in0=psb[bank], in1=csq[bi], op=ALU.mult),
                            reads=[tps[bank], t_csq[bi]], writes=[t_tq[hb][bi]])
            for s in range(8):
                bank = nmisc % 4
                nmisc += 1
                for kc in range(4):
                    mm(psb[bank], wkv[:, kc, hp * 128:(hp + 1) * 128], CKVN[:, kc, s * 512:(s + 1) * 512], kc == 0, kc == 3,
                       [twkv, t_ckvn[s]], [tps[bank]])
                P.add("act", lambda e, bank=bank, hb=hb, s=s: e.activation(
                    out=KH[hb][:, s * 512:(s + 1) * 512], in_=psb[bank], func=AF.Copy),
                    reads=[tps[bank]], writes=[t_kh[hb][s]])
            if hp == 0:
                for s_ in range(8):
                    for half in range(2):
                        bank = nmisc % 4
                        nmisc += 1
                        for t2 in range(2):
                            tt = 2 * half + t2
                            for kc in range(4):
                                mm(psb[bank][:, t2 * 256:(t2 + 1) * 256],
                                   CKVN[:, kc, s_ * 512 + tt * 128: s_ * 512 + (tt + 1) * 128], wkv[:, kc, 256:512],
                                   kc == 0, kc == 3, [twkv, t_ckvn[s_]], [tps[bank]])
                        P.add("dve", lambda e, bank=bank, s_=s_, half=half: e.tensor_copy(
                            out=VH[:, 4 * s_ + 2 * half:4 * s_ + 2 * half + 2, :],
                            in_=psb[bank].rearrange("p (t d) -> p t d", t=2)),
                            reads=[tps[bank]], writes=[t_vh[s_]])
            for bi in range(2):
                if bi == 0:
                    units = [(0, 0), (1, 1), (2, 2), (3, None)]
                else:
                    units = [(0, None), (1, None), (2, None), (3, None), (4, 3), (5, 4), (6, 5), (7, None)]
                diag_slot = 3 if bi == 0 else 7
                kts = []
                for (s, gc) in units:
                    for j in range(4):
                        kts.append((s, j, gc, s == diag_slot))
                po, pl = 4 + 2 * bi, 5 + 2 * bi
                n = len(kts)

                def QK(i):
                    s, j, gc, dg = kts[i]
                    c0 = 128 * j if dg else 0
                    bank = i % 4
                    kcol = (4 * s + j) * 128
                    mm(psb[bank][:, c0:512], KH[hb][:, kcol:kcol + 128], QN[hb][:, bi, c0:512], True, False,
                       [t_kh[hb][s], t_qn[hb][bi]], [tps[bank]])
                    mm(psb[bank][:, c0:512], K2[:, kcol:kcol + 128], TQ[hb][:, bi, c0:512], False, True,
                       [t_k2[s], t_tq[hb][bi]], [tps[bank]])

                QK(0)
                QK(1)
                for i in range(n):
                    if i + 2 < n:
                        QK(i + 2)
                    s, j, gc, dg = kts[i]
                    c0 = 128 * j if dg else 0
                    bank = i % 4
                    pb = npt % 4
                    npt += 1
                    if gc is None:
                        P.add("act", lambda e, bank=bank, pb=pb, c0=c0: e.activation(
                            out=PT[pb][:, c0:512], in_=psb[bank][:, c0:512], func=AF.Exp, scale=ATTN_SCALE),
                            reads=[tps[bank]], writes=[t_pt[pb]])
                    else:
                        P.add("act", lambda e, bank=bank, pb=pb, gc=gc: e.activation(
                            out=PT[pb], in_=psb[bank], func=AF.Exp, scale=ATTN_SCALE, bias=gate[:, gc:gc + 1]),
                            reads=[tps[bank], t_const], writes=[t_pt[pb]])
                    if dg:
                        P.add("dve", lambda e, pb=pb, c0=c0: e.memset(PT[pb][64:128, c0:c0 + 64], 0.0),
                              reads=[t_pt[pb]], writes=[t_pt[pb]])
                    mm(psb[po][:, c0:512], VH[:, 4 * s + j, hp * 128:(hp + 1) * 128], PT[pb][:, c0:512], i == 0, i == n - 1,
                       [t_vh[s], t_pt[pb]], [tps[po]])
                    mm(psb[pl][:, c0:512], ones_bf, PT[pb][:, c0:512], i == 0, i == n - 1,
                       [t_const, t_pt[pb]], [tps[pl]])
                P.add("dve", lambda e, pl=pl, bi=bi: e.reciprocal(out=rc[bi], in_=psb[pl]), reads=[tps[pl]], writes=[t_rc[bi]])
                P.add("dve", lambda e, po=po, bi=bi: e.tensor_tensor(out=rc[bi], in0=psb[po], in1=rc[bi], op=ALU.mult),
                      reads=[tps[po], t_rc[bi]], writes=[t_rc[bi]])
                P.add("dve", lambda e, bi=bi, hh=hh: e.tensor_tensor(
                    out=Y[:, 16 + hh, bi * 512:(bi + 1) * 512], in0=rc[bi], in1=Y[:, 16 + hh, bi * 512:(bi + 1) * 512],
                    op=ALU.mult), reads=[t_rc[bi], t_y[16 + hh][bi]], writes=[t_y[16 + hh][bi]])
                if bi == 0:
                    pf_mul(2 * hh)
                    pf_dma(2 * hh + 1)
            if hh + 2 < 16:
                hload(hh + 2)
            if hp == 1 and pr + 2 < 8:
                pload(pr + 2)
            pf_mul(2 * hh + 1)
        P.barrier()

        WS, t_ws = wstream(R_CKVN)
        o3 = R_CKVN + 24576
        T1 = [v32(o3 + b * 8320, 512) for b in range(2)]
        UU = [v32(o3 + b * 8320 + 2048, 516) for b in range(2)]
        CC = [v32(o3 + b * 8320 + 4112, 512) for b in range(2)]
        GG = [v32(o3 + b * 8320 + 6160, 512) for b in range(2)]
        t_t1 = [Tr(), Tr()]
        t_uu = [Tr(), Tr()]
        t_cc = [Tr(), Tr()]
        t_gg = [Tr(), Tr()]
        o3b = o3 + 2 * 8320
        rb2 = [v32(o3b + b * 2048, 512) for b in range(2)]
        t_rb2 = [Tr(), Tr()]
        o3c = o3b + 4096
        xst1 = [v32(R_XG + b * 4096, 1024).rearrange("p (k t) -> p k t", k=2) for b in range(8)]
        t_xst1 = [Tr() for _ in range(8)]
        o3d = o3c + 8192
        xhs = v32(o3d, 128).rearrange("p (k t) -> p k t", k=32)
        xgh = v16(o3d + 512, 128).rearrange("p (k t) -> p k t", k=32)
        xhq = v16(o3d + 768, 128).rearrange("p (k t) -> p k t", k=32)
        th = v32(o3d + 1024, 8)
        assert o3d + 1024 + 32 <= R_MISC
        t_h = Tr()
        mts = list(range(29, 93))
        for idx in range(3):
            wload(idx, mts[idx])
        Yc = v16(R_XG, 16384).rearrange("p (f t) -> p f t", f=16)
        nr = 0
        for bi, s in ((1, 7),):
            for kg in range(16):
                stb = nr % 8
                nr += 1
                P.dma("sp", lambda e, s=s, kg=kg, stb=stb: e.dma_start(out=xst1[stb], in_=xT[s, :, 2 * kg:2 * kg + 2, :]),
                      writes=[t_xst1[stb]])
                for j in range(2):
                    kc = 2 * kg + j
                    P.add("dve", lambda e, bi=bi, kc=kc, j=j, stb=stb: e.tensor_scalar(
                        out=XG[bi][:, kc, :], in0=xst1[stb][:, j, :], scalar1=g_in_t[:, kc:kc + 1], scalar2=None,
                        op0=ALU.mult), reads=[t_xst1[stb], t_const], writes=[t_xg[bi][kc // 4]])
        for bi in range(2):
            P.add("dve", lambda e, bi=bi: e.tensor_tensor(out=rb2[bi], in0=rb[bi], in1=rb[bi], op=ALU.mult),
                  reads=[t_rb[bi]], writes=[t_rb2[bi]])
        P.dma("sp", lambda e: e.dma_start(out=xhs, in_=xh), writes=[t_h])
        P.add("dve", lambda e: e.tensor_tensor(out=xgh, in0=xhs, in1=g_in_t.unsqueeze(2).to_broadcast([128, 32, 4]),
                                               op=ALU.mult), reads=[t_h, t_const], writes=[t_h])
        P.add("act", lambda e: e.activation(out=xhq, in_=xhs, func=AF.Square), reads=[t_h], writes=[t_h])
        for kc in range(32):
            mm(psb[7][:, 0:4], ones_bf, xhq[:, kc, :], kc == 0, kc == 31, [t_h, t_const], [tps[7]])
        rstd_from_psum(psb[7], tps[7], 1.0 / D, sqt, t_sqt, rbh, t_small, n=4)
        P.add("dve", lambda e: e.tensor_tensor(out=rbh2, in0=rbh, in1=rbh, op=ALU.mult), reads=[t_small], writes=[t_small])
        P.barrier()

        for idx, m in enumerate(mts):
            f, which = divmod(idx, 4)
            ws, tw = WS[idx % 3], t_ws[idx % 3]
            banks = [(2 * idx) % 6, (2 * idx + 1) % 6]
            hc = 4 * which
            for kc in range(32):
                mm(psb[banks[0]], ws[:, kc, :], XGA[:, kc, :], kc == 0, kc == 31, [tw, t_xga[kc]], [tps[banks[0]]])
                mm(psb[banks[1]], ws[:, kc, :], XG[1][:, kc, :], kc == 0, kc == 31, [tw, t_xg[1][kc // 4]], [tps[banks[1]]])
                if which < 2:
                    mm(psb[6 + (f % 2)][:, hc:hc + 4], ws[:, kc, :], xgh[:, kc, :], kc == 0, kc == 31, [tw, t_h],
                       [tps[6 + (f % 2)]])
            if idx + 3 < len(mts):
                wload(idx + 3, mts[idx + 3])
            hb_ = 6 + (f % 2)
            for bi in range(2):
                bank = banks[bi]
                tb = bi
                if which == 0:
                    P.add("act", lambda e, bank=bank, tb=tb: e.activation(out=T1[tb], in_=psb[bank], func=AF.Copy),
                          reads=[tps[bank]], writes=[t_t1[tb]])
                elif which == 1:
                    P.add("dve", lambda e, bank=bank, tb=tb: e.tensor_tensor(out=T1[tb], in0=psb[bank], in1=T1[tb], op=ALU.mult),
                          reads=[tps[bank], t_t1[tb]], writes=[t_t1[tb]])
                    P.add("dve", lambda e, tb=tb, bi=bi: e.tensor_tensor(out=UU[tb][:, 2:514], in0=T1[tb], in1=rb2[bi], op=ALU.mult),
                          reads=[t_t1[tb], t_rb2[bi]], writes=[t_uu[tb]])
                    if bi == 0:
                        P.add("act", lambda e, hb_=hb_: e.activation(out=th[:, 0:4], in_=psb[hb_][:, 0:4], func=AF.Copy),
                              reads=[tps[hb_]], writes=[t_small])
                        P.add("dve", lambda e, hb_=hb_: e.tensor_tensor(out=th[:, 0:4], in0=psb[hb_][:, 4:8], in1=th[:, 0:4], op=ALU.mult),
                              reads=[tps[hb_], t_small], writes=[t_small])
                        P.add("dve", lambda e: e.tensor_tensor(out=uh, in0=th[:, 0:4], in1=rbh2, op=ALU.mult),
                              reads=[t_small], writes=[t_small])
                    P.add("dve", lambda e, tb=tb, bi=bi: e.tensor_copy(out=UU[tb][:, 0:2], in_=uh[:, 2 * bi:2 * bi + 2]),
                          reads=[t_small, t_uu[tb]], writes=[t_uu[tb]])
                    cw = 3 * f
                    P.add("act", lambda e, tb=tb, cw=cw: e.activation(out=CC[tb], in_=UU[tb][:, 0:512], func=AF.Copy,
                                                                       scale=convw[:, cw:cw + 1]),
                          reads=[t_uu[tb], t_const], writes=[t_cc[tb]])
                    P.add("dve", lambda e, tb=tb, cw=cw: e.scalar_tensor_tensor(
                        out=CC[tb], in0=UU[tb][:, 1:513], scalar=convw[:, cw + 1:cw + 2], in1=CC[tb], op0=ALU.mult, op1=ALU.add),
                        reads=[t_uu[tb], t_cc[tb], t_const], writes=[t_cc[tb]])
                    P.add("dve", lambda e, tb=tb, cw=cw: e.scalar_tensor_tensor(
                        out=CC[tb], in0=UU[tb][:, 2:514], scalar=convw[:, cw + 2:cw + 3], in1=CC[tb], op0=ALU.mult, op1=ALU.add),
                        reads=[t_uu[tb], t_cc[tb], t_const], writes=[t_cc[tb]])
                elif which == 2:
                    P.add("dve", lambda e, bank=bank, tb=tb, bi=bi: e.tensor_tensor(out=GG[tb], in0=psb[bank], in1=rb[bi], op=ALU.mult),
                          reads=[tps[bank], t_rb[bi]], writes=[t_gg[tb]])
                    P.add("dve", lambda e, tb=tb: e.tensor_tensor(out=GG[tb], in0=GG[tb], in1=CC[tb], op=ALU.mult),
                          reads=[t_gg[tb], t_cc[tb]], writes=[t_gg[tb]])
                else:
                    P.add("dve", lambda e, bank=bank, tb=tb, bi=bi: e.tensor_tensor(out=T1[tb], in0=psb[bank], in1=rb[bi], op=ALU.mult),
                          reads=[tps[bank], t_rb[bi]], writes=[t_t1[tb]])
                    P.add("act", lambda e, tb=tb: e.activation(out=T1[tb], in_=T1[tb], func=AF.Silu),
                          reads=[t_t1[tb]], writes=[t_t1[tb]])
                    P.add("dve", lambda e, tb=tb, f=f, bi=bi: e.tensor_tensor(
                        out=Yc[:, f, bi * 512:(bi + 1) * 512], in0=GG[tb], in1=T1[tb], op=ALU.mult),
                        reads=[t_gg[tb], t_t1[tb]], writes=[t_y[f][bi]])
        P.barrier()

        Ht = [v32(R_Y + tt * 16384, 4096) for tt in range(2)] + [v32(R_XG + 32768 + tt * 16384, 4096) for tt in range(2)]
        t_hh = [Tr() for _ in range(4)]
        WO = [v16(R_CKVN + b * 4096, 2048).rearrange("p (k c) -> p k c", k=4) for b in range(4)]
        t_wo = [Tr() for _ in range(4)]
        XR = [v32(R_CKVN + 16384 + b * 8192, 2048).rearrange("p (t c) -> p t c", t=4) for b in range(2)]
        t_xr = [Tr(), Tr()]
        GF = v32(R_CKVN + 32768, 4096)
        t_gf = Tr()
        JUNK = v16(R_CKVN + 49152, 512)
        t_junk = Tr()
        assert R_CKVN + 49152 + 1024 <= R_MISC
        t_out = [Tr("out%d" % i) for i in range(4)]
        t_ssq = [[Tr() for _ in range(8)] for _ in range(4)]
        P.dma("sp", lambda e: e.dma_start(out=GF, in_=gfin_d), writes=[t_gf])
        wlist = [(grp, ct, kg) for grp in range(2) for ct in range(8) for kg in range(8)]

        def woload(i):
            grp, ct, kg = wlist[i]
            P.dma("pool", lambda e: e.dma_start(out=WO[i % 4], in_=w_out_t[ct, kg], max_dma_last_dim=8192), writes=[t_wo[i % 4]])

        for i in range(4):
            woload(i)
        wi = 0
        for grp in range(2):
            for ct in range(8):
                xb = ct % 2
                P.dma("sp", lambda e, grp=grp, ct=ct, xb=xb: e.dma_start(
                    out=XR[xb], in_=xres[grp * 512:(grp + 1) * 512, ct * 512:(ct + 1) * 512].rearrange("(t p) c -> p t c", p=128)),
                    writes=[t_xr[xb]])
                pbase = 4 * (ct % 2)
                for kg in range(8):
                    wo, two = WO[wi % 4], t_wo[wi % 4]
                    for tt in range(4):
                        for kcc in range(4):
                            kc = 4 * kg + kcc
                            tok0 = grp * 512 + tt * 128
                            ysrc = Yc[:, kc, tok0:tok0 + 128] if kc < 16 else Y[:, kc, tok0:tok0 + 128]
                            mm(psb[pbase + tt], ysrc, wo[:, kcc, :], kc == 0, kc == 31,
                               [two, t_y[kc][grp]], [tps[pbase + tt]])
                    if wi + 4 < len(wlist):
                        woload(wi + 4)
                    wi += 1
                for tt in range(4):
                    P.add("dve", lambda e, pbase=pbase, tt=tt, ct=ct, xb=xb: e.tensor_tensor(
                        out=Ht[tt][:, ct * 512:(ct + 1) * 512], in0=psb[pbase + tt], in1=XR[xb][:, tt, :], op=ALU.add),
                        reads=[tps[pbase + tt], t_xr[xb]], writes=[t_hh[tt]])
                    P.add("act", lambda e, tt=tt, ct=ct: e.activation(
                        out=JUNK, in_=Ht[tt][:, ct * 512:(ct + 1) * 512], func=AF.Square,
                        accum_out=ssq[:, tt * 8 + ct: tt * 8 + ct + 1]),
                        reads=[t_hh[tt]], writes=[t_ssq[tt][ct], t_junk])
            P.add("dve", lambda e: e.reduce_sum(out=tot, in_=ssq.rearrange("p (t c) -> p t c", t=4), axis=AX.X),
                  reads=[t_small] + [t for row in t_ssq for t in row], writes=[t_small])
            P.add("act", lambda e: e.activation(out=tot, in_=tot, func=AF.Sqrt, scale=1.0 / D, bias=eps_ap),
                  reads=[t_small, t_const], writes=[t_small])
            P.add("dve", lambda e: e.reciprocal(out=rfin, in_=tot), reads=[t_small], writes=[t_small])
            for tt in range(4):
                for ct in range(8):
                    P.add("dve", lambda e, tt=tt, ct=ct: e.scalar_tensor_tensor(
                        out=Ht[tt][:, ct * 512:(ct + 1) * 512], in0=Ht[tt][:, ct * 512:(ct + 1) * 512],
                        scalar=rfin[:, tt:tt + 1], in1=GF[:, ct * 512:(ct + 1) * 512], op0=ALU.mult, op1=ALU.mult),
                        reads=[t_hh[tt], t_small, t_gf], writes=[t_hh[tt]])
                r0 = grp * 512 + tt * 128
                P.dma("sp", lambda e, tt=tt, r0=r0: e.dma_start(out=out[r0:r0 + 128, :], in_=Ht[tt]),
                      reads=[t_hh[tt]], sem_tr=t_out[tt])
        P.barrier()
        P.emit(nc, stack)
    return nc


def _tile_w(w, cols, kc):
    sub = w[:, cols]
    return np.ascontiguousarray(sub.reshape(kc, 128, len(cols)).transpose(1, 0, 2))


def _prep_weights(g_in, w_in, conv_w, q_norm_g, w_uq, kv_norm_g, w_ukv, w_out, g_final):
    w_in = w_in[0]
    ar = np.arange
    perm = np.concatenate([ar(0, 32), ar(32, 64), ar(32, 64), ar(0, 32)])
    col_tiles = []
    for m in range(4):
        col_tiles.append(9216 + m * 128 + ar(128))
    col_tiles.append(9728 + perm)
    for m in range(8):
        col_tiles.append(8192 + m * 128 + ar(128))
    for m in range(16):
        col_tiles.append(9792 + m * 128 + ar(128))
    for f in range(16):
        for base in (2048, 4096, 0, 6144):
            col_tiles.append(base + f * 128 + ar(128))
    assert len(col_tiles) == NMT
    w_in_t = np.empty((NMT, 128, 32, 128), np.float32)
    for m, cols in enumerate(col_tiles):
        w_in_t[m] = _tile_w(w_in, cols, 32)
    w_uq_t = np.empty((16, 128, 8, 256), np.float32)
    for h in range(16):
        cols = np.concatenate([h * 192 + ar(128), h * 192 + 128 + perm])
        w_uq_t[h] = _tile_w(w_uq[0], cols, 8)
    w_ukv_t = np.empty((8, 128, 4, 512), np.float32)
    for pr in range(8):
        h0, h1 = 2 * pr, 2 * pr + 1
        cols = np.concatenate([h0 * 256 + ar(128), h1 * 256 + ar(128), h0 * 256 + 128 + ar(128), h1 * 256 + 128 + ar(128)])
        w_ukv_t[pr] = _tile_w(w_ukv[0], cols, 4)
    w_out_t = np.ascontiguousarray(w_out[0].reshape(8, 4, 128, 8, 512).transpose(3, 0, 2, 1, 4))
    dmat = np.zeros((128, 128), np.float32)
    i64 = ar(64)
    for a in (0, 64):
        for b in (0, 64):
            dmat[a + i64, b + i64] = 1.0
    return {
        "w_in_t": w_in_t, "w_uq_t": w_uq_t, "w_ukv_t": w_ukv_t, "w_out_t": w_out_t,
        "g_in_t": np.ascontiguousarray(g_in[0].reshape(32, 128).T),
        "g_q_t": np.ascontiguousarray(q_norm_g[0].reshape(8, 128).T),
        "g_kv_t": np.ascontiguousarray(kv_norm_g[0].reshape(4, 128).T),
        "convw_t": np.ascontiguousarray(conv_w[0].reshape(3, 16, 128).transpose(2, 1, 0).reshape(128, 48)),
        "gfin_b": np.ascontiguousarray(np.broadcast_to(g_final[None, :], (128, D))),
        "dmat": dmat,
    }


def _rope_table():
    pos = np.arange(S, dtype=np.float32)
    inv_freq = (1.0 / (np.float32(10000.0) ** (np.arange(0, 64, 2, dtype=np.float32) / np.float32(64)))).astype(np.float32)
    ang = (pos[:, None] * inv_freq[None, :]).astype(np.float32)
    c = np.cos(ang).astype(np.float32).T
    s = np.sin(ang).astype(np.float32).T
    return np.concatenate([c, c, -s, s], axis=0)


def _slots(k):
    low = [i for i in range(4) if i != k]
    high = [i for i in range(4, 8) if i != 7 - k]
    return low + [k] + high + [7 - k]


def kernel(x, g_in, w_in, conv_w, q_norm_g, w_uq, kv_norm_g, w_ukv, w_out, g_final):
    x = np.asarray(x, np.float32)
    wts = _prep_weights(*(np.asarray(a, np.float32) for a in
                          (g_in, w_in, conv_w, q_norm_g, w_uq, kv_norm_g, w_ukv, w_out, g_final)))
    cs_full = _rope_table()
    in_maps = []
    for c in range(NCORES):
        b, k = divmod(c, 4)
        sl = _slots(k)
        xb = x[b]
        xT = np.empty((8, 128, 32, 512), np.float32)
        cs_tab = np.empty((8, 128, 512), np.float32)
        for si, blk in enumerate(sl):
            xT[si] = xb[blk * BLK:(blk + 1) * BLK, :].reshape(512, 32, 128).transpose(2, 1, 0)
            cs_tab[si] = cs_full[:, blk * BLK:(blk + 1) * BLK]
        A, B = k, 7 - k
        xres = np.concatenate([xb[A * BLK:(A + 1) * BLK], xb[B * BLK:(B + 1) * BLK]], axis=0)
        halo = np.zeros((4, D), np.float32)
        if A > 0:
            halo[0:2] = xb[A * BLK - 2:A * BLK]
        halo[2:4] = xb[B * BLK - 2:B * BLK]
        xh = np.ascontiguousarray(halo.reshape(4, 32, 128).transpose(2, 1, 0))
        gate = np.zeros((128, 6), np.float32)
        for j in range(3):
            gate[:, j] = 0.0 if sl[j] < A else NEG
            gate[:, 3 + j] = 0.0 if sl[4 + j] < B else NEG
        m = {"xT": xT, "xres": np.ascontiguousarray(xres), "xh": xh, "cs_tab": cs_tab, "gate": gate}
        m.update(wts)
        in_maps.append(m)
    nc = build_program()
    res = run_bass_kernel_spmd(nc, in_maps, core_ids=list(range(NCORES)))
    outp = np.empty((2, S, D), np.float32)
    for c in range(NCORES):
        b, k = divmod(c, 4)
        o = res.results[c]["out"]
        outp[b, k * BLK:(k + 1) * BLK] = o[0:512]
        outp[b, (7 - k) * BLK:(8 - k) * BLK] = o[512:1024]
    return outp
```
